# Optimizing a Trainium2 kernel written in Bass

```python
import math
import jax, jax.numpy as jnp
from jax import lax
import numpy as np

D_MODEL = 1024
BATCH = 2
SEQ = 8192
DEPTH = 1
DEC_BATCH = 8
DEC_SEQ = 32
PAST_LEN = 1024

CHUNK = 64
N_RET_HEADS = 4
RET_DK = 256
RET_DV = 512
RET_QK_W = N_RET_HEADS * RET_DK
RET_V = N_RET_HEADS * RET_DV
CONV_W = 1024
CONV_K = 3
N_MEM = 256
N_MEM_HEADS = 4
MEM_HD = 256
MEM_W = N_MEM_HEADS * MEM_HD
D_FF = 2816
FFN_K = 3
N_BRANCH = 3
ROPE_BASE = 10000.0
EPS = 1e-6

IN_SPLITS = (
    RET_QK_W,
    2 * RET_QK_W,
    2 * RET_QK_W + RET_V,
    2 * RET_QK_W + 2 * RET_V,
    2 * RET_QK_W + 2 * RET_V + CONV_W,
    2 * RET_QK_W + 2 * RET_V + 2 * CONV_W,
    2 * RET_QK_W + 2 * RET_V + 3 * CONV_W,
    2 * RET_QK_W + 2 * RET_V + 3 * CONV_W + MEM_W,
)
IN_COLS = 2 * RET_QK_W + 2 * RET_V + 3 * CONV_W + MEM_W + N_BRANCH * D_MODEL

kernel_name = "hybrid_retention_shortconv_memory_encoder_step"


def rmsnorm(x, g):
    xf = x.astype(jnp.float32)
    r = lax.rsqrt(jnp.mean(xf * xf, axis=-1, keepdims=True) + EPS)
    return (xf * r).astype(x.dtype) * g


def rotary(x, pos):
    half = x.shape[-1] // 2
    inv = ROPE_BASE ** (-jnp.arange(half, dtype=jnp.float32) / half)
    ang = pos.astype(jnp.float32)[:, None] * inv[None, :]
    cos = jnp.cos(ang)[None, :, None, :]
    sin = jnp.sin(ang)[None, :, None, :]
    x1, x2 = x[..., :half], x[..., half:]
    return jnp.concatenate([x1 * cos - x2 * sin, x1 * sin + x2 * cos], axis=-1)


def ret_log_decay():
    return jnp.log(1.0 - 2.0 ** (-5.0 - jnp.arange(N_RET_HEADS, dtype=jnp.float32)))


def retention_block(q, k, v, s_prev, log_g):
    L = q.shape[2]
    idx = jnp.arange(L, dtype=jnp.float32)
    diff = idx[:, None] - idx[None, :]
    lg = log_g[:, None, None]
    dmask = jnp.where(diff >= 0, jnp.exp(lg * jnp.maximum(diff, 0.0)), 0.0)
    scores = jnp.einsum('bhld,bhmd->bhlm', q, k) * dmask
    intra = jnp.einsum('bhlm,bhme->bhle', scores, v)
    q_decay = jnp.exp(log_g[:, None] * (idx + 1.0)[None, :])
    cross = jnp.einsum('bhld,bhde->bhle', q * q_decay[..., None], s_prev)
    k_decay = jnp.exp(log_g[:, None] * (L - 1.0 - idx)[None, :])
    s_new = (jnp.exp(log_g * L)[:, None, None] * s_prev
             + jnp.einsum('bhld,bhle->bhde', k * k_decay[..., None], v))
    return intra + cross, s_new


def retention(q, k, v, s0):
    log_g = ret_log_decay()
    b, L = q.shape[0], q.shape[1]
    if L <= CHUNK:
        o, s = retention_block(q.transpose(0, 2, 1, 3), k.transpose(0, 2, 1, 3),
                               v.transpose(0, 2, 1, 3), s0, log_g)
        return o.transpose(0, 2, 1, 3), s
    n = L // CHUNK

    def to_chunks(t):
        return t.reshape(b, n, CHUNK, t.shape[2], t.shape[3]).transpose(1, 0, 3, 2, 4)

    def step(s, qkv):
        qc, kc, vc = qkv
        o, s = retention_block(qc, kc, vc, s, log_g)
        return s, o

    s, o = lax.scan(step, s0, (to_chunks(q), to_chunks(k), to_chunks(v)))
    o = o.transpose(1, 0, 3, 2, 4).reshape(b, L, N_RET_HEADS, RET_DV)
    return o, s


def head_groupnorm(o, g, dtype):
    mu = jnp.mean(o, axis=-1, keepdims=True)
    var = jnp.mean(jnp.square(o - mu), axis=-1, keepdims=True)
    on = (o - mu) * lax.rsqrt(var + EPS)
    return on.reshape(o.shape[0], o.shape[1], RET_V).astype(dtype) * g


def causal_dwconv(u, buf, w):
    kw = w.shape[0]
    L = u.shape[1]
    full = jnp.concatenate([buf.astype(u.dtype), u], axis=1)
    out = full[:, 0:L] * w[0]
    for j in range(1, kw):
        out = out + full[:, j:j + L] * w[j]
    return out, full[:, L:]


def memory_kv(mem, g_mem, w_mem_kv):
    m = rmsnorm(mem, g_mem)
    kv = m @ w_mem_kv
    b = mem.shape[0]
    k = kv[..., :MEM_W].reshape(b, N_MEM, N_MEM_HEADS, MEM_HD)
    v = kv[..., MEM_W:].reshape(b, N_MEM, N_MEM_HEADS, MEM_HD)
    return k, v


def memory_attend(q, mk, mv):
    s = jnp.einsum('blhd,bmhd->bhlm', q, mk).astype(jnp.float32) * (MEM_HD ** -0.5)
    p = jax.nn.softmax(s, axis=-1).astype(mv.dtype)
    o = jnp.einsum('bhlm,bmhd->blhd', p, mv)
    return o.reshape(q.shape[0], q.shape[1], MEM_W)


def layer(x, pos, s_ret, buf_conv, buf_ffn, mem_k, mem_v,
          g_mix, w_in, g_ret_gn, w_conv, w_br_ret, w_br_conv, w_br_mem, w_out,
          g_ffn, w_ffn_in, w_ffn_conv, w_ffn_down):
    b, L = x.shape[0], x.shape[1]
    h = rmsnorm(x, g_mix)
    z = h @ w_in
    q, k, v, gr, cb, cc, cx, mq, gates = jnp.split(z, IN_SPLITS, axis=-1)
    q = rotary(q.reshape(b, L, N_RET_HEADS, RET_DK).astype(jnp.float32), pos)
    k = rotary(k.reshape(b, L, N_RET_HEADS, RET_DK).astype(jnp.float32), pos) * (RET_DK ** -0.5)
    v = v.reshape(b, L, N_RET_HEADS, RET_DV).astype(jnp.float32)
    o_ret, s_new = retention(q, k, v, s_ret.astype(jnp.float32))
    o_ret = head_groupnorm(o_ret, g_ret_gn, x.dtype) * jax.nn.silu(gr)
    y_c, buf_conv_new = causal_dwconv(cc * cx, buf_conv, w_conv)
    o_conv = cb * y_c
    o_mem = memory_attend(mq.reshape(b, L, N_MEM_HEADS, MEM_HD), mem_k, mem_v)
    g = jax.nn.sigmoid(gates).reshape(b, L, N_BRANCH, D_MODEL)
    merged = (g[:, :, 0] * (o_ret @ w_br_ret)
              + g[:, :, 1] * (o_conv @ w_br_conv)
              + g[:, :, 2] * (o_mem @ w_br_mem))
    x = x + merged @ w_out
    h2 = rmsnorm(x, g_ffn)
    up = h2 @ w_ffn_in
    a, u = up[..., :D_FF], up[..., D_FF:]
    a_c, buf_ffn_new = causal_dwconv(a, buf_ffn, w_ffn_conv)
    x = x + (jax.nn.silu(a_c) * u) @ w_ffn_down
    return x, s_new.astype(x.dtype), buf_conv_new, buf_ffn_new


def setup_inputs(seed: int = 0) -> dict:
    key = jax.random.key(seed)
    ks = jax.random.split(key, 24)
    f32 = jnp.float32

    def nrm(k, shape, scale):
        return jax.random.normal(k, shape, f32) * scale

    def gain(k, shape):
        return 1.0 + 0.05 * jax.random.normal(k, shape, f32)

    return {
        "x_prompt": nrm(ks[0], (BATCH, SEQ, D_MODEL), 1.0),
        "x_sample": nrm(ks[1], (DEC_BATCH, DEC_SEQ, D_MODEL), 1.0),
        "mem_prompt": nrm(ks[2], (BATCH, N_MEM, D_MODEL), 1.0),
        "state_ret": nrm(ks[3], (DEPTH, DEC_BATCH, N_RET_HEADS, RET_DK, RET_DV), 1.0),
        "state_conv": nrm(ks[4], (DEPTH, DEC_BATCH, CONV_K - 1, CONV_W), 1.0),
        "state_ffn_conv": nrm(ks[5], (DEPTH, DEC_BATCH, FFN_K - 1, D_FF), 1.0),
        "cache_mem_k": nrm(ks[6], (DEPTH, DEC_BATCH, N_MEM, N_MEM_HEADS, MEM_HD), 1.0),
        "cache_mem_v": nrm(ks[7], (DEPTH, DEC_BATCH, N_MEM, N_MEM_HEADS, MEM_HD), 1.0),
        "g_mix": gain(ks[8], (DEPTH, D_MODEL)),
        "w_in": nrm(ks[9], (DEPTH, D_MODEL, IN_COLS), D_MODEL ** -0.5),
        "g_ret_gn": gain(ks[10], (DEPTH, RET_V)),
        "w_conv": nrm(ks[11], (DEPTH, CONV_K, CONV_W), CONV_K ** -0.5),
        "g_mem": gain(ks[12], (DEPTH, D_MODEL)),
        "w_mem_kv": nrm(ks[13], (DEPTH, D_MODEL, 2 * MEM_W), D_MODEL ** -0.5),
        "w_br_ret": nrm(ks[14], (DEPTH, RET_V, D_MODEL), RET_V ** -0.5),
        "w_br_conv": nrm(ks[15], (DEPTH, CONV_W, D_MODEL), CONV_W ** -0.5),
        "w_br_mem": nrm(ks[16], (DEPTH, MEM_W, D_MODEL), MEM_W ** -0.5),
        "w_out": nrm(ks[17], (DEPTH, D_MODEL, D_MODEL), D_MODEL ** -0.5),
        "g_ffn": gain(ks[18], (DEPTH, D_MODEL)),
        "w_ffn_in": nrm(ks[19], (DEPTH, D_MODEL, 2 * D_FF), D_MODEL ** -0.5),
        "w_ffn_conv": nrm(ks[20], (DEPTH, FFN_K, D_FF), FFN_K ** -0.5),
        "w_ffn_down": nrm(ks[21], (DEPTH, D_FF, D_MODEL), D_FF ** -0.5),
        "g_final": gain(ks[22], (D_MODEL,)),
    }


def reference(x_prompt, x_sample, mem_prompt, state_ret, state_conv, state_ffn_conv,
              cache_mem_k, cache_mem_v, g_mix, w_in, g_ret_gn, w_conv, g_mem, w_mem_kv,
              w_br_ret, w_br_conv, w_br_mem, w_out, g_ffn, w_ffn_in, w_ffn_conv, w_ffn_down,
              g_final):
    bp = x_prompt.shape[0]
    dt = x_prompt.dtype
    pos_p = jnp.arange(SEQ, dtype=jnp.int32)
    pos_s = PAST_LEN + jnp.arange(DEC_SEQ, dtype=jnp.int32)
    xp, xs = x_prompt, x_sample
    ret_p, conv_p, ffn_p, mk_p, mv_p = [], [], [], [], []
    ret_s, conv_s, ffn_s = [], [], []
    for l in range(DEPTH):
        lw = (g_mix[l], w_in[l], g_ret_gn[l], w_conv[l], w_br_ret[l], w_br_conv[l],
              w_br_mem[l], w_out[l], g_ffn[l], w_ffn_in[l], w_ffn_conv[l], w_ffn_down[l])
        mk, mv = memory_kv(mem_prompt, g_mem[l], w_mem_kv[l])
        s0 = jnp.zeros((bp, N_RET_HEADS, RET_DK, RET_DV), jnp.float32)
        b0c = jnp.zeros((bp, CONV_K - 1, CONV_W), dt)
        b0f = jnp.zeros((bp, FFN_K - 1, D_FF), dt)
        xp, sp, cp, fp = layer(xp, pos_p, s0, b0c, b0f, mk, mv, *lw)
        ret_p.append(sp); conv_p.append(cp); ffn_p.append(fp); mk_p.append(mk); mv_p.append(mv)
        xs, ss, cs, fs = layer(xs, pos_s, state_ret[l], state_conv[l], state_ffn_conv[l],
                               cache_mem_k[l], cache_mem_v[l], *lw)
        ret_s.append(ss); conv_s.append(cs); ffn_s.append(fs)
    y_prompt = rmsnorm(xp, g_final)
    y_sample = rmsnorm(xs, g_final)
    new_state_ret_prompt = jnp.stack(ret_p, 0)
    new_state_conv_prompt = jnp.stack(conv_p, 0)
    new_state_ffn_conv_prompt = jnp.stack(ffn_p, 0)
    new_cache_mem_k_prompt = jnp.stack(mk_p, 0)
    new_cache_mem_v_prompt = jnp.stack(mv_p, 0)
    new_state_ret_sample = jnp.stack(ret_s, 0)
    new_state_conv_sample = jnp.stack(conv_s, 0)
    new_state_ffn_conv_sample = jnp.stack(ffn_s, 0)
    return (y_prompt, y_sample, new_state_ret_prompt, new_state_conv_prompt,
            new_state_ffn_conv_prompt, new_cache_mem_k_prompt, new_cache_mem_v_prompt,
            new_state_ret_sample, new_state_conv_sample, new_state_ffn_conv_sample)
```

```python
import numpy as np
import concourse.bass as bass
import concourse.mybir as mybir
from concourse.bass_utils import run_bass_kernel_spmd

F32 = mybir.dt.float32
BF = mybir.dt.bfloat16
AF = mybir.ActivationFunctionType
ALU = mybir.AluOpType
AX = mybir.AxisListType

D = 1024
SEQ = 8192
SEGT = 2048
NH = 4
DFF = 2816
NFF = 22
EPS = 1e-6
PAST = 1024
IN_COLS = 13312
C_Q, C_K, C_V, C_GR, C_CB, C_CC, C_CX, C_MQ, C_G = 0, 1024, 2048, 4096, 6144, 7168, 8192, 9216, 10240
GAM = [1.0 - 2.0 ** (-5.0 - h) for h in range(NH)]
PRE = 6144
XROWS = PRE + 4 + SEGT
NROPE = XROWS + 32

A_MASK = 0
A_KDEC = A_MASK + 2 * 4 * 128
A_ROWD = A_KDEC + 12
A_COEF = A_ROWD + 8
A_SEL = A_COEF + 24
A_HV = A_SEL + 6
A_GMIX = A_HV + 1
A_GFFN = A_GMIX + 8
A_GMEM = A_GFFN + 8
A_GGN = A_GMEM + 8
A_WCONV = A_GGN + 16
A_WFC = A_WCONV + 24
A_SCONV = A_WFC + 66
A_SFFN = A_SCONV + 16
A_EPS = A_SFFN + 44
A_KDT = A_EPS + 1
A_KSC = A_KDT + 16
A_ROWT = A_KSC + 16
NAUX = A_ROWT + 16


class Tok:
    __slots__ = ("sem", "val")

    def __init__(self, sem, val):
        self.sem = sem
        self.val = val


class Buf:
    __slots__ = ("w", "r")

    def __init__(self):
        self.w = None
        self.r = []


class Sched:
    def __init__(self, nc, sems, dma_sems):
        self.nc = nc
        self.sem = sems
        self.cnt = {k: 0 for k in sems}
        self.prog = {k: [] for k in sems}
        self.seen = {k: {} for k in sems}
        self.dma_sems = dma_sems
        self.dma_i = {k: 0 for k in dma_sems}
        self.dma_val = {}
        self.dma_last = {}

    def _waits(self, eng, toks):
        out = []
        seen = self.seen[eng]
        for t in toks:
            if t is None:
                continue
            if eng == "pe" and t.sem is self.sem["pe"]:
                continue
            k = id(t.sem)
            if seen.get(k, 0) < t.val:
                seen[k] = t.val
                out.append((t.sem, t.val))
        return out

    def _deps(self, reads, writes):
        toks = []
        for b in reads:
            toks.append(b.w)
        for b in writes:
            toks.append(b.w)
            toks.extend(b.r)
        return toks

    def op(self, eng, fn, reads=(), writes=()):
        waits = self._waits(eng, self._deps(reads, writes))
        self.cnt[eng] += 1
        tok = Tok(self.sem[eng], self.cnt[eng])
        self.prog[eng].append((waits, fn, self.sem[eng], 1))
        for b in reads:
            b.r.append(tok)
        for b in writes:
            b.w = tok
            b.r = []
        return tok

    def dma(self, q, out, in_, reads=(), writes=(), **kw):
        ring = self.dma_sems[q]
        s = ring[self.dma_i[q] % len(ring)]
        self.dma_i[q] += 1
        toks = self._deps(reads, writes)
        toks.append(self.dma_last.get(id(s)))
        waits = self._waits(q, toks)
        v = self.dma_val.get(id(s), 0) + 16
        self.dma_val[id(s)] = v
        tok = Tok(s, v)
        self.dma_last[id(s)] = tok
        self.prog[q].append((waits, lambda e: e.dma_start(out=out, in_=in_, **kw), s, 16))
        for b in reads:
            b.r.append(tok)
        for b in writes:
            b.w = tok
            b.r = []
        return tok

    def wait_all(self, eng, toks):
        waits = self._waits(eng, toks)
        if waits:
            self.prog[eng].append((waits, None, None, 0))

    def replay(self, eng, e):
        for waits, fn, sem, inc in self.prog[eng]:
            for s, v in waits:
                e.wait_ge(s, v)
            if fn is not None:
                ins = fn(e)
                ins.then_inc(sem, inc)


STAGE = 99


def build_nc():
    nc = bass.Bass("TRN2", target_bir_lowering=False)

    def din(name, shape, dt=F32):
        return nc.dram_tensor(name, list(shape), dt, kind="ExternalInput").ap()

    def dout(name, shape, dt=F32):
        return nc.dram_tensor(name, list(shape), dt, kind="ExternalOutput").ap()

    xp = din("xp", [XROWS, D])
    xs = din("xs", [32, D])
    memp = din("memp", [256, D])
    st_ret = din("st_ret", [NH, 256, 512])
    cmk = din("cmk", [256, D])
    cmv = din("cmv", [256, D])
    w_in = din("w_in", [D, IN_COLS])
    w_mem_kv = din("w_mem_kv", [D, 2048])
    w_br_ret = din("w_br_ret", [2048, D])
    w_br_conv = din("w_br_conv", [D, D])
    w_br_mem = din("w_br_mem", [D, D])
    w_out = din("w_out", [D, D])
    w_ffn_in = din("w_ffn_in", [D, 2 * DFF])
    w_ffn_down = din("w_ffn_down", [DFF, D])
    rope_d = din("rope", [2, 128, NROPE])
    aux_d = din("aux", [128, NAUX])
    gfin_d = din("gfin", [128, D])

    y_d = dout("y", [SEGT, D])
    ys_d = dout("ys", [32, D])
    o_sret = dout("o_sret", [128, 8 * 512])
    o_sconv = dout("o_sconv", [128, 16])
    o_sffn = dout("o_sffn", [128, 44])
    o_mk = dout("o_mk", [256, D])
    o_mv = dout("o_mv", [256, D])
    s_sret = dout("s_sret", [128, 8 * 512])
    s_sconv = dout("s_sconv", [128, 16])
    s_sffn = dout("s_sffn", [128, 44])


    NBLK = 64
    wscr = nc.dram_tensor("wscr", [NBLK, 128, 4096], BF)

    def sb(name, shape, dt):
        return nc.alloc_sbuf_tensor("sb_" + name, shape, dt)
    ident = sb("ident", [128, 128], BF)
    identf = sb("identf", [128, 128], F32)
    aux = sb("aux", [128, NAUX], F32)
    gfin = sb("gfin_sb", [128, D], F32)
    rope = sb("rope_sb", [128, 2, 512], F32)
    S = sb("S", [128, 8, 512], F32)
    Sb = sb("Sb", [128, 8, 512], BF)
    mkT_p = sb("mkT_p", [128, 8, 256], BF)
    mv_p = sb("mv_p", [128, 2, 1024], BF)
    mkT_s = sb("mkT_s", [128, 8, 256], BF)
    mv_s = sb("mv_s", [128, 2, 1024], BF)
    xt = sb("xt", [128, 4, 1024], F32)
    xbm = sb("xbm", [128, 4096], BF)
    hT = sb("hT", [128, 8, 512], BF)
    qf = sb("qf", [128, 2, 512], BF)
    kf = sb("kf", [128, 2, 512], BF)
    kd = sb("kd", [128, 4, 256], BF)
    vt = sb("vt", [128, 4, 512], BF)
    gs = sb("gs", [128, 4, 512], BF)
    o_sb = sb("o_sb", [128, 4, 512], F32)
    o_tm = sb("o_tm", [128, 4, 512], BF)
    att = sb("att", [128, 1280], BF)
    scr = sb("scr", [128, 3, 516], F32)
    big = sb("big", [128, 24, 512], BF)
    o_memT = sb("o_memT", [128, 8, 512], BF)
    misc = sb("misc", [128, 4096], F32)
    om = sb("om", [128, 4, 256], BF)
    pT4 = sb("pT4", [128, 8, 128], BF)
    cbuf = sb("cbuf", [128, 520], F32)
    small = sb("small", [128, 64], F32)
    wring = [sb(f"w{i}", [128, 8, 512], BF) for i in range(4)]
    psum = [nc.alloc_psum_tensor(f"ps{i}", [128, 512], F32) for i in range(8)]

    xb = xbm[:, :].rearrange("p (s f) -> p s f", s=4)
    merged = xbm[:, :].rearrange("p (c t) -> p c t", c=8)
    macc = misc[:, 0:2048].rearrange("p (c t) -> p c t", c=4)
    yts = [misc[:, 2048:3072], misc[:, 3072:4096]]
    Ss = misc[:, :].rearrange("p (c e) -> p c e", c=8)
    Ssb = big[:, 16:24, :]

    sem_names = ["pe", "act", "dve", "pool", "sp"]
    sems = {k: nc.alloc_semaphore(f"sem_{k}") for k in sem_names}
    dma_sems = {"sp": [nc.alloc_semaphore(f"dsp{i}") for i in range(24)],
                "pool": [nc.alloc_semaphore(f"dpl{i}") for i in range(16)],
                "act": [nc.alloc_semaphore(f"dac{i}") for i in range(24)]}
    cc_sem = nc.alloc_semaphore("cc_sem")
    sc = Sched(nc, sems, dma_sems)

    B = {}

    def nb(name, n=None):
        B[name] = Buf() if n is None else [Buf() for _ in range(n)]
        return B[name]

    b_ident = nb("ident"); b_aux = nb("aux"); b_gfin = nb("gfin"); b_rope = nb("rope")
    b_S = nb("S", 8); b_Sb = nb("Sb", 8)
    b_mkT_p = nb("mkT_p"); b_mv_p = nb("mv_p"); b_mkT_s = nb("mkT_s"); b_mv_s = nb("mv_s")
    b_xt = nb("xt", 4); b_xbm = nb("xbm", 8); b_hT = nb("hT", 8)
    b_qf = nb("qf"); b_kf = nb("kf"); b_kd = nb("kd", 4); b_vt = nb("vt", 4); b_gs = nb("gs", 4)
    b_osb = nb("osb", 4); b_otm = nb("otm", 4); b_att = nb("att"); b_scr = nb("scr", 3)
    b_big = nb("big", 24); b_omT = nb("omT", 8); b_macc = nb("macc"); b_yt = nb("yt", 2)
    b_om = nb("om", 4); b_pT = nb("pT"); b_cbuf = nb("cbuf"); b_small = nb("small", 8)
    b_w = nb("w", 4); b_ps = nb("ps", 8)
    b_ccsrc = nb("ccsrc"); b_ccdst = nb("ccdst")
    b_misc_all = [b_macc, b_yt[0], b_yt[1]]

    st = {"ps": 0, "w": 0, "out_toks": [], "q": "sp", "wmode": "cast", "wn": 0}

    def PS():
        i = st["ps"] % 8
        st["ps"] += 1
        return psum[i], b_ps[i]

    def A(col, n=1):
        return aux[:, col:col + n]

    b_wscr = [Buf() for _ in range(NBLK)]

    def wload(pieces):
        i = st["w"] % 4
        st["w"] += 1
        wt = wring[i]
        mode = st["wmode"]
        kc = pieces[0][1]
        ctot = sum(p[2] for p in pieces)
        n = st["wn"]
        st["wn"] += 1
        if mode == "save0":
            mode = "save" if n % 2 == 0 else "cast"
        elif mode == "save1":
            mode = "scratch" if n % 2 == 0 else "save"
        if mode == "scratch":
            sc.dma("pool", wt[:, 0:kc, 0:ctot], wscr[n, :, 0:kc * ctot].rearrange("p (k c) -> p k c", k=kc),
                   reads=[b_wscr[n]], writes=[b_w[i]])
            return wt, b_w[i]
        c0 = 0
        for (src, kc_, ncols) in pieces:
            sc.dma("pool", wt[:, 0:kc, c0:c0 + ncols], src.rearrange("(k p) n -> p k n", p=128),
                   reads=[], writes=[b_w[i]])
            c0 += ncols
        if mode == "save":
            sc.dma("sp", wscr[n, :, 0:kc * ctot].rearrange("p (k c) -> p k c", k=kc), wt[:, 0:kc, 0:ctot],
                   reads=[b_w[i]], writes=[b_wscr[n]])
        return wt, b_w[i]

    def w_in_blk(col0, ncols=512):
        return wload([(w_in[:, col0:col0 + ncols], 8, ncols)])

    def act(out, in_, func, reads, writes, **kw):
        sc.op("act", lambda e: e.activation(out=out, in_=in_, func=func, **kw), reads, writes)

    def tt(out, in0, in1, op, reads, writes):
        sc.op("dve", lambda e: e.tensor_tensor(out=out, in0=in0, in1=in1, op=op), reads, writes)

    def ts(out, in0, s1, s2, op0, op1, reads, writes):
        if op1 is None:
            sc.op("dve", lambda e: e.tensor_scalar(out=out, in0=in0, scalar1=s1, scalar2=None, op0=op0), reads, writes)
        else:
            sc.op("dve", lambda e: e.tensor_scalar(out=out, in0=in0, scalar1=s1, scalar2=s2, op0=op0, op1=op1), reads, writes)

    def stt(out, in0, scalar, in1, op0, op1, reads, writes):
        sc.op("dve", lambda e: e.scalar_tensor_tensor(out=out, in0=in0, scalar=scalar, in1=in1, op0=op0, op1=op1), reads, writes)

    def mm(ps_ap, lhsT, rhs, start, stop, reads, bps):
        sc.op("pe", lambda e: e.matmul(ps_ap, lhsT=lhsT, rhs=rhs, start=start, stop=stop), reads, [bps])

    def tr(out_ap, in_ap, rows, reads, bps):
        sc.op("pe", lambda e: e.transpose(out=out_ap, in_=in_ap, identity=ident[:rows, :rows]), list(reads) + [b_ident], [bps])

    dbg_d = dout("dbg", [128, 4096]) if STAGE < 99 else None

    def dump(ap, bufs):
        st["out_toks"].append(sc.dma(st["q"], dbg_d[:, 0:ap.shape[-1]] if len(ap.shape) == 2 else dbg_d[:, :], ap, reads=bufs))

    def finish():
        sc.wait_all(st["q"], st["out_toks"])
        with nc.Block() as block:
            @block.sync
            def _(e):
                sc.replay("sp", e)

            @block.gpsimd
            def _(e):
                sc.replay("pool", e)

            @block.scalar
            def _(e):
                sc.replay("act", e)

            @block.vector
            def _(e):
                sc.replay("dve", e)

            @block.tensor
            def _(e):
                sc.replay("pe", e)
        return nc

    sc.dma(st["q"], aux[:, :], aux_d[:, :], writes=[b_aux])
    sc.dma(st["q"], gfin[:, :], gfin_d[:, :], writes=[b_gfin])
    sc.op("pool", lambda e: e.memset(identf[:, :], 0.0), [], [b_ident])
    sc.op("pool", lambda e: e.affine_select(out=identf[:, :], in_=identf[:, :], pattern=[[-1, 128]],
                                            compare_op=ALU.not_equal, fill=1.0, base=0, channel_multiplier=1), [], [b_ident])
    sc.op("dve", lambda e: e.tensor_copy(out=ident[:, :], in_=identf[:, :]), [], [b_ident])
    sc.op("dve", lambda e: e.memset(small[:, :], 0.0), [], b_small)

    def norm_to_fm(segs, src_fn, b_src_fn0, g_col, phase="all"):
        def b_src_fn(si):
            r = b_src_fn0(si)
            return r if isinstance(r, list) else [r]
        n = len(segs)
        ssq = small[:, 0:4]
        rr = small[:, 4:8]
        if phase in ("all", "A"):
            sc.op("dve", lambda e: e.memset(ssq, 0.0), [], [b_small[0]])
            for si, (rows, col0) in enumerate(segs):
                junk = xb[:rows, si, :]
                act(junk, src_fn(si), AF.Square, b_src_fn(si), [b_xbm[2 * si], b_xbm[2 * si + 1], b_small[0]],
                    accum_out=small[:rows, si:si + 1])
            ts(rr[:, :n], ssq[:, :n], 1.0 / D, EPS, ALU.mult, ALU.add, [b_small[0]], [b_small[1]])
            act(rr[:, :n], rr[:, :n], AF.Sqrt, [b_small[1]], [b_small[1]])
            sc.op("dve", lambda e: e.reciprocal(out=rr[:, :n], in_=rr[:, :n]), [b_small[1]], [b_small[1]])
            for si, (rows, col0) in enumerate(segs):
                act(xb[:rows, si, :], src_fn(si), AF.Copy, b_src_fn(si) + [b_small[1]], [b_xbm[2 * si], b_xbm[2 * si + 1]],
                    scale=rr[:rows, si:si + 1])
        if phase in ("all", "B"):
            for si, (rows, col0) in enumerate(segs):
                pt, bp = PS()
                ptb = pt[:, :].bitcast(BF)
                for k in range(8):
                    tr(ptb[:, k * 128:k * 128 + rows], xb[:rows, si, k * 128:(k + 1) * 128], rows,
                       [b_xbm[2 * si], b_xbm[2 * si + 1]], bp)
                tt(hT[:, :, col0:col0 + rows], ptb.rearrange("p (k t) -> p k t", k=8)[:, :, :rows],
                   A(g_col, 8).unsqueeze(2).broadcast_to([128, 8, rows]), ALU.mult, [bp, b_aux], b_hT)

    def mem_finish(mkT, b_mkT, mv, b_mv):
        for i in range(2):
            act(xb[:, i, :], xt[:, i, :], AF.Copy, [b_xt[i]], [b_xbm[2 * i], b_xbm[2 * i + 1]])
            sc.op("dve", lambda e, i=i: e.tensor_copy(out=mv[:, i, :], in_=xt[:, 2 + i, :]), [b_xt[2 + i]], [b_mv])
        for i in range(2):
            pt, bp = PS()
            ptb = pt[:, :].bitcast(BF)
            for c in range(8):
                tr(ptb[:, c * 128:(c + 1) * 128], xb[:, i, c * 128:(c + 1) * 128], 128, [b_xbm[2 * i], b_xbm[2 * i + 1]], bp)
            act(mkT[:, :, i * 128:(i + 1) * 128], ptb.rearrange("p (c t) -> p c t", c=8), AF.Copy, [bp], [b_mkT])

    def load_rope(c0, T, parts):
        for (dc, scol, n) in parts:
            sc.dma(st["q"], rope[:, :, dc:dc + n], rope_d[:, :, scol:scol + n].rearrange("t p n -> p t n"), writes=[b_rope])

    def rotary(ps1, b1, ps2, b2, dst, b_dst, T):
        cosT = rope[:, 0, :T]
        sinT = rope[:, 1, :T]
        t1 = scr[:, 0, :T]
        t2 = scr[:, 1, :T]
        tt(t1, ps1[:, :T], cosT, ALU.mult, [b1, b_rope], [b_scr[0]])
        tt(t2, ps2[:, :T], sinT, ALU.mult, [b2, b_rope], [b_scr[1]])
        tt(dst[:, 0, :T], t1, t2, ALU.subtract, [b_scr[0], b_scr[1]], [b_dst])
        tt(t1, ps1[:, :T], sinT, ALU.mult, [b1, b_rope], [b_scr[0]])
        tt(t2, ps2[:, :T], cosT, ALU.mult, [b2, b_rope], [b_scr[1]])
        tt(dst[:, 1, :T], t1, t2, ALU.add, [b_scr[0], b_scr[1]], [b_dst])

    def fm_proj(wt, bw, wc0, T):
        pt, bp = PS()
        for k in range(8):
            mm(pt[:, :T], wt[:, k, wc0:wc0 + 128], hT[:, k, :T], k == 0, k == 7, [bw, b_hT[k]], bp)
        return pt, bp

    def tm_proj(wt, bw, rows, col0, ncols=512):
        pt, bp = PS()
        for k in range(8):
            mm(pt[:rows, :ncols], hT[:, k, col0:col0 + rows], wt[:, k, :ncols], k == 0, k == 7, [bw, b_hT[k]], bp)
        return pt, bp

    def head_kv(h, segs, T, wqk, bwqk, kc0=256):
        p1, bp1 = fm_proj(wqk, bwqk, kc0, T)
        p2, bp2 = fm_proj(wqk, bwqk, kc0 + 128, T)
        rotary(p1, bp1, p2, bp2, kf, b_kf, T)
        wv, bwv = w_in_blk(C_V + h * 512)
        for si, (rows, col0, kind, _s) in enumerate(segs):
            pt, bp = PS()
            ptb = pt[:, :].bitcast(BF)
            for j in range(2):
                tr(ptb[:rows, j * 128:(j + 1) * 128], kf[:, j, col0:col0 + rows], 128, [b_kf], bp)
            ts(kd[:rows, si, :], ptb[:rows, 0:256], A(A_KDEC + kind * 4 + h)[:rows, :], None, ALU.mult, None,
               [bp, b_aux], [b_kd[si]])
            pv, bpv = tm_proj(wv, bwv, rows, col0)
            act(vt[:rows, si, :], pv[:rows, :], AF.Copy, [bpv], [b_vt[si]])

    SDEC = [[GAM[h] ** 128 for h in range(NH)], [GAM[h] ** 32 for h in range(NH)], [GAM[h] ** 4 for h in range(NH)]]

    def state_update(h, si, rows, kind, Sf, bSf, Sbf, bSbf, with_bf=True):
        for j in range(2):
            c = h * 2 + j
            pt, bp = PS()
            mm(pt[:, :], kd[:rows, si, j * 128:(j + 1) * 128], vt[:rows, si, :], True, True, [b_kd[si], b_vt[si]], bp)
            stt(Sf[:, c, :], Sf[:, c, :], float(SDEC[kind][h]), pt[:, :], ALU.mult, ALU.add, [bp], bSf(c))
            if with_bf:
                act(Sbf[:, c, :], Sf[:, c, :], AF.Copy, bSf(c), bSbf(c))

    mainS = (S, lambda c: [b_S[c]], Sb, lambda c: [b_Sb[c]])
    sampS = (Ss, lambda c: b_misc_all, Ssb, lambda c: [b_big[16 + c]])

    if STAGE <= 0:
        dump(identf[:, :], [b_ident])
        return finish()
    sc.op("dve", lambda e: e.memset(S[:, :, :].rearrange("p c e -> p (c e)"), 0.0), [], b_S)
    NT1 = PRE // 512
    p1sets = [
        dict(xt=xt, bxt=[b_xt[0], b_xt[1], b_xt[2], b_xt[3]], xb=xb, bxb=[[b_xbm[2 * q], b_xbm[2 * q + 1]] for q in range(4)],
             hT=hT, bhT=b_hT, rope=rope, brope=[b_rope]),
        dict(xt=misc[:, :].rearrange("p (s f) -> p s f", s=4), bxt=[b_macc, b_macc, b_yt[0], b_yt[1]],
             xb=o_memT[:, :, :].rearrange("p (s a) t -> p s (a t)", s=4), bxb=[[b_omT[2 * q], b_omT[2 * q + 1]] for q in range(4)],
             hT=big[:, 0:8, :], bhT=b_big[0:8], rope=o_sb[:, 0:2, :], brope=[b_osb[0], b_osb[1]]),
    ]
    hsets = [dict(kf=kf, bkf=[b_kf], kd=kd, bkd=b_kd, vt=vt, bvt=b_vt),
             dict(kf=qf, bkf=[b_qf], kd=o_tm[:, :, 0:256], bkd=b_otm, vt=gs, bvt=b_gs)]

    def p1_heads(it):
        dist = (NT1 - 1 - it) * 512
        return [h for h in range(NH) if GAM[h] ** dist >= 1e-12]

    def p1_front(it, bs):
        r0 = it * 512
        for q in range(4):
            sc.dma(st["q"], bs["xt"][:, q, :], xp[r0 + q * 128:r0 + (q + 1) * 128, :], writes=[bs["bxt"][q]])
        sc.dma(st["q"], bs["rope"][:, :, :], rope_d[:, :, r0:r0 + 512].rearrange("t p n -> p t n"), writes=bs["brope"])
        ssq = small[:, 0:4]
        rr = small[:, 4:8]
        sc.op("dve", lambda e: e.memset(ssq, 0.0), [], [b_small[0]])
        for q in range(4):
            act(bs["xb"][:, q, :], bs["xt"][:, q, :], AF.Square, [bs["bxt"][q]], bs["bxb"][q] + [b_small[0]], accum_out=small[:, q:q + 1])
        ts(rr, ssq, 1.0 / D, EPS, ALU.mult, ALU.add, [b_small[0]], [b_small[1]])
        act(rr, rr, AF.Sqrt, [b_small[1]], [b_small[1]])
        sc.op("dve", lambda e: e.reciprocal(out=rr, in_=rr), [b_small[1]], [b_small[1]])
        for q in range(4):
            act(bs["xb"][:, q, :], bs["xt"][:, q, :], AF.Copy, [bs["bxt"][q], b_small[1]], bs["bxb"][q], scale=rr[:, q:q + 1])
            pt, bp = PS()
            ptb = pt[:, :].bitcast(BF)
            for k in range(8):
                tr(ptb[:, k * 128:(k + 1) * 128], bs["xb"][:, q, k * 128:(k + 1) * 128], 128, bs["bxb"][q], bp)
            tt(bs["hT"][:, :, q * 128:(q + 1) * 128], ptb.rearrange("p (k t) -> p k t", k=8),
               A(A_GMIX, 8).unsqueeze(2).broadcast_to([128, 8, 128]), ALU.mult, [bp, b_aux], bs["bhT"])

    def p1_A(bs, h, hs):
        hT_, bhT_ = bs["hT"], bs["bhT"]
        cosT, sinT = bs["rope"][:, 0, :], bs["rope"][:, 1, :]
        wk, bwk = wload([(w_in[:, C_K + h * 256:C_K + (h + 1) * 256], 8, 256)])
        wv, bwv = w_in_blk(C_V + h * 512)
        pk = []
        for j in range(2):
            pt, bp = PS()
            for k in range(8):
                mm(pt[:, :], wk[:, k, j * 128:(j + 1) * 128], hT_[:, k, :], k == 0, k == 7, [bwk, bhT_[k]], bp)
            pk.append((pt, bp))
        for q in range(4):
            pv, bpv = PS()
            for k in range(8):
                mm(pv[:, :], hT_[:, k, q * 128:(q + 1) * 128], wv[:, k, :], k == 0, k == 7, [bwv, bhT_[k]], bpv)
            act(hs["vt"][:, q, :], pv[:, :], AF.Copy, [bpv], [hs["bvt"][q]])
        (p1_, b1_), (p2_, b2_) = pk
        t1 = scr[:, 0, :512]
        t2 = scr[:, 1, :512]
        kfd = hs["kf"]
        tt(t1, p1_[:, :], cosT, ALU.mult, [b1_] + bs["brope"], [b_scr[0]])
        tt(t2, p2_[:, :], sinT, ALU.mult, [b2_] + bs["brope"], [b_scr[1]])
        tt(kfd[:, 0, :], t1, t2, ALU.subtract, [b_scr[0], b_scr[1]], hs["bkf"])
        tt(t1, p1_[:, :], sinT, ALU.mult, [b1_] + bs["brope"], [b_scr[0]])
        tt(t2, p2_[:, :], cosT, ALU.mult, [b2_] + bs["brope"], [b_scr[1]])
        tt(kfd[:, 1, :], t1, t2, ALU.add, [b_scr[0], b_scr[1]], hs["bkf"])

    def p1_B(h, hs):
        kfd = hs["kf"]
        for q in range(4):
            pt, bp = PS()
            ptb = pt[:, :].bitcast(BF)
            for j in range(2):
                tr(ptb[:, j * 128:(j + 1) * 128], kfd[:, j, q * 128:(q + 1) * 128], 128, hs["bkf"], bp)
            ts(hs["kd"][:, q, :], ptb[:, 0:256], A(A_KDT + q * 4 + h), None, ALU.mult, None, [bp, b_aux], [hs["bkd"][q]])
        for j in range(2):
            c = h * 2 + j
            pt, bp = PS()
            for q in range(4):
                mm(pt[:, :], hs["kd"][:, q, j * 128:(j + 1) * 128], hs["vt"][:, q, :], q == 0, q == 3, [hs["bkd"][q], hs["bvt"][q]], bp)
            stt(S[:, c, :], S[:, c, :], float(GAM[h] ** 512), pt[:, :], ALU.mult, ALU.add, [bp], [b_S[c]])

    p1_front(0, p1sets[0])
    flat = [(it, h) for it in range(NT1) for h in p1_heads(it)]
    prevB = None
    seen_tiles = set()
    for n, (it, h) in enumerate(flat):
        if it not in seen_tiles:
            seen_tiles.add(it)
            if it + 1 < NT1:
                p1_front(it + 1, p1sets[(it + 1) % 2])
        hs = hsets[n % 2]
        p1_A(p1sets[it % 2], h, hs)
        if prevB is not None:
            prevB()
        prevB = (lambda h=h, hs=hs: p1_B(h, hs))
    prevB()
    for c in range(8):
        act(Sb[:, c, :], S[:, c, :], AF.Copy, [b_S[c]], [b_Sb[c]])
    if STAGE <= 1:
        dump(S[:, :, :].rearrange("p c e -> p (c e)"), b_S)
        return finish()
    for i in range(2):
        sc.dma(st["q"], xt[:, i, :], memp[i * 128:(i + 1) * 128, :], writes=[b_xt[i]])
    norm_to_fm([(128, 0), (128, 128)], lambda si: xt[:, si, :], lambda si: b_xt[si], A_GMEM)
    for nbk in range(4):
        wt, bw = wload([(w_mem_kv[:, nbk * 512:(nbk + 1) * 512], 8, 512)])
        for i in range(2):
            pt, bp = tm_proj(wt, bw, 128, i * 128)
            kvi = (nbk // 2) * 2 + i
            act(xt[:, kvi, (nbk % 2) * 512:(nbk % 2 + 1) * 512], pt[:, :], AF.Copy, [bp], [b_xt[kvi]])
    for i in range(2):
        st["out_toks"].append(sc.dma(st["q"], o_mk[i * 128:(i + 1) * 128, :], xt[:, i, :], reads=[b_xt[i]]))
        st["out_toks"].append(sc.dma(st["q"], o_mv[i * 128:(i + 1) * 128, :], xt[:, 2 + i, :], reads=[b_xt[2 + i]]))
    mem_finish(mkT_p, b_mkT_p, mv_p, b_mv_p)
    for i in range(2):
        sc.dma(st["q"], xt[:, i, :], cmk[i * 128:(i + 1) * 128, :], writes=[b_xt[i]])
        sc.dma(st["q"], xt[:, 2 + i, :], cmv[i * 128:(i + 1) * 128, :], writes=[b_xt[2 + i]])
    mem_finish(mkT_s, b_mkT_s, mv_s, b_mv_s)

    if STAGE <= 3:
        return finish()
    if STAGE <= 4:
        dump(S[:, :, :].rearrange("p c e -> p (c e)"), b_S)
        return finish()
    zero2 = small[:, 8:10]
    sc.op("dve", lambda e: e.memset(small[:, 8:16], 0.0), [], [b_small[2]])
    chalo = sb("chalo", [128, 8, 2], F32); b_chalo = nb("chalo", 8)
    ahalo = sb("ahalo", [128, NFF, 2], F32); b_ahalo = nb("ahalo", NFF)
    shalo_c = sb("shalo_c", [128, 8, 2], F32); b_shalo_c = nb("shalo_c", 8)
    shalo_a = sb("shalo_a", [128, NFF, 2], F32); b_shalo_a = nb("shalo_a", NFF)

    def tile_front(segs, phase):
        tmp = [(o_sb[:, 0:2, :].rearrange("p a b -> p (a b)"), [b_osb[0], b_osb[1]]),
               (o_sb[:, 2:4, :].rearrange("p a b -> p (a b)"), [b_osb[2], b_osb[3]]),
               (gs[:, :, :].rearrange("p a b -> p (a b)").bitcast(F32), list(b_gs)),
               (vt[:, :, :].rearrange("p a b -> p (a b)").bitcast(F32), list(b_vt))]
        if phase == "A":
            for si, s_ in enumerate(segs):
                sc.dma(st["q"], tmp[si][0], s_["xsrc"], writes=tmp[si][1])
            load_rope(0, 512, [(0, segs[0]["rope"], 512)])
        norm_to_fm([(128, q * 128) for q in range(4)], lambda si: tmp[si][0], lambda si: tmp[si][1], A_GMIX, phase)

    def run_tile(segs, convsegs, is_mini, prefetched=False, next_front=None, wmode="scratch"):
        T = sum(s["rows"] for s in segs)
        nseg = len(segs)
        st["wmode"] = wmode
        st["wn"] = 0
        for si, s in enumerate(segs):
            sc.dma(st["q"], xt[:s["rows"], si, :], s["xsrc"], writes=[b_xt[si]])
        if not prefetched:
            if is_mini:
                load_rope(0, T, [(s["col0"], s["rope"], s["rows"]) for s in segs])
            else:
                load_rope(0, T, [(0, segs[0]["rope"], T)])
            norm_to_fm([(s["rows"], s["col0"]) for s in segs], lambda si: xt[:segs[si]["rows"], si, :], lambda si: b_xt[si], A_GMIX)

        pending = [None]
        for h in range(NH):
            wqk, bwqk = wload([(w_in[:, C_Q + h * 256:C_Q + (h + 1) * 256], 8, 256),
                               (w_in[:, C_K + h * 256:C_K + (h + 1) * 256], 8, 256)])
            wv, bwv = w_in_blk(C_V + h * 512)
            wg, bwg = w_in_blk(C_GR + h * 512)
            q1, bq1 = fm_proj(wqk, bwqk, 0, T)
            q2, bq2 = fm_proj(wqk, bwqk, 128, T)
            k1, bk1 = fm_proj(wqk, bwqk, 256, T)
            k2, bk2 = fm_proj(wqk, bwqk, 384, T)
            rotary(q1, bq1, q2, bq2, qf, b_qf, T)
            rotary(k1, bk1, k2, bk2, kf, b_kf, T)
            for si, s in enumerate(segs):
                rows, col0 = s["rows"], s["col0"]
                pv, bpv = tm_proj(wv, bwv, rows, col0)
                act(vt[:rows, si, :], pv[:rows, :], AF.Copy, [bpv], [b_vt[si]])
            if is_mini and pending[0] is not None:
                pending[0]()
                pending[0] = None

            def gr_proj():
                for si, s in enumerate(segs):
                    rows, col0 = s["rows"], s["col0"]
                    pg, bpg = tm_proj(wg, bwg, rows, col0)
                    act(gs[:rows, si, :], pg[:rows, :], AF.Silu, [bpg], [b_gs[si]])
            gr_early = True
            if gr_early:
                gr_proj()
            for si, s in enumerate(segs):
                rows, col0, kind = s["rows"], s["col0"], s["kind"]
                pt, bp = PS()
                ptb = pt[:, :].bitcast(BF)
                for j in range(2):
                    tr(ptb[:rows, j * 128:(j + 1) * 128], kf[:, j, col0:col0 + rows], 128, [b_kf], bp)
                kcol = (A_KDEC + kind * 4 + h) if is_mini else (A_KDT + si * 4 + h)
                ts(kd[:rows, si, :], ptb[:rows, 0:256], A(kcol)[:rows, :], None, ALU.mult, None,
                   [bp, b_aux], [b_kd[si]])
            stats = small[:, 16:40].rearrange("p (s k) -> p s k", s=4)
            mvs = small[:, 40:48].rearrange("p (s k) -> p s k", s=4)
            if is_mini:
                for si, s in enumerate(segs):
                    rows, col0, kind = s["rows"], s["col0"], s["kind"]
                    Sf, bSf, Sbf, bSbf = s["state"]
                    mk = 0
                    sT = att[:, 0:128]
                    pss, bpss = PS()
                    for j in range(2):
                        mm(pss[:rows, :rows], kf[:, j, col0:col0 + rows], qf[:, j, col0:col0 + rows], j == 0, j == 1, [b_kf, b_qf], bpss)
                    tt(sT[:rows, :rows], pss[:rows, :rows], aux[:rows, A_MASK + (mk * 4 + h) * 128:A_MASK + (mk * 4 + h) * 128 + rows],
                       ALU.mult, [bpss, b_aux], [b_att])
                    po, bpo = PS()
                    mm(po[:rows, :], sT[:rows, :rows], vt[:rows, si, :], True, False, [b_att, b_vt[si]], bpo)
                    for j in range(2):
                        mm(po[:rows, :], qf[:, j, col0:col0 + rows], Sbf[:, h * 2 + j, :], False, j == 1, [b_qf] + bSbf(h * 2 + j), bpo)
                    act(o_sb[:rows, si, :], po[:rows, :], AF.Copy, [bpo, b_aux], [b_osb[si]], scale=A(A_ROWD + mk * 4 + h)[:rows, :])
                    state_update(h, si, rows, kind, Sf, bSf, Sbf, bSbf)
                    sc.op("dve", lambda e, rows=rows, si=si: e.bn_stats(out=stats[:rows, si, :], in_=o_sb[:rows, si, :]), [b_osb[si]], [b_small[3]])
                    sc.op("dve", lambda e, rows=rows, si=si: e.bn_aggr(out=mvs[:rows, si, :], in_=stats[:rows, si, :]), [b_small[3]], [b_small[4]])
            else:
                offs = [0, 512, 896, 1152]
                for kp in range(4):
                    N = (4 - kp) * 128
                    pss, bpss = PS()
                    for j in range(2):
                        mm(pss[:, :N], kf[:, j, kp * 128:(kp + 1) * 128], qf[:, j, kp * 128:512], j == 0, j == 1, [b_kf, b_qf], bpss)
                    stt(att[:, offs[kp]:offs[kp] + 128], pss[:, 0:128], float(GAM[h] ** (-128.0 * kp)),
                        aux[:, A_MASK + h * 128:A_MASK + (h + 1) * 128], ALU.mult, ALU.mult, [bpss, b_aux], [b_att])
                    if N > 128:
                        ts(att[:, offs[kp] + 128:offs[kp] + N], pss[:, 128:N], A(A_KSC + kp * 4 + h), None, ALU.mult, None,
                           [bpss, b_aux], [b_att])
                if pending[0] is not None:
                    pending[0]()
                    pending[0] = None
                if not gr_early:
                    gr_proj()
                spend = []
                for j in range(2):
                    pt, bp = PS()
                    for si in range(4):
                        mm(pt[:, :], kd[:, si, j * 128:(j + 1) * 128], vt[:, si, :], si == 0, si == 3, [b_kd[si], b_vt[si]], bp)
                    spend.append((pt, bp))
                for si in range(4):
                    po, bpo = PS()
                    for kp in range(si + 1):
                        o0 = offs[kp] + (si - kp) * 128
                        mm(po[:, :], att[:, o0:o0 + 128], vt[:, kp, :], kp == 0, False, [b_att, b_vt[kp]], bpo)
                    for j in range(2):
                        mm(po[:, :], qf[:, j, si * 128:(si + 1) * 128], Sb[:, h * 2 + j, :], False, j == 1, [b_qf, b_Sb[h * 2 + j]], bpo)
                    act(o_sb[:, si, :], po[:, :], AF.Copy, [bpo, b_aux], [b_osb[si]], scale=A(A_ROWT + si * 4 + h))
                    sc.op("dve", lambda e, si=si: e.bn_stats(out=stats[:, si, :], in_=o_sb[:, si, :]), [b_osb[si]], [b_small[3]])
                    sc.op("dve", lambda e, si=si: e.bn_aggr(out=mvs[:, si, :], in_=stats[:, si, :]), [b_small[3]], [b_small[4]])
                for j in range(2):
                    c = h * 2 + j
                    pt, bp = spend[j]
                    stt(S[:, c, :], S[:, c, :], float(GAM[h] ** 512), pt[:, :], ALU.mult, ALU.add, [bp], [b_S[c]])
                    act(Sb[:, c, :], S[:, c, :], AF.Copy, [b_S[c]], [b_Sb[c]])
            rstd = small[:, 48:52]
            ts(rstd[:, :nseg], mvs[:, :nseg, 1], EPS, None, ALU.add, None, [b_small[4]], [b_small[5]])
            act(rstd[:, :nseg], rstd[:, :nseg], AF.Sqrt, [b_small[5]], [b_small[5]])
            sc.op("dve", lambda e: e.reciprocal(out=rstd[:, :nseg], in_=rstd[:, :nseg]), [b_small[5]], [b_small[5]])
            for si, s in enumerate(segs):
                rows, col0 = s["rows"], s["col0"]
                ts(o_sb[:rows, si, :], o_sb[:rows, si, :], mvs[:rows, si, 0:1], rstd[:rows, si:si + 1], ALU.subtract, ALU.mult,
                   [b_small[4], b_small[5]], [b_osb[si]])
                tt(o_tm[:rows, si, :], o_sb[:rows, si, :], gs[:rows, si, :], ALU.mult, [b_osb[si], b_gs[si]], [b_otm[si]])

            def fin(h=h):
                for si, s in enumerate(segs):
                    rows, col0 = s["rows"], s["col0"]
                    pt, bp = PS()
                    ptb = pt[:, :].bitcast(BF)
                    for c in range(4):
                        tr(ptb[:, c * 128:c * 128 + rows], o_tm[:rows, si, c * 128:(c + 1) * 128], rows, [b_otm[si]], bp)
                    for c in range(4):
                        act(big[:, h * 4 + c, col0:col0 + rows], ptb[:, c * 128:c * 128 + rows], AF.Copy, [bp, b_aux], [b_big[h * 4 + c]],
                            scale=A(A_GGN + h * 4 + c))
            pending[0] = fin
        for s in segs:
            if s["kind"] == 1:
                st["out_toks"].append(sc.dma(st["q"], s_sret[:, :], misc[:, :], reads=b_misc_all))

        def conv_gen():
            for cg in range(2):
                wcc, bwcc = w_in_blk(C_CC + cg * 512)
                wcx, bwcx = w_in_blk(C_CX + cg * 512)
                wcb, bwcb = w_in_blk(C_CB + cg * 512)
                if is_mini:
                    c0 = cg * 4
                    banks = []
                    for (ww, bww) in ((wcc, bwcc), (wcx, bwcx), (wcb, bwcb)):
                        pp, bpp = PS()
                        for c4 in range(4):
                            for k in range(8):
                                mm(pp[:, c4 * T:(c4 + 1) * T], ww[:, k, c4 * 128:(c4 + 1) * 128], hT[:, k, :T], k == 0, k == 7, [bww, b_hT[k]], bpp)
                        banks.append((pp[:, 0:4 * T].rearrange("p (c t) -> p c t", c=4), bpp))
                    (pcc3, bpcc), (pcx3, bpcx), (pcb3, bpcb) = banks
                    if pending[0] is not None:
                        pending[0]()
                        pending[0] = None
                    wv = aux[:, A_WCONV + c0 * 3:A_WCONV + (c0 + 4) * 3].rearrange("p (c j) -> p c j", j=3)
                    off = 0
                    o2 = 0
                    for cs in convsegs:
                        f0, n = cs["col0"], cs["n"]
                        hb, bhb = cs["cbufh"]
                        bsl = bhb[c0:c0 + 4]
                        W_ = 4 * (n + 2)
                        cb3 = cbuf[:, off:off + W_].rearrange("p (c t) -> p c t", c=4)
                        ty3 = scr[:, 1, o2:o2 + 4 * n].rearrange("p (c t) -> p c t", c=4)
                        tm3 = scr[:, 2, o2:o2 + 4 * n].rearrange("p (c t) -> p c t", c=4)
                        sc.op("dve", lambda e, cb3=cb3, hb=hb, c0=c0: e.tensor_copy(out=cb3[:, :, 0:2], in_=hb[:, c0:c0 + 4, :]), bsl, [b_cbuf])
                        act(cb3[:, :, 2:2 + n], pcc3[:, :, f0:f0 + n], AF.Copy, [bpcc], [b_cbuf])
                        tt(cb3[:, :, 2:2 + n], cb3[:, :, 2:2 + n], pcx3[:, :, f0:f0 + n], ALU.mult, [bpcx], [b_cbuf])
                        sc.op("dve", lambda e, cb3=cb3, hb=hb, n=n, c0=c0: e.tensor_copy(out=hb[:, c0:c0 + 4, :], in_=cb3[:, :, n:n + 2]), [b_cbuf], bsl)
                        tt(ty3, cb3[:, :, 2:2 + n], wv[:, :, 2:3].broadcast_to([128, 4, n]), ALU.mult, [b_cbuf, b_aux], [b_scr[1]])
                        tt(tm3, cb3[:, :, 1:1 + n], wv[:, :, 1:2].broadcast_to([128, 4, n]), ALU.mult, [b_cbuf, b_aux], [b_scr[2]])
                        tt(ty3, ty3, tm3, ALU.add, [b_scr[2]], [b_scr[1]])
                        tt(tm3, cb3[:, :, 0:n], wv[:, :, 0:1].broadcast_to([128, 4, n]), ALU.mult, [b_cbuf, b_aux], [b_scr[2]])
                        tt(ty3, ty3, tm3, ALU.add, [b_scr[2]], [b_scr[1]])
                        tt(big[:, 16 + c0:16 + c0 + 4, f0:f0 + n], ty3, pcb3[:, :, f0:f0 + n], ALU.mult, [b_scr[1], bpcb], b_big[16 + c0:16 + c0 + 4])
                        off += W_
                        o2 += 4 * n
                    yield
                    continue
                for c4 in range(4):
                    c = cg * 4 + c4
                    pcc, bpcc = fm_proj(wcc, bwcc, c4 * 128, T)
                    pcx, bpcx = fm_proj(wcx, bwcx, c4 * 128, T)
                    pcb, bpcb = fm_proj(wcb, bwcb, c4 * 128, T)
                    if pending[0] is not None:
                        pending[0]()
                        pending[0] = None
                    cb_, bcb_ = (cbuf, b_cbuf) if c % 2 == 0 else (scr[:, 0, :], b_scr[0])
                    tyb, btyb = (scr[:, 1, :], b_scr[1]) if c % 2 == 0 else (scr[:, 2, :], b_scr[2])
                    off = 0
                    for cs in convsegs:
                        f0, n = cs["col0"], cs["n"]
                        hin, bhin = cs["cin"](c)
                        sc.op("dve", lambda e, off=off, hin=hin, cb_=cb_: e.tensor_copy(out=cb_[:, off:off + 2], in_=hin), [bhin], [bcb_])
                        act(cb_[:, off + 2:off + 2 + n], pcc[:, f0:f0 + n], AF.Copy, [bpcc], [bcb_])
                        tt(cb_[:, off + 2:off + 2 + n], cb_[:, off + 2:off + 2 + n], pcx[:, f0:f0 + n], ALU.mult, [bpcx], [bcb_])
                        hout, bhout = cs["cout"](c)
                        sc.op("dve", lambda e, off=off, n=n, hout=hout, cb_=cb_: e.tensor_copy(out=hout, in_=cb_[:, off + n:off + n + 2]), [bcb_], [bhout])
                        ty = tyb[:, :n]
                        ts(ty, cb_[:, off + 2:off + 2 + n], A(A_WCONV + c * 3 + 2), None, ALU.mult, None, [bcb_, b_aux], [btyb])
                        stt(ty, cb_[:, off + 1:off + 1 + n], A(A_WCONV + c * 3 + 1), ty, ALU.mult, ALU.add, [bcb_], [btyb])
                        stt(ty, cb_[:, off:off + n], A(A_WCONV + c * 3 + 0), ty, ALU.mult, ALU.add, [bcb_], [btyb])
                        tt(big[:, 16 + c, f0:f0 + n], ty, pcb[:, f0:f0 + n], ALU.mult, [btyb, bpcb], [b_big[16 + c]])
                        off += n + 2
                    yield
        def mem_gen():
            for h in range(NH):
                if h % 2 == 0:
                    wmq, bwmq = w_in_blk(C_MQ + (h // 2) * 512)
                mqf = qf
                for j in range(2):
                    pq, bpq = fm_proj(wmq, bwmq, (h % 2) * 256 + j * 128, T)
                    act(mqf[:, j, :T], pq[:, :T], AF.Copy, [bpq], [b_qf], scale=1.0 / 16.0)
                memsel = [((mkT_s, b_mkT_s, mv_s, b_mv_s) if s_["mem"] == "s" else (mkT_p, b_mkT_p, mv_p, b_mv_p)) for s_ in segs]
                pexp4 = att[:, 0:1024].rearrange("p (s m) -> p s m", s=4)
                nmx = small[:, 52:56]
                ssum = small[:, 56:60]
                sc.op("dve", lambda e: e.memset(ssum, 0.0), [], [b_small[7]])
                psl = []
                for si, s in enumerate(segs):
                    rows, col0 = s["rows"], s["col0"]
                    mkT, bmkT, mv, bmv = memsel[si]
                    pss, bpss = PS()
                    for j in range(2):
                        mm(pss[:rows, :256], mqf[:, j, col0:col0 + rows], mkT[:, h * 2 + j, :], j == 0, j == 1, [b_qf, bmkT], bpss)
                    psl.append((pss, bpss))
                for si, s in enumerate(segs):
                    rows = s["rows"]
                    pss, bpss = psl[si]
                    sc.op("dve", lambda e, rows=rows, pss=pss, si=si: e.tensor_reduce(out=nmx[:rows, si:si + 1], in_=pss[:rows, :256], axis=AX.X, op=ALU.max, negate=True),
                          [bpss], [b_small[6]])
                    act(pexp4[:rows, si, :], pss[:rows, :256], AF.Exp, [bpss, b_small[6]], [b_att, b_small[7]], bias=nmx[:rows, si:si + 1],
                        accum_out=ssum[:rows, si:si + 1])
                yield
                pt, bp = PS()
                ptb = pt[:, :].bitcast(BF)
                for si, s in enumerate(segs):
                    rows = s["rows"]
                    for i in range(2):
                        tr(ptb[:, (si * 2 + i) * 128:(si * 2 + i) * 128 + rows], pexp4[:rows, si, i * 128:(i + 1) * 128], rows, [b_att], bp)
                for si, s in enumerate(segs):
                    rows = s["rows"]
                    act(pT4[:, si * 2:si * 2 + 2, :rows], ptb[:, si * 256:(si + 1) * 256].rearrange("p (i t) -> p i t", i=2)[:, :, :rows], AF.Copy, [bp], [b_pT])
                yield
                sc.op("dve", lambda e: e.reciprocal(out=ssum, in_=ssum), [b_small[7]], [b_small[7]])
                for si, s in enumerate(segs):
                    rows = s["rows"]
                    mkT, bmkT, mv, bmv = memsel[si]
                    pom, bpom = PS()
                    for i in range(2):
                        mm(pom[:rows, :256], pT4[:, si * 2 + i, :rows], mv[:, i, h * 256:(h + 1) * 256], i == 0, i == 1, [b_pT, bmv], bpom)
                    ts(om[:rows, si, :], pom[:rows, :256], ssum[:rows, si:si + 1], None, ALU.mult, None, [bpom, b_small[7]], [b_om[si]])
                yield
                pt, bp = PS()
                ptb = pt[:, :].bitcast(BF)
                for si, s in enumerate(segs):
                    rows = s["rows"]
                    for i in range(2):
                        tr(ptb[:, (si * 2 + i) * 128:(si * 2 + i) * 128 + rows], om[:rows, si, i * 128:(i + 1) * 128], rows, [b_om[si]], bp)
                for si, s in enumerate(segs):
                    rows, col0 = s["rows"], s["col0"]
                    for i in range(2):
                        act(o_memT[:, h * 2 + i, col0:col0 + rows], ptb[:, (si * 2 + i) * 128:(si * 2 + i) * 128 + rows], AF.Copy, [bp], [b_omT[h * 2 + i]])
                yield
        gm, gc = mem_gen(), conv_gen()
        alive_m, alive_c = True, True
        it_ = 0
        while alive_m or alive_c:
            if alive_m:
                try:
                    next(gm)
                except StopIteration:
                    alive_m = False
            conv_now = (it_ % 2 == 0 and (it_ == 0 or it_ >= 8)) if is_mini else (it_ % 4 in (0, 1))
            if alive_c and (conv_now or not alive_m):
                try:
                    next(gc)
                except StopIteration:
                    alive_c = False
            it_ += 1

        for cg in range(2):
            for br in range(3):
                if br == 0:
                    wsrc, nk, inbuf, binb = w_br_ret, 16, (lambda k: big[:, k, :T]), (lambda k: b_big[k])
                elif br == 1:
                    wsrc, nk, inbuf, binb = w_br_conv, 8, (lambda k: big[:, 16 + k, :T]), (lambda k: b_big[16 + k])
                else:
                    wsrc, nk, inbuf, binb = w_br_mem, 8, (lambda k: o_memT[:, k, :T]), (lambda k: b_omT[k])
                wgt, bwgt = w_in_blk(C_G + br * 1024 + cg * 512)
                pbs = [PS() for _ in range(4)]
                for kb in range(nk // 8):
                    wb, bwb = wload([(wsrc[kb * 1024:(kb + 1) * 1024, cg * 512:(cg + 1) * 512], 8, 512)])
                    for c4 in range(4):
                        for k in range(8):
                            kk = kb * 8 + k
                            mm(pbs[c4][0][:, :T], wb[:, k, c4 * 128:(c4 + 1) * 128], inbuf(kk), kk == 0, kk == nk - 1, [bwb, binb(kk)], pbs[c4][1])
                for c4 in range(4):
                    pg, bpg = fm_proj(wgt, bwgt, c4 * 128, T)
                    sg = scr[:, 2, :T]
                    act(sg, pg[:, :T], AF.Sigmoid, [bpg], [b_scr[2]])
                    if br == 0:
                        tt(macc[:, c4, :T], sg, pbs[c4][0][:, :T], ALU.mult, [b_scr[2], pbs[c4][1]], [b_macc])
                    else:
                        tt(sg, sg, pbs[c4][0][:, :T], ALU.mult, [b_scr[2], pbs[c4][1]], [b_scr[2]])
                        if br == 1:
                            tt(macc[:, c4, :T], macc[:, c4, :T], sg, ALU.add, [b_scr[2]], [b_macc])
                        else:
                            tt(merged[:, cg * 4 + c4, :T], macc[:, c4, :T], sg, ALU.add, [b_scr[2], b_macc], [b_xbm[cg * 4 + c4]])

        for nbk in range(2):
            wt, bw = wload([(w_out[:, nbk * 512:(nbk + 1) * 512], 8, 512)])
            for si, s in enumerate(segs):
                rows, col0 = s["rows"], s["col0"]
                pt, bp = PS()
                for k in range(8):
                    mm(pt[:rows, :], merged[:, k, col0:col0 + rows], wt[:, k, :], k == 0, k == 7, [bw, b_xbm[k]], bp)
                tt(xt[:rows, si, nbk * 512:(nbk + 1) * 512], xt[:rows, si, nbk * 512:(nbk + 1) * 512], pt[:rows, :], ALU.add, [bp], [b_xt[si]])

        norm_to_fm([(s["rows"], s["col0"]) for s in segs], lambda si: xt[:segs[si]["rows"], si, :], lambda si: b_xt[si], A_GFFN)
        ffn_prev = []
        for blk in range(6):
            ncol = 512 if blk < 5 else 256
            wa, bwa = wload([(w_ffn_in[:, blk * 512:blk * 512 + ncol], 8, ncol)])
            wu, bwu = wload([(w_ffn_in[:, DFF + blk * 512:DFF + blk * 512 + ncol], 8, ncol)])
            if is_mini:
                nc_ = ncol // 128
                c0 = blk * 4
                pa, bpa = PS()
                pu, bpu = PS()
                for (pp, bpp, ww, bww) in ((pa, bpa, wa, bwa), (pu, bpu, wu, bwu)):
                    for c4 in range(nc_):
                        for k in range(8):
                            mm(pp[:, c4 * T:(c4 + 1) * T], ww[:, k, c4 * 128:(c4 + 1) * 128], hT[:, k, :T], k == 0, k == 7, [bww, b_hT[k]], bpp)
                pa3 = pa[:, 0:nc_ * T].rearrange("p (c t) -> p c t", c=nc_)
                pu3 = pu[:, 0:nc_ * T].rearrange("p (c t) -> p c t", c=nc_)
                wv = aux[:, A_WFC + c0 * 3:A_WFC + (c0 + nc_) * 3].rearrange("p (c j) -> p c j", j=3)
                off = 0
                o2 = 0
                for cs in convsegs:
                    f0, n = cs["col0"], cs["n"]
                    ab, bab = cs["abuf"]
                    bsl = bab[c0:c0 + nc_]
                    W_ = nc_ * (n + 2)
                    cb3 = cbuf[:, off:off + W_].rearrange("p (c t) -> p c t", c=nc_)
                    ty3 = scr[:, 1, o2:o2 + nc_ * n].rearrange("p (c t) -> p c t", c=nc_)
                    tm3 = scr[:, 2, o2:o2 + nc_ * n].rearrange("p (c t) -> p c t", c=nc_)
                    sc.op("dve", lambda e, cb3=cb3, ab=ab, c0=c0, nc_=nc_: e.tensor_copy(out=cb3[:, :, 0:2], in_=ab[:, c0:c0 + nc_, :]), bsl, [b_cbuf])
                    act(cb3[:, :, 2:2 + n], pa3[:, :, f0:f0 + n], AF.Copy, [bpa], [b_cbuf])
                    sc.op("dve", lambda e, cb3=cb3, ab=ab, n=n, c0=c0, nc_=nc_: e.tensor_copy(out=ab[:, c0:c0 + nc_, :], in_=cb3[:, :, n:n + 2]), [b_cbuf], bsl)
                    tt(ty3, cb3[:, :, 2:2 + n], wv[:, :, 2:3].broadcast_to([128, nc_, n]), ALU.mult, [b_cbuf, b_aux], [b_scr[1]])
                    tt(tm3, cb3[:, :, 1:1 + n], wv[:, :, 1:2].broadcast_to([128, nc_, n]), ALU.mult, [b_cbuf, b_aux], [b_scr[2]])
                    tt(ty3, ty3, tm3, ALU.add, [b_scr[2]], [b_scr[1]])
                    tt(tm3, cb3[:, :, 0:n], wv[:, :, 0:1].broadcast_to([128, nc_, n]), ALU.mult, [b_cbuf, b_aux], [b_scr[2]])
                    tt(ty3, ty3, tm3, ALU.add, [b_scr[2]], [b_scr[1]])
                    act(ty3, ty3, AF.Silu, [b_scr[1]], [b_scr[1]])
                    tt(big[:, c0:c0 + nc_, f0:f0 + n], ty3, pu3[:, :, f0:f0 + n], ALU.mult, [b_scr[1], bpu], b_big[c0:c0 + nc_])
                    off += W_
                    o2 += nc_ * n
                continue
            for c4 in range(ncol // 128):
                c = blk * 4 + c4
                pa, bpa = fm_proj(wa, bwa, c4 * 128, T)
                pu, bpu = fm_proj(wu, bwu, c4 * 128, T)
                ffn_fin = []
                cb_, bcb_ = (cbuf, b_cbuf) if c % 2 == 0 else (scr[:, 0, :], b_scr[0])
                tyb, btyb = (scr[:, 1, :], b_scr[1]) if c % 2 == 0 else (scr[:, 2, :], b_scr[2])
                off = 0
                for cs in convsegs:
                    f0, n = cs["col0"], cs["n"]
                    hin, bhin = cs["ain"](c)
                    sc.op("dve", lambda e, off=off, hin=hin, cb_=cb_: e.tensor_copy(out=cb_[:, off:off + 2], in_=hin), [bhin], [bcb_])
                    act(cb_[:, off + 2:off + 2 + n], pa[:, f0:f0 + n], AF.Copy, [bpa], [bcb_])
                    hout, bhout = cs["aout"](c)
                    if cs.get("hv_cols"):
                        ts(cb_[:, off + 2:off + 2 + cs["hv_cols"]], cb_[:, off + 2:off + 2 + cs["hv_cols"]], A(A_HV), None, ALU.mult, None,
                           [b_aux], [bcb_])
                    sc.op("dve", lambda e, off=off, n=n, hout=hout, cb_=cb_: e.tensor_copy(out=hout, in_=cb_[:, off + n:off + n + 2]), [bcb_], [bhout])
                    ty = tyb[:, off:off + n]
                    ts(ty, cb_[:, off + 2:off + 2 + n], A(A_WFC + c * 3 + 2), None, ALU.mult, None, [bcb_, b_aux], [btyb])
                    stt(ty, cb_[:, off + 1:off + 1 + n], A(A_WFC + c * 3 + 1), ty, ALU.mult, ALU.add, [bcb_], [btyb])
                    stt(ty, cb_[:, off:off + n], A(A_WFC + c * 3 + 0), ty, ALU.mult, ALU.add, [bcb_], [btyb])
                    act(ty, ty, AF.Silu, [btyb], [btyb])
                    ffn_fin.append((lambda c=c, f0=f0, n=n, ty=ty, pu=pu, btyb=btyb, bpu=bpu:
                                    tt(big[:, c, f0:f0 + n], ty, pu[:, f0:f0 + n], ALU.mult, [btyb, bpu], [b_big[c]])))
                    off += n + 2
                for fn_ in ffn_prev:
                    fn_()
                ffn_prev[:] = ffn_fin
                ffn_fin = []
        for fn_ in ffn_prev:
            fn_()
        if next_front is not None:
            next_front("A")
        for nbk in range(2):
            pds = [PS() for _ in range(nseg)]
            for kb in range(3):
                kc = 8 if kb < 2 else 6
                wt, bw = wload([(w_ffn_down[kb * 1024:kb * 1024 + kc * 128, nbk * 512:(nbk + 1) * 512], kc, 512)])
                for si, s in enumerate(segs):
                    rows, col0 = s["rows"], s["col0"]
                    for k in range(kc):
                        kk = kb * 8 + k
                        mm(pds[si][0][:rows, :], big[:, kk, col0:col0 + rows], wt[:, k, :], kk == 0, kk == NFF - 1, [bw, b_big[kk]], pds[si][1])
            for si, s in enumerate(segs):
                rows = s["rows"]
                tt(xt[:rows, si, nbk * 512:(nbk + 1) * 512], xt[:rows, si, nbk * 512:(nbk + 1) * 512], pds[si][0][:rows, :], ALU.add,
                   [pds[si][1]], [b_xt[si]])

        if next_front is not None:
            next_front("B")
        ssq = small[:, 0:4]
        rr = small[:, 4:8]
        sc.op("dve", lambda e: e.memset(ssq, 0.0), [], [b_small[0]])
        for si, s in enumerate(segs):
            rows = s["rows"]
            act(xb[:rows, si, :], xt[:rows, si, :], AF.Square, [b_xt[si]], [b_xbm[2 * si], b_xbm[2 * si + 1], b_small[0]],
                accum_out=small[:rows, si:si + 1])
        ts(rr[:, :nseg], ssq[:, :nseg], 1.0 / D, EPS, ALU.mult, ALU.add, [b_small[0]], [b_small[1]])
        act(rr[:, :nseg], rr[:, :nseg], AF.Sqrt, [b_small[1]], [b_small[1]])
        sc.op("dve", lambda e: e.reciprocal(out=rr[:, :nseg], in_=rr[:, :nseg]), [b_small[1]], [b_small[1]])
        for si, s in enumerate(segs):
            rows = s["rows"]
            if s["ydst"] is None:
                continue
            yt = yts[si % 2]
            byt = b_yt[si % 2]
            act(yt[:rows, :], xt[:rows, si, :], AF.Copy, [b_xt[si], b_small[1]], [byt], scale=rr[:rows, si:si + 1])
            tt(yt[:rows, :], yt[:rows, :], gfin[:rows, :], ALU.mult, [b_gfin], [byt])
            p0 = s.get("p0", 0)
            st["out_toks"].append(sc.dma(st["q"], s["ydst"], yt[p0:rows, :], reads=[byt]))

    sc.op("dve", lambda e: e.tensor_copy(out=shalo_c[:, :, :].rearrange("p c r -> p (c r)"), in_=aux[:, A_SCONV:A_SCONV + 16]), [b_aux], b_shalo_c)
    sc.op("dve", lambda e: e.tensor_copy(out=shalo_a[:, :, :].rearrange("p c r -> p (c r)"), in_=aux[:, A_SFFN:A_SFFN + 44]), [b_aux], b_shalo_a)
    main_segs = []
    for it in range(4):
        r0 = PRE + it * 512
        segl = []
        for q in range(4):
            y0 = it * 512 + q * 128 - 4
            d = dict(rows=128, col0=q * 128, kind=0, xsrc=xp[r0 + q * 128:r0 + (q + 1) * 128, :], rope=r0, mem="p", state=mainS)
            if y0 < 0:
                d.update(ydst=y_d[0:124, :], p0=4)
            else:
                d.update(ydst=y_d[y0:y0 + 128, :])
            segl.append(d)
        main_segs.append(segl)

    def mk_conv(first):
        d = dict(col0=0, n=512,
                 cin=(lambda c: (zero2, b_small[2])) if first else (lambda c: (chalo[:, c, :], b_chalo[c])),
                 cout=lambda c: (chalo[:, c, :], b_chalo[c]),
                 ain=(lambda c: (zero2, b_small[2])) if first else (lambda c: (ahalo[:, c, :], b_ahalo[c])),
                 aout=lambda c: (ahalo[:, c, :], b_ahalo[c]))
        if first:
            d["hv_cols"] = 4
        return [d]

    if STAGE > 5:
        for it in range(4):
            nf = (lambda ph, it=it: tile_front(main_segs[it + 1], ph)) if it + 1 < 4 else None
            run_tile(main_segs[it], mk_conv(it == 0), False, it > 0, nf, ("save0", "save1", "scratch", "scratch")[it])

    sc.dma(st["q"], Ss, st_ret.rearrange("h (j p) e -> p (h j) e", p=128), reads=[], writes=b_misc_all)
    for c in range(8):
        act(Ssb[:, c, :], Ss[:, c, :], AF.Copy, b_misc_all, [b_big[16 + c]])
    mini_segs = [
        dict(rows=32, col0=0, kind=1, xsrc=xs[:, :], rope=XROWS, mem="s", state=sampS, ydst=ys_d[:, :]),
        dict(rows=4, col0=32, kind=2, xsrc=xp[PRE + SEGT:PRE + SEGT + 4, :], rope=PRE + SEGT, mem="p", state=mainS, ydst=y_d[SEGT - 4:SEGT, :]),
    ]
    mini_conv = [
        dict(col0=0, n=32, cin=lambda c: (shalo_c[:, c, :], b_shalo_c[c]), cout=lambda c: (shalo_c[:, c, :], b_shalo_c[c]),
             ain=lambda c: (shalo_a[:, c, :], b_shalo_a[c]), aout=lambda c: (shalo_a[:, c, :], b_shalo_a[c]),
             abuf=(shalo_a, b_shalo_a), cbufh=(shalo_c, b_shalo_c)),
        dict(col0=32, n=4, cin=lambda c: (chalo[:, c, :], b_chalo[c]), cout=lambda c: (chalo[:, c, :], b_chalo[c]),
             ain=lambda c: (ahalo[:, c, :], b_ahalo[c]), aout=lambda c: (ahalo[:, c, :], b_ahalo[c]), abuf=(ahalo, b_ahalo), cbufh=(chalo, b_chalo)),
    ]
    run_tile(mini_segs, mini_conv, True, False, None, "scratch" if STAGE > 5 else "save")
    st["out_toks"].append(sc.dma(st["q"], s_sconv[:, :], shalo_c[:, :, :].rearrange("p c r -> p (c r)"), reads=b_shalo_c))
    st["out_toks"].append(sc.dma(st["q"], s_sffn[:, :], shalo_a[:, :, :].rearrange("p c r -> p (c r)"), reads=b_shalo_a))

    st["out_toks"].append(sc.dma(st["q"], o_sret[:, :], S[:, :, :].rearrange("p c e -> p (c e)"), reads=b_S))
    st["out_toks"].append(sc.dma(st["q"], o_sconv[:, :], chalo[:, :, :].rearrange("p c r -> p (c r)"), reads=b_chalo))
    st["out_toks"].append(sc.dma(st["q"], o_sffn[:, :], ahalo[:, :, :].rearrange("p c r -> p (c r)"), reads=b_ahalo))
    return finish()


def _tables(core):
    b, j = core // 4, core % 4
    t0 = j * SEGT
    half = 128
    inv = (10000.0 ** (-np.arange(half, dtype=np.float32) / half)).astype(np.float32)
    pos = np.concatenate([np.maximum(t0 - (PRE + 4) + np.arange(XROWS), 0), PAST + np.arange(32)]).astype(np.float32)
    ang = inv[:, None] * pos[None, :]
    rope = np.stack([np.cos(ang), np.sin(ang)]).astype(np.float32)
    aux = np.zeros((128, NAUX), np.float32)
    m = np.arange(128, dtype=np.float64)
    for h in range(NH):
        g = GAM[h]
        l = np.arange(128)
        mk0 = np.where(l[None, :] >= m[:, None], (g ** (-(m[:, None] + 1.0))) / 16.0, 0.0)
        mk1 = np.where((l[None, :] >= m[:, None]) & (m[:, None] >= 2), (g ** (-(m[:, None] - 1.0))) / 16.0, 0.0)
        aux[:, A_MASK + (0 * 4 + h) * 128:A_MASK + (0 * 4 + h) * 128 + 128] = mk0
        aux[:, A_MASK + (1 * 4 + h) * 128:A_MASK + (1 * 4 + h) * 128 + 128] = mk1
        aux[:, A_KDEC + 0 * 4 + h] = g ** (127.0 - m) / 16.0
        aux[:, A_KDEC + 1 * 4 + h] = np.where(m < 32, g ** (31.0 - np.minimum(m, 31)), 0.0) / 16.0
        aux[:, A_KDEC + 2 * 4 + h] = np.where(m < 4, g ** (3.0 - np.minimum(m, 3)), 0.0) / 16.0
        for sg in range(4):
            aux[:, A_KDT + sg * 4 + h] = g ** (511.0 - (sg * 128 + m)) / 16.0
            aux[:, A_KSC + sg * 4 + h] = g ** (-(128.0 * sg + m + 1.0)) / 16.0
            aux[:, A_ROWT + sg * 4 + h] = g ** (128.0 * sg + m + 1.0)
        aux[:, A_ROWD + 0 * 4 + h] = g ** (m + 1.0)
        aux[:, A_ROWD + 1 * 4 + h] = g ** (m - 1.0)
        for slot in range(6):
            bb, d = slot // 3, slot % 3 + 1
            aux[:, A_COEF + slot * 4 + h] = (g ** (float(SEGT) * (d - 1 - j))) if (bb == b and d > j) else 0.0
    for slot in range(6):
        bb, d = slot // 3, slot % 3 + 1
        aux[:, A_SEL + slot] = 1.0 if (bb == b and d == j) else 0.0
    aux[:, A_HV] = 1.0 if j >= 1 else 0.0
    aux[:, A_EPS] = EPS
    return rope, aux


def _fm(v, n):
    return np.ascontiguousarray(np.asarray(v, np.float32).reshape(n, 128).T)


def make_in_maps(x_prompt, x_sample, mem_prompt, state_ret, state_conv, state_ffn_conv, cache_mem_k, cache_mem_v,
           g_mix, w_in, g_ret_gn, w_conv, g_mem, w_mem_kv, w_br_ret, w_br_conv, w_br_mem, w_out, g_ffn,
           w_ffn_in, w_ffn_conv, w_ffn_down, g_final):
    f = lambda a: np.ascontiguousarray(np.asarray(a, dtype=np.float32))
    x_prompt, x_sample, mem_prompt = f(x_prompt), f(x_sample), f(mem_prompt)
    shared = dict(w_in=f(w_in[0]), w_mem_kv=f(w_mem_kv[0]), w_br_ret=f(w_br_ret[0]), w_br_conv=f(w_br_conv[0]),
                  w_br_mem=f(w_br_mem[0]), w_out=f(w_out[0]), w_ffn_in=f(w_ffn_in[0]), w_ffn_down=f(w_ffn_down[0]),
                  gfin=np.ascontiguousarray(np.broadcast_to(f(g_final)[None, :], (128, D))))
    in_maps = []
    for c in range(8):
        b, j = c // 4, c % 4
        t0 = j * SEGT
        xp = np.zeros((XROWS, D), np.float32)
        lo = max(t0 - (PRE + 4), 0)
        xp[lo - (t0 - (PRE + 4)):] = x_prompt[b, lo:t0 + SEGT]
        rope, aux = _tables(c)
        aux[:, A_GMIX:A_GMIX + 8] = _fm(g_mix[0], 8)
        aux[:, A_GFFN:A_GFFN + 8] = _fm(g_ffn[0], 8)
        aux[:, A_GMEM:A_GMEM + 8] = _fm(g_mem[0], 8)
        aux[:, A_GGN:A_GGN + 16] = _fm(g_ret_gn[0], 16)
        aux[:, A_WCONV:A_WCONV + 24] = np.asarray(w_conv[0], np.float32).reshape(3, 8, 128).transpose(2, 1, 0).reshape(128, 24)
        aux[:, A_WFC:A_WFC + 66] = np.asarray(w_ffn_conv[0], np.float32).reshape(3, NFF, 128).transpose(2, 1, 0).reshape(128, 66)
        aux[:, A_SCONV:A_SCONV + 16] = np.asarray(state_conv[0, c], np.float32).reshape(2, 8, 128).transpose(2, 1, 0).reshape(128, 16)
        aux[:, A_SFFN:A_SFFN + 44] = np.asarray(state_ffn_conv[0, c], np.float32).reshape(2, NFF, 128).transpose(2, 1, 0).reshape(128, 44)
        m = dict(shared)
        m.update(xp=xp, xs=f(x_sample[c]), memp=f(mem_prompt[b]), st_ret=f(state_ret[0, c]),
                 cmk=f(cache_mem_k[0, c]).reshape(256, D), cmv=f(cache_mem_v[0, c]).reshape(256, D),
                 rope=rope, aux=aux)
        in_maps.append(m)
    return in_maps


def kernel(**inputs):
    in_maps = make_in_maps(**inputs)
    nc = build_nc()
    res = run_bass_kernel_spmd(nc, in_maps, core_ids=list(range(8)))
    R = res.results

    def unstate(a):
        return np.asarray(a, np.float32).reshape(128, 4, 2, 512).transpose(1, 2, 0, 3).reshape(4, 256, 512)

    def unfm(a, n):
        return np.asarray(a, np.float32).reshape(128, n, 2).transpose(2, 1, 0).reshape(2, n * 128)

    y_prompt = np.stack([np.concatenate([R[b * 4 + j]["y"] for j in range(4)], 0) for b in range(2)]).astype(np.float32)
    y_sample = np.stack([R[c]["ys"] for c in range(8)]).astype(np.float32)
    nsr_p = np.stack([unstate(R[b * 4 + 3]["o_sret"]) for b in range(2)])[None]
    nsc_p = np.stack([unfm(R[b * 4 + 3]["o_sconv"], 8) for b in range(2)])[None]
    nsf_p = np.stack([unfm(R[b * 4 + 3]["o_sffn"], NFF) for b in range(2)])[None]
    nmk_p = np.stack([np.asarray(R[b * 4]["o_mk"], np.float32).reshape(256, 4, 256) for b in range(2)])[None]
    nmv_p = np.stack([np.asarray(R[b * 4]["o_mv"], np.float32).reshape(256, 4, 256) for b in range(2)])[None]
    nsr_s = np.stack([unstate(R[c]["s_sret"]) for c in range(8)])[None]
    nsc_s = np.stack([unfm(R[c]["s_sconv"], 8) for c in range(8)])[None]
    nsf_s = np.stack([unfm(R[c]["s_sffn"], NFF) for c in range(8)])[None]
    return (y_prompt, y_sample, nsr_p, nsc_p, nsf_p, nmk_p, nmv_p, nsr_s, nsc_s, nsf_s)
```

```python
import numpy as np
import concourse.bass as bass
import concourse.mybir as mybir
from concourse.bass_utils import run_bass_kernel_spmd

F32 = mybir.dt.float32
BF = mybir.dt.bfloat16
AF = mybir.ActivationFunctionType
ALU = mybir.AluOpType
AX = mybir.AxisListType

D = 1024
SEQ = 8192
SEGT = 2048
NH = 4
DFF = 2816
NFF = 22
EPS = 1e-6
PAST = 1024
IN_COLS = 13312
C_Q, C_K, C_V, C_GR, C_CB, C_CC, C_CX, C_MQ, C_G = 0, 1024, 2048, 4096, 6144, 7168, 8192, 9216, 10240
GAM = [1.0 - 2.0 ** (-5.0 - h) for h in range(NH)]
PRE = 6144
XROWS = PRE + 4 + SEGT
NROPE = XROWS + 32

A_MASK = 0
A_KDEC = A_MASK + 2 * 4 * 128
A_ROWD = A_KDEC + 12
A_COEF = A_ROWD + 8
A_SEL = A_COEF + 24
A_HV = A_SEL + 6
A_GMIX = A_HV + 1
A_GFFN = A_GMIX + 8
A_GMEM = A_GFFN + 8
A_GGN = A_GMEM + 8
A_WCONV = A_GGN + 16
A_WFC = A_WCONV + 24
A_SCONV = A_WFC + 66
A_SFFN = A_SCONV + 16
A_EPS = A_SFFN + 44
A_KDT = A_EPS + 1
A_KSC = A_KDT + 16
A_ROWT = A_KSC + 16
NAUX = A_ROWT + 16


class Tok:
    __slots__ = ("sem", "val")

    def __init__(self, sem, val):
        self.sem = sem
        self.val = val


class Buf:
    __slots__ = ("w", "r")

    def __init__(self):
        self.w = None
        self.r = []


class Sched:
    def __init__(self, nc, sems, dma_sems):
        self.nc = nc
        self.sem = sems
        self.cnt = {k: 0 for k in sems}
        self.prog = {k: [] for k in sems}
        self.seen = {k: {} for k in sems}
        self.dma_sems = dma_sems
        self.dma_i = {k: 0 for k in dma_sems}
        self.dma_val = {}
        self.dma_last = {}

    def _waits(self, eng, toks):
        out = []
        seen = self.seen[eng]
        for t in toks:
            if t is None:
                continue
            if eng == "pe" and t.sem is self.sem["pe"]:
                continue
            k = id(t.sem)
            if seen.get(k, 0) < t.val:
                seen[k] = t.val
                out.append((t.sem, t.val))
        return out

    def _deps(self, reads, writes):
        toks = []
        for b in reads:
            toks.append(b.w)
        for b in writes:
            toks.append(b.w)
            toks.extend(b.r)
        return toks

    def op(self, eng, fn, reads=(), writes=()):
        waits = self._waits(eng, self._deps(reads, writes))
        self.cnt[eng] += 1
        tok = Tok(self.sem[eng], self.cnt[eng])
        self.prog[eng].append((waits, fn, self.sem[eng], 1))
        for b in reads:
            b.r.append(tok)
        for b in writes:
            b.w = tok
            b.r = []
        return tok

    def dma(self, q, out, in_, reads=(), writes=(), **kw):
        ring = self.dma_sems[q]
        s = ring[self.dma_i[q] % len(ring)]
        self.dma_i[q] += 1
        toks = self._deps(reads, writes)
        toks.append(self.dma_last.get(id(s)))
        waits = self._waits(q, toks)
        v = self.dma_val.get(id(s), 0) + 16
        self.dma_val[id(s)] = v
        tok = Tok(s, v)
        self.dma_last[id(s)] = tok
        self.prog[q].append((waits, lambda e: e.dma_start(out=out, in_=in_, **kw), s, 16))
        for b in reads:
            b.r.append(tok)
        for b in writes:
            b.w = tok
            b.r = []
        return tok

    def wait_all(self, eng, toks):
        waits = self._waits(eng, toks)
        if waits:
            self.prog[eng].append((waits, None, None, 0))

    def replay(self, eng, e):
        for waits, fn, sem, inc in self.prog[eng]:
            for s, v in waits:
                e.wait_ge(s, v)
            if fn is not None:
                ins = fn(e)
                ins.then_inc(sem, inc)


STAGE = 99


def build_nc():
    nc = bass.Bass("TRN2", target_bir_lowering=False)

    def din(name, shape, dt=F32):
        return nc.dram_tensor(name, list(shape), dt, kind="ExternalInput").ap()

    def dout(name, shape, dt=F32):
        return nc.dram_tensor(name, list(shape), dt, kind="ExternalOutput").ap()

    xp = din("xp", [XROWS, D])
    xs = din("xs", [32, D])
    memp = din("memp", [256, D])
    st_ret = din("st_ret", [NH, 256, 512])
    cmk = din("cmk", [256, D])
    cmv = din("cmv", [256, D])
    w_in = din("w_in", [D, IN_COLS])
    w_mem_kv = din("w_mem_kv", [D, 2048])
    w_br_ret = din("w_br_ret", [2048, D])
    w_br_conv = din("w_br_conv", [D, D])
    w_br_mem = din("w_br_mem", [D, D])
    w_out = din("w_out", [D, D])
    w_ffn_in = din("w_ffn_in", [D, 2 * DFF])
    w_ffn_down = din("w_ffn_down", [DFF, D])
    rope_d = din("rope", [2, 128, NROPE])
    aux_d = din("aux", [128, NAUX])
    gfin_d = din("gfin", [128, D])

    y_d = dout("y", [SEGT, D])
    ys_d = dout("ys", [32, D])
    o_sret = dout("o_sret", [128, 8 * 512])
    o_sconv = dout("o_sconv", [128, 16])
    o_sffn = dout("o_sffn", [128, 44])
    o_mk = dout("o_mk", [256, D])
    o_mv = dout("o_mv", [256, D])
    s_sret = dout("s_sret", [128, 8 * 512])
    s_sconv = dout("s_sconv", [128, 16])
    s_sffn = dout("s_sffn", [128, 44])


    NBLK = 64
    wscr = nc.dram_tensor("wscr", [NBLK, 128, 4096], BF)

    def sb(name, shape, dt):
        return nc.alloc_sbuf_tensor("sb_" + name, shape, dt)
    ident = sb("ident", [128, 128], BF)
    identf = sb("identf", [128, 128], F32)
    aux = sb("aux", [128, NAUX], F32)
    gfin = sb("gfin_sb", [128, D], F32)
    rope = sb("rope_sb", [128, 2, 512], F32)
    S = sb("S", [128, 8, 512], F32)
    Sb = sb("Sb", [128, 8, 512], BF)
    mkT_p = sb("mkT_p", [128, 8, 256], BF)
    mv_p = sb("mv_p", [128, 2, 1024], BF)
    mkT_s = sb("mkT_s", [128, 8, 256], BF)
    mv_s = sb("mv_s", [128, 2, 1024], BF)
    xt = sb("xt", [128, 4, 1024], F32)
    xbm = sb("xbm", [128, 4096], BF)
    hT = sb("hT", [128, 8, 512], BF)
    qf = sb("qf", [128, 2, 512], BF)
    kf = sb("kf", [128, 2, 512], BF)
    kd = sb("kd", [128, 4, 256], BF)
    vt = sb("vt", [128, 4, 512], BF)
    gs = sb("gs", [128, 4, 512], BF)
    o_sb = sb("o_sb", [128, 4, 512], F32)
    o_tm = sb("o_tm", [128, 4, 512], BF)
    att = sb("att", [128, 1280], BF)
    scr = sb("scr", [128, 3, 516], F32)
    big = sb("big", [128, 24, 512], BF)
    o_memT = sb("o_memT", [128, 8, 512], BF)
    misc = sb("misc", [128, 4096], F32)
    om = sb("om", [128, 4, 256], BF)
    pT4 = sb("pT4", [128, 8, 128], BF)
    cbuf = sb("cbuf", [128, 520], F32)
    small = sb("small", [128, 64], F32)
    wring = [sb(f"w{i}", [128, 8, 512], BF) for i in range(4)]
    psum = [nc.alloc_psum_tensor(f"ps{i}", [128, 512], F32) for i in range(8)]

    xb = xbm[:, :].rearrange("p (s f) -> p s f", s=4)
    merged = xbm[:, :].rearrange("p (c t) -> p c t", c=8)
    macc = misc[:, 0:2048].rearrange("p (c t) -> p c t", c=4)
    yts = [misc[:, 2048:3072], misc[:, 3072:4096]]
    Ss = misc[:, :].rearrange("p (c e) -> p c e", c=8)
    Ssb = big[:, 16:24, :]

    sem_names = ["pe", "act", "dve", "pool", "sp"]
    sems = {k: nc.alloc_semaphore(f"sem_{k}") for k in sem_names}
    dma_sems = {"sp": [nc.alloc_semaphore(f"dsp{i}") for i in range(24)],
                "pool": [nc.alloc_semaphore(f"dpl{i}") for i in range(16)],
                "act": [nc.alloc_semaphore(f"dac{i}") for i in range(24)]}
    cc_sem = nc.alloc_semaphore("cc_sem")
    sc = Sched(nc, sems, dma_sems)

    B = {}

    def nb(name, n=None):
        B[name] = Buf() if n is None else [Buf() for _ in range(n)]
        return B[name]

    b_ident = nb("ident"); b_aux = nb("aux"); b_gfin = nb("gfin"); b_rope = nb("rope")
    b_S = nb("S", 8); b_Sb = nb("Sb", 8)
    b_mkT_p = nb("mkT_p"); b_mv_p = nb("mv_p"); b_mkT_s = nb("mkT_s"); b_mv_s = nb("mv_s")
    b_xt = nb("xt", 4); b_xbm = nb("xbm", 8); b_hT = nb("hT", 8)
    b_qf = nb("qf"); b_kf = nb("kf"); b_kd = nb("kd", 4); b_vt = nb("vt", 4); b_gs = nb("gs", 4)
    b_osb = nb("osb", 4); b_otm = nb("otm", 4); b_att = nb("att"); b_scr = nb("scr", 3)
    b_big = nb("big", 24); b_omT = nb("omT", 8); b_macc = nb("macc"); b_yt = nb("yt", 2)
    b_om = nb("om", 4); b_pT = nb("pT"); b_cbuf = nb("cbuf"); b_small = nb("small", 8)
    b_w = nb("w", 4); b_ps = nb("ps", 8)
    b_ccsrc = nb("ccsrc"); b_ccdst = nb("ccdst")
    b_misc_all = [b_macc, b_yt[0], b_yt[1]]

    st = {"ps": 0, "w": 0, "out_toks": [], "q": "sp", "wmode": "cast", "wn": 0}

    def PS():
        i = st["ps"] % 8
        st["ps"] += 1
        return psum[i], b_ps[i]

    def A(col, n=1):
        return aux[:, col:col + n]

    b_wscr = [Buf() for _ in range(NBLK)]

    def wload(pieces):
        i = st["w"] % 4
        st["w"] += 1
        wt = wring[i]
        mode = st["wmode"]
        kc = pieces[0][1]
        ctot = sum(p[2] for p in pieces)
        n = st["wn"]
        st["wn"] += 1
        if mode == "save0":
            mode = "save" if n % 2 == 0 else "cast"
        elif mode == "save1":
            mode = "scratch" if n % 2 == 0 else "save"
        if mode == "scratch":
            sc.dma("pool", wt[:, 0:kc, 0:ctot], wscr[n, :, 0:kc * ctot].rearrange("p (k c) -> p k c", k=kc),
                   reads=[b_wscr[n]], writes=[b_w[i]])
            return wt, b_w[i]
        c0 = 0
        for (src, kc_, ncols) in pieces:
            sc.dma("pool", wt[:, 0:kc, c0:c0 + ncols], src.rearrange("(k p) n -> p k n", p=128),
                   reads=[], writes=[b_w[i]])
            c0 += ncols
        if mode == "save":
            sc.dma("sp", wscr[n, :, 0:kc * ctot].rearrange("p (k c) -> p k c", k=kc), wt[:, 0:kc, 0:ctot],
                   reads=[b_w[i]], writes=[b_wscr[n]])
        return wt, b_w[i]

    def w_in_blk(col0, ncols=512):
        return wload([(w_in[:, col0:col0 + ncols], 8, ncols)])

    def act(out, in_, func, reads, writes, **kw):
        sc.op("act", lambda e: e.activation(out=out, in_=in_, func=func, **kw), reads, writes)

    def tt(out, in0, in1, op, reads, writes):
        sc.op("dve", lambda e: e.tensor_tensor(out=out, in0=in0, in1=in1, op=op), reads, writes)

    def ts(out, in0, s1, s2, op0, op1, reads, writes):
        if op1 is None:
            sc.op("dve", lambda e: e.tensor_scalar(out=out, in0=in0, scalar1=s1, scalar2=None, op0=op0), reads, writes)
        else:
            sc.op("dve", lambda e: e.tensor_scalar(out=out, in0=in0, scalar1=s1, scalar2=s2, op0=op0, op1=op1), reads, writes)

    def stt(out, in0, scalar, in1, op0, op1, reads, writes):
        sc.op("dve", lambda e: e.scalar_tensor_tensor(out=out, in0=in0, scalar=scalar, in1=in1, op0=op0, op1=op1), reads, writes)

    def mm(ps_ap, lhsT, rhs, start, stop, reads, bps):
        sc.op("pe", lambda e: e.matmul(ps_ap, lhsT=lhsT, rhs=rhs, start=start, stop=stop), reads, [bps])

    def tr(out_ap, in_ap, rows, reads, bps):
        sc.op("pe", lambda e: e.transpose(out=out_ap, in_=in_ap, identity=ident[:rows, :rows]), list(reads) + [b_ident], [bps])

    dbg_d = dout("dbg", [128, 4096]) if STAGE < 99 else None

    def dump(ap, bufs):
        st["out_toks"].append(sc.dma(st["q"], dbg_d[:, 0:ap.shape[-1]] if len(ap.shape) == 2 else dbg_d[:, :], ap, reads=bufs))

    def finish():
        sc.wait_all(st["q"], st["out_toks"])
        with nc.Block() as block:
            @block.sync
            def _(e):
                sc.replay("sp", e)

            @block.gpsimd
            def _(e):
                sc.replay("pool", e)

            @block.scalar
            def _(e):
                sc.replay("act", e)

            @block.vector
            def _(e):
                sc.replay("dve", e)

            @block.tensor
            def _(e):
                sc.replay("pe", e)
        return nc

    sc.dma(st["q"], aux[:, :], aux_d[:, :], writes=[b_aux])
    sc.dma(st["q"], gfin[:, :], gfin_d[:, :], writes=[b_gfin])
    sc.op("pool", lambda e: e.memset(identf[:, :], 0.0), [], [b_ident])
    sc.op("pool", lambda e: e.affine_select(out=identf[:, :], in_=identf[:, :], pattern=[[-1, 128]],
                                            compare_op=ALU.not_equal, fill=1.0, base=0, channel_multiplier=1), [], [b_ident])
    sc.op("dve", lambda e: e.tensor_copy(out=ident[:, :], in_=identf[:, :]), [], [b_ident])
    sc.op("dve", lambda e: e.memset(small[:, :], 0.0), [], b_small)

    def norm_to_fm(segs, src_fn, b_src_fn0, g_col, phase="all"):
        def b_src_fn(si):
            r = b_src_fn0(si)
            return r if isinstance(r, list) else [r]
        n = len(segs)
        ssq = small[:, 0:4]
        rr = small[:, 4:8]
        if phase in ("all", "A"):
            sc.op("dve", lambda e: e.memset(ssq, 0.0), [], [b_small[0]])
            for si, (rows, col0) in enumerate(segs):
                junk = xb[:rows, si, :]
                act(junk, src_fn(si), AF.Square, b_src_fn(si), [b_xbm[2 * si], b_xbm[2 * si + 1], b_small[0]],
                    accum_out=small[:rows, si:si + 1])
            ts(rr[:, :n], ssq[:, :n], 1.0 / D, EPS, ALU.mult, ALU.add, [b_small[0]], [b_small[1]])
            act(rr[:, :n], rr[:, :n], AF.Sqrt, [b_small[1]], [b_small[1]])
            sc.op("dve", lambda e: e.reciprocal(out=rr[:, :n], in_=rr[:, :n]), [b_small[1]], [b_small[1]])
            for si, (rows, col0) in enumerate(segs):
                act(xb[:rows, si, :], src_fn(si), AF.Copy, b_src_fn(si) + [b_small[1]], [b_xbm[2 * si], b_xbm[2 * si + 1]],
                    scale=rr[:rows, si:si + 1])
        if phase in ("all", "B"):
            for si, (rows, col0) in enumerate(segs):
                pt, bp = PS()
                ptb = pt[:, :].bitcast(BF)
                for k in range(8):
                    tr(ptb[:, k * 128:k * 128 + rows], xb[:rows, si, k * 128:(k + 1) * 128], rows,
                       [b_xbm[2 * si], b_xbm[2 * si + 1]], bp)
                tt(hT[:, :, col0:col0 + rows], ptb.rearrange("p (k t) -> p k t", k=8)[:, :, :rows],
                   A(g_col, 8).unsqueeze(2).broadcast_to([128, 8, rows]), ALU.mult, [bp, b_aux], b_hT)

    def mem_finish(mkT, b_mkT, mv, b_mv):
        for i in range(2):
            act(xb[:, i, :], xt[:, i, :], AF.Copy, [b_xt[i]], [b_xbm[2 * i], b_xbm[2 * i + 1]])
            sc.op("dve", lambda e, i=i: e.tensor_copy(out=mv[:, i, :], in_=xt[:, 2 + i, :]), [b_xt[2 + i]], [b_mv])
        for i in range(2):
            pt, bp = PS()
            ptb = pt[:, :].bitcast(BF)
            for c in range(8):
                tr(ptb[:, c * 128:(c + 1) * 128], xb[:, i, c * 128:(c + 1) * 128], 128, [b_xbm[2 * i], b_xbm[2 * i + 1]], bp)
            act(mkT[:, :, i * 128:(i + 1) * 128], ptb.rearrange("p (c t) -> p c t", c=8), AF.Copy, [bp], [b_mkT])

    def load_rope(c0, T, parts):
        for (dc, scol, n) in parts:
            sc.dma(st["q"], rope[:, :, dc:dc + n], rope_d[:, :, scol:scol + n].rearrange("t p n -> p t n"), writes=[b_rope])

    def rotary(ps1, b1, ps2, b2, dst, b_dst, T):
        cosT = rope[:, 0, :T]
        sinT = rope[:, 1, :T]
        t1 = scr[:, 0, :T]
        t2 = scr[:, 1, :T]
        tt(t1, ps1[:, :T], cosT, ALU.mult, [b1, b_rope], [b_scr[0]])
        tt(t2, ps2[:, :T], sinT, ALU.mult, [b2, b_rope], [b_scr[1]])
        tt(dst[:, 0, :T], t1, t2, ALU.subtract, [b_scr[0], b_scr[1]], [b_dst])
        tt(t1, ps1[:, :T], sinT, ALU.mult, [b1, b_rope], [b_scr[0]])
        tt(t2, ps2[:, :T], cosT, ALU.mult, [b2, b_rope], [b_scr[1]])
        tt(dst[:, 1, :T], t1, t2, ALU.add, [b_scr[0], b_scr[1]], [b_dst])

    def fm_proj(wt, bw, wc0, T):
        pt, bp = PS()
        for k in range(8):
            mm(pt[:, :T], wt[:, k, wc0:wc0 + 128], hT[:, k, :T], k == 0, k == 7, [bw, b_hT[k]], bp)
        return pt, bp

    def tm_proj(wt, bw, rows, col0, ncols=512):
        pt, bp = PS()
        for k in range(8):
            mm(pt[:rows, :ncols], hT[:, k, col0:col0 + rows], wt[:, k, :ncols], k == 0, k == 7, [bw, b_hT[k]], bp)
        return pt, bp

    def head_kv(h, segs, T, wqk, bwqk, kc0=256):
        p1, bp1 = fm_proj(wqk, bwqk, kc0, T)
        p2, bp2 = fm_proj(wqk, bwqk, kc0 + 128, T)
        rotary(p1, bp1, p2, bp2, kf, b_kf, T)
        wv, bwv = w_in_blk(C_V + h * 512)
        for si, (rows, col0, kind, _s) in enumerate(segs):
            pt, bp = PS()
            ptb = pt[:, :].bitcast(BF)
            for j in range(2):
                tr(ptb[:rows, j * 128:(j + 1) * 128], kf[:, j, col0:col0 + rows], 128, [b_kf], bp)
            ts(kd[:rows, si, :], ptb[:rows, 0:256], A(A_KDEC + kind * 4 + h)[:rows, :], None, ALU.mult, None,
               [bp, b_aux], [b_kd[si]])
            pv, bpv = tm_proj(wv, bwv, rows, col0)
            act(vt[:rows, si, :], pv[:rows, :], AF.Copy, [bpv], [b_vt[si]])

    SDEC = [[GAM[h] ** 128 for h in range(NH)], [GAM[h] ** 32 for h in range(NH)], [GAM[h] ** 4 for h in range(NH)]]

    def state_update(h, si, rows, kind, Sf, bSf, Sbf, bSbf, with_bf=True):
        for j in range(2):
            c = h * 2 + j
            pt, bp = PS()
            mm(pt[:, :], kd[:rows, si, j * 128:(j + 1) * 128], vt[:rows, si, :], True, True, [b_kd[si], b_vt[si]], bp)
            stt(Sf[:, c, :], Sf[:, c, :], float(SDEC[kind][h]), pt[:, :], ALU.mult, ALU.add, [bp], bSf(c))
            if with_bf:
                act(Sbf[:, c, :], Sf[:, c, :], AF.Copy, bSf(c), bSbf(c))

    mainS = (S, lambda c: [b_S[c]], Sb, lambda c: [b_Sb[c]])
    sampS = (Ss, lambda c: b_misc_all, Ssb, lambda c: [b_big[16 + c]])

    if STAGE <= 0:
        dump(identf[:, :], [b_ident])
        return finish()
    sc.op("dve", lambda e: e.memset(S[:, :, :].rearrange("p c e -> p (c e)"), 0.0), [], b_S)
    NT1 = PRE // 512
    p1sets = [
        dict(xt=xt, bxt=[b_xt[0], b_xt[1], b_xt[2], b_xt[3]], xb=xb, bxb=[[b_xbm[2 * q], b_xbm[2 * q + 1]] for q in range(4)],
             hT=hT, bhT=b_hT, rope=rope, brope=[b_rope]),
        dict(xt=misc[:, :].rearrange("p (s f) -> p s f", s=4), bxt=[b_macc, b_macc, b_yt[0], b_yt[1]],
             xb=o_memT[:, :, :].rearrange("p (s a) t -> p s (a t)", s=4), bxb=[[b_omT[2 * q], b_omT[2 * q + 1]] for q in range(4)],
             hT=big[:, 0:8, :], bhT=b_big[0:8], rope=o_sb[:, 0:2, :], brope=[b_osb[0], b_osb[1]]),
    ]
    hsets = [dict(kf=kf, bkf=[b_kf], kd=kd, bkd=b_kd, vt=vt, bvt=b_vt),
             dict(kf=qf, bkf=[b_qf], kd=o_tm[:, :, 0:256], bkd=b_otm, vt=gs, bvt=b_gs)]

    def p1_heads(it):
        dist = (NT1 - 1 - it) * 512
        return [h for h in range(NH) if GAM[h] ** dist >= 1e-12]

    def p1_front(it, bs):
        r0 = it * 512
        for q in range(4):
            sc.dma(st["q"], bs["xt"][:, q, :], xp[r0 + q * 128:r0 + (q + 1) * 128, :], writes=[bs["bxt"][q]])
        sc.dma(st["q"], bs["rope"][:, :, :], rope_d[:, :, r0:r0 + 512].rearrange("t p n -> p t n"), writes=bs["brope"])
        ssq = small[:, 0:4]
        rr = small[:, 4:8]
        sc.op("dve", lambda e: e.memset(ssq, 0.0), [], [b_small[0]])
        for q in range(4):
            act(bs["xb"][:, q, :], bs["xt"][:, q, :], AF.Square, [bs["bxt"][q]], bs["bxb"][q] + [b_small[0]], accum_out=small[:, q:q + 1])
        ts(rr, ssq, 1.0 / D, EPS, ALU.mult, ALU.add, [b_small[0]], [b_small[1]])
        act(rr, rr, AF.Sqrt, [b_small[1]], [b_small[1]])
        sc.op("dve", lambda e: e.reciprocal(out=rr, in_=rr), [b_small[1]], [b_small[1]])
        for q in range(4):
            act(bs["xb"][:, q, :], bs["xt"][:, q, :], AF.Copy, [bs["bxt"][q], b_small[1]], bs["bxb"][q], scale=rr[:, q:q + 1])
            pt, bp = PS()
            ptb = pt[:, :].bitcast(BF)
            for k in range(8):
                tr(ptb[:, k * 128:(k + 1) * 128], bs["xb"][:, q, k * 128:(k + 1) * 128], 128, bs["bxb"][q], bp)
            tt(bs["hT"][:, :, q * 128:(q + 1) * 128], ptb.rearrange("p (k t) -> p k t", k=8),
               A(A_GMIX, 8).unsqueeze(2).broadcast_to([128, 8, 128]), ALU.mult, [bp, b_aux], bs["bhT"])

    def p1_A(bs, h, hs):
        hT_, bhT_ = bs["hT"], bs["bhT"]
        cosT, sinT = bs["rope"][:, 0, :], bs["rope"][:, 1, :]
        wk, bwk = wload([(w_in[:, C_K + h * 256:C_K + (h + 1) * 256], 8, 256)])
        wv, bwv = w_in_blk(C_V + h * 512)
        pk = []
        for j in range(2):
            pt, bp = PS()
            for k in range(8):
                mm(pt[:, :], wk[:, k, j * 128:(j + 1) * 128], hT_[:, k, :], k == 0, k == 7, [bwk, bhT_[k]], bp)
            pk.append((pt, bp))
        for q in range(4):
            pv, bpv = PS()
            for k in range(8):
                mm(pv[:, :], hT_[:, k, q * 128:(q + 1) * 128], wv[:, k, :], k == 0, k == 7, [bwv, bhT_[k]], bpv)
            act(hs["vt"][:, q, :], pv[:, :], AF.Copy, [bpv], [hs["bvt"][q]])
        (p1_, b1_), (p2_, b2_) = pk
        t1 = scr[:, 0, :512]
        t2 = scr[:, 1, :512]
        kfd = hs["kf"]
        tt(t1, p1_[:, :], cosT, ALU.mult, [b1_] + bs["brope"], [b_scr[0]])
        tt(t2, p2_[:, :], sinT, ALU.mult, [b2_] + bs["brope"], [b_scr[1]])
        tt(kfd[:, 0, :], t1, t2, ALU.subtract, [b_scr[0], b_scr[1]], hs["bkf"])
        tt(t1, p1_[:, :], sinT, ALU.mult, [b1_] + bs["brope"], [b_scr[0]])
        tt(t2, p2_[:, :], cosT, ALU.mult, [b2_] + bs["brope"], [b_scr[1]])
        tt(kfd[:, 1, :], t1, t2, ALU.add, [b_scr[0], b_scr[1]], hs["bkf"])

    def p1_B(h, hs):
        kfd = hs["kf"]
        for q in range(4):
            pt, bp = PS()
            ptb = pt[:, :].bitcast(BF)
            for j in range(2):
                tr(ptb[:, j * 128:(j + 1) * 128], kfd[:, j, q * 128:(q + 1) * 128], 128, hs["bkf"], bp)
            ts(hs["kd"][:, q, :], ptb[:, 0:256], A(A_KDT + q * 4 + h), None, ALU.mult, None, [bp, b_aux], [hs["bkd"][q]])
        for j in range(2):
            c = h * 2 + j
            pt, bp = PS()
            for q in range(4):
                mm(pt[:, :], hs["kd"][:, q, j * 128:(j + 1) * 128], hs["vt"][:, q, :], q == 0, q == 3, [hs["bkd"][q], hs["bvt"][q]], bp)
            stt(S[:, c, :], S[:, c, :], float(GAM[h] ** 512), pt[:, :], ALU.mult, ALU.add, [bp], [b_S[c]])

    p1_front(0, p1sets[0])
    flat = [(it, h) for it in range(NT1) for h in p1_heads(it)]
    prevB = None
    seen_tiles = set()
    for n, (it, h) in enumerate(flat):
        if it not in seen_tiles:
            seen_tiles.add(it)
            if it + 1 < NT1:
                p1_front(it + 1, p1sets[(it + 1) % 2])
        hs = hsets[n % 2]
        p1_A(p1sets[it % 2], h, hs)
        if prevB is not None:
            prevB()
        prevB = (lambda h=h, hs=hs: p1_B(h, hs))
    prevB()
    for c in range(8):
        act(Sb[:, c, :], S[:, c, :], AF.Copy, [b_S[c]], [b_Sb[c]])
    if STAGE <= 1:
        dump(S[:, :, :].rearrange("p c e -> p (c e)"), b_S)
        return finish()
    for i in range(2):
        sc.dma(st["q"], xt[:, i, :], memp[i * 128:(i + 1) * 128, :], writes=[b_xt[i]])
    norm_to_fm([(128, 0), (128, 128)], lambda si: xt[:, si, :], lambda si: b_xt[si], A_GMEM)
    for nbk in range(4):
        wt, bw = wload([(w_mem_kv[:, nbk * 512:(nbk + 1) * 512], 8, 512)])
        for i in range(2):
            pt, bp = tm_proj(wt, bw, 128, i * 128)
            kvi = (nbk // 2) * 2 + i
            act(xt[:, kvi, (nbk % 2) * 512:(nbk % 2 + 1) * 512], pt[:, :], AF.Copy, [bp], [b_xt[kvi]])
    for i in range(2):
        st["out_toks"].append(sc.dma(st["q"], o_mk[i * 128:(i + 1) * 128, :], xt[:, i, :], reads=[b_xt[i]]))
        st["out_toks"].append(sc.dma(st["q"], o_mv[i * 128:(i + 1) * 128, :], xt[:, 2 + i, :], reads=[b_xt[2 + i]]))
    mem_finish(mkT_p, b_mkT_p, mv_p, b_mv_p)
    for i in range(2):
        sc.dma(st["q"], xt[:, i, :], cmk[i * 128:(i + 1) * 128, :], writes=[b_xt[i]])
        sc.dma(st["q"], xt[:, 2 + i, :], cmv[i * 128:(i + 1) * 128, :], writes=[b_xt[2 + i]])
    mem_finish(mkT_s, b_mkT_s, mv_s, b_mv_s)

    if STAGE <= 3:
        return finish()
    if STAGE <= 4:
        dump(S[:, :, :].rearrange("p c e -> p (c e)"), b_S)
        return finish()
    zero2 = small[:, 8:10]
    sc.op("dve", lambda e: e.memset(small[:, 8:16], 0.0), [], [b_small[2]])
    chalo = sb("chalo", [128, 8, 2], F32); b_chalo = nb("chalo", 8)
    ahalo = sb("ahalo", [128, NFF, 2], F32); b_ahalo = nb("ahalo", NFF)
    shalo_c = sb("shalo_c", [128, 8, 2], F32); b_shalo_c = nb("shalo_c", 8)
    shalo_a = sb("shalo_a", [128, NFF, 2], F32); b_shalo_a = nb("shalo_a", NFF)

    def tile_front(segs, phase):
        tmp = [(o_sb[:, 0:2, :].rearrange("p a b -> p (a b)"), [b_osb[0], b_osb[1]]),
               (o_sb[:, 2:4, :].rearrange("p a b -> p (a b)"), [b_osb[2], b_osb[3]]),
               (gs[:, :, :].rearrange("p a b -> p (a b)").bitcast(F32), list(b_gs)),
               (vt[:, :, :].rearrange("p a b -> p (a b)").bitcast(F32), list(b_vt))]
        if phase == "A":
            for si, s_ in enumerate(segs):
                sc.dma(st["q"], tmp[si][0], s_["xsrc"], writes=tmp[si][1])
            load_rope(0, 512, [(0, segs[0]["rope"], 512)])
        norm_to_fm([(128, q * 128) for q in range(4)], lambda si: tmp[si][0], lambda si: tmp[si][1], A_GMIX, phase)

    def run_tile(segs, convsegs, is_mini, prefetched=False, next_front=None, wmode="scratch"):
        T = sum(s["rows"] for s in segs)
        nseg = len(segs)
        st["wmode"] = wmode
        st["wn"] = 0
        for si, s in enumerate(segs):
            sc.dma(st["q"], xt[:s["rows"], si, :], s["xsrc"], writes=[b_xt[si]])
        if not prefetched:
            if is_mini:
                load_rope(0, T, [(s["col0"], s["rope"], s["rows"]) for s in segs])
            else:
                load_rope(0, T, [(0, segs[0]["rope"], T)])
            norm_to_fm([(s["rows"], s["col0"]) for s in segs], lambda si: xt[:segs[si]["rows"], si, :], lambda si: b_xt[si], A_GMIX)

        pending = [None]
        for h in range(NH):
            wqk, bwqk = wload([(w_in[:, C_Q + h * 256:C_Q + (h + 1) * 256], 8, 256),
                               (w_in[:, C_K + h * 256:C_K + (h + 1) * 256], 8, 256)])
            wv, bwv = w_in_blk(C_V + h * 512)
            wg, bwg = w_in_blk(C_GR + h * 512)
            q1, bq1 = fm_proj(wqk, bwqk, 0, T)
            q2, bq2 = fm_proj(wqk, bwqk, 128, T)
            k1, bk1 = fm_proj(wqk, bwqk, 256, T)
            k2, bk2 = fm_proj(wqk, bwqk, 384, T)
            rotary(q1, bq1, q2, bq2, qf, b_qf, T)
            rotary(k1, bk1, k2, bk2, kf, b_kf, T)
            for si, s in enumerate(segs):
                rows, col0 = s["rows"], s["col0"]
                pv, bpv = tm_proj(wv, bwv, rows, col0)
                act(vt[:rows, si, :], pv[:rows, :], AF.Copy, [bpv], [b_vt[si]])
            if is_mini and pending[0] is not None:
                pending[0]()
                pending[0] = None

            def gr_proj():
                for si, s in enumerate(segs):
                    rows, col0 = s["rows"], s["col0"]
                    pg, bpg = tm_proj(wg, bwg, rows, col0)
                    act(gs[:rows, si, :], pg[:rows, :], AF.Silu, [bpg], [b_gs[si]])
            gr_early = True
            if gr_early:
                gr_proj()
            for si, s in enumerate(segs):
                rows, col0, kind = s["rows"], s["col0"], s["kind"]
                pt, bp = PS()
                ptb = pt[:, :].bitcast(BF)
                for j in range(2):
                    tr(ptb[:rows, j * 128:(j + 1) * 128], kf[:, j, col0:col0 + rows], 128, [b_kf], bp)
                kcol = (A_KDEC + kind * 4 + h) if is_mini else (A_KDT + si * 4 + h)
                ts(kd[:rows, si, :], ptb[:rows, 0:256], A(kcol)[:rows, :], None, ALU.mult, None,
                   [bp, b_aux], [b_kd[si]])
            stats = small[:, 16:40].rearrange("p (s k) -> p s k", s=4)
            mvs = small[:, 40:48].rearrange("p (s k) -> p s k", s=4)
            if is_mini:
                for si, s in enumerate(segs):
                    rows, col0, kind = s["rows"], s["col0"], s["kind"]
                    Sf, bSf, Sbf, bSbf = s["state"]
                    mk = 0
                    sT = att[:, 0:128]
                    pss, bpss = PS()
                    for j in range(2):
                        mm(pss[:rows, :rows], kf[:, j, col0:col0 + rows], qf[:, j, col0:col0 + rows], j == 0, j == 1, [b_kf, b_qf], bpss)
                    tt(sT[:rows, :rows], pss[:rows, :rows], aux[:rows, A_MASK + (mk * 4 + h) * 128:A_MASK + (mk * 4 + h) * 128 + rows],
                       ALU.mult, [bpss, b_aux], [b_att])
                    po, bpo = PS()
                    mm(po[:rows, :], sT[:rows, :rows], vt[:rows, si, :], True, False, [b_att, b_vt[si]], bpo)
                    for j in range(2):
                        mm(po[:rows, :], qf[:, j, col0:col0 + rows], Sbf[:, h * 2 + j, :], False, j == 1, [b_qf] + bSbf(h * 2 + j), bpo)
                    act(o_sb[:rows, si, :], po[:rows, :], AF.Copy, [bpo, b_aux], [b_osb[si]], scale=A(A_ROWD + mk * 4 + h)[:rows, :])
                    state_update(h, si, rows, kind, Sf, bSf, Sbf, bSbf)
                    sc.op("dve", lambda e, rows=rows, si=si: e.bn_stats(out=stats[:rows, si, :], in_=o_sb[:rows, si, :]), [b_osb[si]], [b_small[3]])
                    sc.op("dve", lambda e, rows=rows, si=si: e.bn_aggr(out=mvs[:rows, si, :], in_=stats[:rows, si, :]), [b_small[3]], [b_small[4]])
            else:
                offs = [0, 512, 896, 1152]
                for kp in range(4):
                    N = (4 - kp) * 128
                    pss, bpss = PS()
                    for j in range(2):
                        mm(pss[:, :N], kf[:, j, kp * 128:(kp + 1) * 128], qf[:, j, kp * 128:512], j == 0, j == 1, [b_kf, b_qf], bpss)
                    stt(att[:, offs[kp]:offs[kp] + 128], pss[:, 0:128], float(GAM[h] ** (-128.0 * kp)),
                        aux[:, A_MASK + h * 128:A_MASK + (h + 1) * 128], ALU.mult, ALU.mult, [bpss, b_aux], [b_att])
                    if N > 128:
                        ts(att[:, offs[kp] + 128:offs[kp] + N], pss[:, 128:N], A(A_KSC + kp * 4 + h), None, ALU.mult, None,
                           [bpss, b_aux], [b_att])
                if pending[0] is not None:
                    pending[0]()
                    pending[0] = None
                if not gr_early:
                    gr_proj()
                spend = []
                for j in range(2):
                    pt, bp = PS()
                    for si in range(4):
                        mm(pt[:, :], kd[:, si, j * 128:(j + 1) * 128], vt[:, si, :], si == 0, si == 3, [b_kd[si], b_vt[si]], bp)
                    spend.append((pt, bp))
                for si in range(4):
                    po, bpo = PS()
                    for kp in range(si + 1):
                        o0 = offs[kp] + (si - kp) * 128
                        mm(po[:, :], att[:, o0:o0 + 128], vt[:, kp, :], kp == 0, False, [b_att, b_vt[kp]], bpo)
                    for j in range(2):
                        mm(po[:, :], qf[:, j, si * 128:(si + 1) * 128], Sb[:, h * 2 + j, :], False, j == 1, [b_qf, b_Sb[h * 2 + j]], bpo)
                    act(o_sb[:, si, :], po[:, :], AF.Copy, [bpo, b_aux], [b_osb[si]], scale=A(A_ROWT + si * 4 + h))
                    sc.op("dve", lambda e, si=si: e.bn_stats(out=stats[:, si, :], in_=o_sb[:, si, :]), [b_osb[si]], [b_small[3]])
                    sc.op("dve", lambda e, si=si: e.bn_aggr(out=mvs[:, si, :], in_=stats[:, si, :]), [b_small[3]], [b_small[4]])
                for j in range(2):
                    c = h * 2 + j
                    pt, bp = spend[j]
                    stt(S[:, c, :], S[:, c, :], float(GAM[h] ** 512), pt[:, :], ALU.mult, ALU.add, [bp], [b_S[c]])
                    act(Sb[:, c, :], S[:, c, :], AF.Copy, [b_S[c]], [b_Sb[c]])
            rstd = small[:, 48:52]
            ts(rstd[:, :nseg], mvs[:, :nseg, 1], EPS, None, ALU.add, None, [b_small[4]], [b_small[5]])
            act(rstd[:, :nseg], rstd[:, :nseg], AF.Sqrt, [b_small[5]], [b_small[5]])
            sc.op("dve", lambda e: e.reciprocal(out=rstd[:, :nseg], in_=rstd[:, :nseg]), [b_small[5]], [b_small[5]])
            for si, s in enumerate(segs):
                rows, col0 = s["rows"], s["col0"]
                ts(o_sb[:rows, si, :], o_sb[:rows, si, :], mvs[:rows, si, 0:1], rstd[:rows, si:si + 1], ALU.subtract, ALU.mult,
                   [b_small[4], b_small[5]], [b_osb[si]])
                tt(o_tm[:rows, si, :], o_sb[:rows, si, :], gs[:rows, si, :], ALU.mult, [b_osb[si], b_gs[si]], [b_otm[si]])

            def fin(h=h):
                for si, s in enumerate(segs):
                    rows, col0 = s["rows"], s["col0"]
                    pt, bp = PS()
                    ptb = pt[:, :].bitcast(BF)
                    for c in range(4):
                        tr(ptb[:, c * 128:c * 128 + rows], o_tm[:rows, si, c * 128:(c + 1) * 128], rows, [b_otm[si]], bp)
                    for c in range(4):
                        act(big[:, h * 4 + c, col0:col0 + rows], ptb[:, c * 128:c * 128 + rows], AF.Copy, [bp, b_aux], [b_big[h * 4 + c]],
                            scale=A(A_GGN + h * 4 + c))
            pending[0] = fin
        for s in segs:
            if s["kind"] == 1:
                st["out_toks"].append(sc.dma(st["q"], s_sret[:, :], misc[:, :], reads=b_misc_all))

        def conv_gen():
            for cg in range(2):
                wcc, bwcc = w_in_blk(C_CC + cg * 512)
                wcx, bwcx = w_in_blk(C_CX + cg * 512)
                wcb, bwcb = w_in_blk(C_CB + cg * 512)
                if is_mini:
                    c0 = cg * 4
                    banks = []
                    for (ww, bww) in ((wcc, bwcc), (wcx, bwcx), (wcb, bwcb)):
                        pp, bpp = PS()
                        for c4 in range(4):
                            for k in range(8):
                                mm(pp[:, c4 * T:(c4 + 1) * T], ww[:, k, c4 * 128:(c4 + 1) * 128], hT[:, k, :T], k == 0, k == 7, [bww, b_hT[k]], bpp)
                        banks.append((pp[:, 0:4 * T].rearrange("p (c t) -> p c t", c=4), bpp))
                    (pcc3, bpcc), (pcx3, bpcx), (pcb3, bpcb) = banks
                    if pending[0] is not None:
                        pending[0]()
                        pending[0] = None
                    wv = aux[:, A_WCONV + c0 * 3:A_WCONV + (c0 + 4) * 3].rearrange("p (c j) -> p c j", j=3)
                    off = 0
                    o2 = 0
                    for cs in convsegs:
                        f0, n = cs["col0"], cs["n"]
                        hb, bhb = cs["cbufh"]
                        bsl = bhb[c0:c0 + 4]
                        W_ = 4 * (n + 2)
                        cb3 = cbuf[:, off:off + W_].rearrange("p (c t) -> p c t", c=4)
                        ty3 = scr[:, 1, o2:o2 + 4 * n].rearrange("p (c t) -> p c t", c=4)
                        tm3 = scr[:, 2, o2:o2 + 4 * n].rearrange("p (c t) -> p c t", c=4)
                        sc.op("dve", lambda e, cb3=cb3, hb=hb, c0=c0: e.tensor_copy(out=cb3[:, :, 0:2], in_=hb[:, c0:c0 + 4, :]), bsl, [b_cbuf])
                        act(cb3[:, :, 2:2 + n], pcc3[:, :, f0:f0 + n], AF.Copy, [bpcc], [b_cbuf])
                        tt(cb3[:, :, 2:2 + n], cb3[:, :, 2:2 + n], pcx3[:, :, f0:f0 + n], ALU.mult, [bpcx], [b_cbuf])
                        sc.op("dve", lambda e, cb3=cb3, hb=hb, n=n, c0=c0: e.tensor_copy(out=hb[:, c0:c0 + 4, :], in_=cb3[:, :, n:n + 2]), [b_cbuf], bsl)
                        tt(ty3, cb3[:, :, 2:2 + n], wv[:, :, 2:3].broadcast_to([128, 4, n]), ALU.mult, [b_cbuf, b_aux], [b_scr[1]])
                        tt(tm3, cb3[:, :, 1:1 + n], wv[:, :, 1:2].broadcast_to([128, 4, n]), ALU.mult, [b_cbuf, b_aux], [b_scr[2]])
                        tt(ty3, ty3, tm3, ALU.add, [b_scr[2]], [b_scr[1]])
                        tt(tm3, cb3[:, :, 0:n], wv[:, :, 0:1].broadcast_to([128, 4, n]), ALU.mult, [b_cbuf, b_aux], [b_scr[2]])
                        tt(ty3, ty3, tm3, ALU.add, [b_scr[2]], [b_scr[1]])
                        tt(big[:, 16 + c0:16 + c0 + 4, f0:f0 + n], ty3, pcb3[:, :, f0:f0 + n], ALU.mult, [b_scr[1], bpcb], b_big[16 + c0:16 + c0 + 4])
                        off += W_
                        o2 += 4 * n
                    yield
                    continue
                for c4 in range(4):
                    c = cg * 4 + c4
                    pcc, bpcc = fm_proj(wcc, bwcc, c4 * 128, T)
                    pcx, bpcx = fm_proj(wcx, bwcx, c4 * 128, T)
                    pcb, bpcb = fm_proj(wcb, bwcb, c4 * 128, T)
                    if pending[0] is not None:
                        pending[0]()
                        pending[0] = None
                    cb_, bcb_ = (cbuf, b_cbuf) if c % 2 == 0 else (scr[:, 0, :], b_scr[0])
                    tyb, btyb = (scr[:, 1, :], b_scr[1]) if c % 2 == 0 else (scr[:, 2, :], b_scr[2])
                    off = 0
                    for cs in convsegs:
                        f0, n = cs["col0"], cs["n"]
                        hin, bhin = cs["cin"](c)
                        sc.op("dve", lambda e, off=off, hin=hin, cb_=cb_: e.tensor_copy(out=cb_[:, off:off + 2], in_=hin), [bhin], [bcb_])
                        act(cb_[:, off + 2:off + 2 + n], pcc[:, f0:f0 + n], AF.Copy, [bpcc], [bcb_])
                        tt(cb_[:, off + 2:off + 2 + n], cb_[:, off + 2:off + 2 + n], pcx[:, f0:f0 + n], ALU.mult, [bpcx], [bcb_])
                        hout, bhout = cs["cout"](c)
                        sc.op("dve", lambda e, off=off, n=n, hout=hout, cb_=cb_: e.tensor_copy(out=hout, in_=cb_[:, off + n:off + n + 2]), [bcb_], [bhout])
                        ty = tyb[:, :n]
                        ts(ty, cb_[:, off + 2:off + 2 + n], A(A_WCONV + c * 3 + 2), None, ALU.mult, None, [bcb_, b_aux], [btyb])
                        stt(ty, cb_[:, off + 1:off + 1 + n], A(A_WCONV + c * 3 + 1), ty, ALU.mult, ALU.add, [bcb_], [btyb])
                        stt(ty, cb_[:, off:off + n], A(A_WCONV + c * 3 + 0), ty, ALU.mult, ALU.add, [bcb_], [btyb])
                        tt(big[:, 16 + c, f0:f0 + n], ty, pcb[:, f0:f0 + n], ALU.mult, [btyb, bpcb], [b_big[16 + c]])
                        off += n + 2
                    yield
        def mem_gen():
            pend4 = [None]
            for h in range(NH):
                if h % 2 == 0:
                    wmq, bwmq = w_in_blk(C_MQ + (h // 2) * 512)
                mqf = qf
                for j in range(2):
                    pq, bpq = fm_proj(wmq, bwmq, (h % 2) * 256 + j * 128, T)
                    act(mqf[:, j, :T], pq[:, :T], AF.Copy, [bpq], [b_qf], scale=1.0 / 16.0)
                memsel = [((mkT_s, b_mkT_s, mv_s, b_mv_s) if s_["mem"] == "s" else (mkT_p, b_mkT_p, mv_p, b_mv_p)) for s_ in segs]
                pexp4 = att[:, 0:1024].rearrange("p (s m) -> p s m", s=4)
                nmx = small[:, 52:56]
                ssum = small[:, 56:60]
                sc.op("dve", lambda e: e.memset(ssum, 0.0), [], [b_small[7]])
                psl = []
                for si, s in enumerate(segs):
                    rows, col0 = s["rows"], s["col0"]
                    mkT, bmkT, mv, bmv = memsel[si]
                    pss, bpss = PS()
                    for j in range(2):
                        mm(pss[:rows, :256], mqf[:, j, col0:col0 + rows], mkT[:, h * 2 + j, :], j == 0, j == 1, [b_qf, bmkT], bpss)
                    psl.append((pss, bpss))
                for si, s in enumerate(segs):
                    rows = s["rows"]
                    pss, bpss = psl[si]
                    sc.op("dve", lambda e, rows=rows, pss=pss, si=si: e.tensor_reduce(out=nmx[:rows, si:si + 1], in_=pss[:rows, :256], axis=AX.X, op=ALU.max, negate=True),
                          [bpss], [b_small[6]])
                    act(pexp4[:rows, si, :], pss[:rows, :256], AF.Exp, [bpss, b_small[6]], [b_att, b_small[7]], bias=nmx[:rows, si:si + 1],
                        accum_out=ssum[:rows, si:si + 1])
                if pend4[0] is not None:
                    pend4[0]()
                    pend4[0] = None
                yield
                pt, bp = PS()
                ptb = pt[:, :].bitcast(BF)
                for si, s in enumerate(segs):
                    rows = s["rows"]
                    for i in range(2):
                        tr(ptb[:, (si * 2 + i) * 128:(si * 2 + i) * 128 + rows], pexp4[:rows, si, i * 128:(i + 1) * 128], rows, [b_att], bp)
                for si, s in enumerate(segs):
                    rows = s["rows"]
                    act(pT4[:, si * 2:si * 2 + 2, :rows], ptb[:, si * 256:(si + 1) * 256].rearrange("p (i t) -> p i t", i=2)[:, :, :rows], AF.Copy, [bp], [b_pT])
                yield
                sc.op("dve", lambda e: e.reciprocal(out=ssum, in_=ssum), [b_small[7]], [b_small[7]])
                for si, s in enumerate(segs):
                    rows = s["rows"]
                    mkT, bmkT, mv, bmv = memsel[si]
                    pom, bpom = PS()
                    for i in range(2):
                        mm(pom[:rows, :256], pT4[:, si * 2 + i, :rows], mv[:, i, h * 256:(h + 1) * 256], i == 0, i == 1, [b_pT, bmv], bpom)
                    ts(om[:rows, si, :], pom[:rows, :256], ssum[:rows, si:si + 1], None, ALU.mult, None, [bpom, b_small[7]], [b_om[si]])

                def s4(h=h):
                    pt, bp = PS()
                    ptb = pt[:, :].bitcast(BF)
                    for si, s in enumerate(segs):
                        rows = s["rows"]
                        for i in range(2):
                            tr(ptb[:, (si * 2 + i) * 128:(si * 2 + i) * 128 + rows], om[:rows, si, i * 128:(i + 1) * 128], rows, [b_om[si]], bp)
                    for si, s in enumerate(segs):
                        rows, col0 = s["rows"], s["col0"]
                        for i in range(2):
                            act(o_memT[:, h * 2 + i, col0:col0 + rows], ptb[:, (si * 2 + i) * 128:(si * 2 + i) * 128 + rows], AF.Copy, [bp], [b_omT[h * 2 + i]])
                pend4[0] = s4
                yield
            pend4[0]()
            pend4[0] = None
            yield
        gm, gc = mem_gen(), conv_gen()
        alive_m, alive_c = True, True
        it_ = 0
        while alive_m or alive_c:
            if alive_m:
                try:
                    next(gm)
                except StopIteration:
                    alive_m = False
            conv_now = (it_ == 0 or it_ == 8) if is_mini else (it_ % 3 in (0, 1))
            if alive_c and (conv_now or not alive_m):
                try:
                    next(gc)
                except StopIteration:
                    alive_c = False
            it_ += 1

        for cg in range(2):
            for br in range(3):
                if br == 0:
                    wsrc, nk, inbuf, binb = w_br_ret, 16, (lambda k: big[:, k, :T]), (lambda k: b_big[k])
                elif br == 1:
                    wsrc, nk, inbuf, binb = w_br_conv, 8, (lambda k: big[:, 16 + k, :T]), (lambda k: b_big[16 + k])
                else:
                    wsrc, nk, inbuf, binb = w_br_mem, 8, (lambda k: o_memT[:, k, :T]), (lambda k: b_omT[k])
                wgt, bwgt = w_in_blk(C_G + br * 1024 + cg * 512)
                pbs = [PS() for _ in range(4)]
                for kb in range(nk // 8):
                    wb, bwb = wload([(wsrc[kb * 1024:(kb + 1) * 1024, cg * 512:(cg + 1) * 512], 8, 512)])
                    for c4 in range(4):
                        for k in range(8):
                            kk = kb * 8 + k
                            mm(pbs[c4][0][:, :T], wb[:, k, c4 * 128:(c4 + 1) * 128], inbuf(kk), kk == 0, kk == nk - 1, [bwb, binb(kk)], pbs[c4][1])
                for c4 in range(4):
                    pg, bpg = fm_proj(wgt, bwgt, c4 * 128, T)
                    sg = scr[:, 2, :T]
                    act(sg, pg[:, :T], AF.Sigmoid, [bpg], [b_scr[2]])
                    if br == 0:
                        tt(macc[:, c4, :T], sg, pbs[c4][0][:, :T], ALU.mult, [b_scr[2], pbs[c4][1]], [b_macc])
                    else:
                        tt(sg, sg, pbs[c4][0][:, :T], ALU.mult, [b_scr[2], pbs[c4][1]], [b_scr[2]])
                        if br == 1:
                            tt(macc[:, c4, :T], macc[:, c4, :T], sg, ALU.add, [b_scr[2]], [b_macc])
                        else:
                            tt(merged[:, cg * 4 + c4, :T], macc[:, c4, :T], sg, ALU.add, [b_scr[2], b_macc], [b_xbm[cg * 4 + c4]])

        for nbk in range(2):
            wt, bw = wload([(w_out[:, nbk * 512:(nbk + 1) * 512], 8, 512)])
            for si, s in enumerate(segs):
                rows, col0 = s["rows"], s["col0"]
                pt, bp = PS()
                for k in range(8):
                    mm(pt[:rows, :], merged[:, k, col0:col0 + rows], wt[:, k, :], k == 0, k == 7, [bw, b_xbm[k]], bp)
                tt(xt[:rows, si, nbk * 512:(nbk + 1) * 512], xt[:rows, si, nbk * 512:(nbk + 1) * 512], pt[:rows, :], ALU.add, [bp], [b_xt[si]])

        norm_to_fm([(s["rows"], s["col0"]) for s in segs], lambda si: xt[:segs[si]["rows"], si, :], lambda si: b_xt[si], A_GFFN)
        ffn_prev = []
        for blk in range(6):
            ncol = 512 if blk < 5 else 256
            wa, bwa = wload([(w_ffn_in[:, blk * 512:blk * 512 + ncol], 8, ncol)])
            wu, bwu = wload([(w_ffn_in[:, DFF + blk * 512:DFF + blk * 512 + ncol], 8, ncol)])
            if is_mini:
                nc_ = ncol // 128
                c0 = blk * 4
                pa, bpa = PS()
                pu, bpu = PS()
                for (pp, bpp, ww, bww) in ((pa, bpa, wa, bwa), (pu, bpu, wu, bwu)):
                    for c4 in range(nc_):
                        for k in range(8):
                            mm(pp[:, c4 * T:(c4 + 1) * T], ww[:, k, c4 * 128:(c4 + 1) * 128], hT[:, k, :T], k == 0, k == 7, [bww, b_hT[k]], bpp)
                pa3 = pa[:, 0:nc_ * T].rearrange("p (c t) -> p c t", c=nc_)
                pu3 = pu[:, 0:nc_ * T].rearrange("p (c t) -> p c t", c=nc_)
                wv = aux[:, A_WFC + c0 * 3:A_WFC + (c0 + nc_) * 3].rearrange("p (c j) -> p c j", j=3)
                off = 0
                o2 = 0
                for cs in convsegs:
                    f0, n = cs["col0"], cs["n"]
                    ab, bab = cs["abuf"]
                    bsl = bab[c0:c0 + nc_]
                    W_ = nc_ * (n + 2)
                    cb3 = cbuf[:, off:off + W_].rearrange("p (c t) -> p c t", c=nc_)
                    ty3 = scr[:, 1, o2:o2 + nc_ * n].rearrange("p (c t) -> p c t", c=nc_)
                    tm3 = scr[:, 2, o2:o2 + nc_ * n].rearrange("p (c t) -> p c t", c=nc_)
                    sc.op("dve", lambda e, cb3=cb3, ab=ab, c0=c0, nc_=nc_: e.tensor_copy(out=cb3[:, :, 0:2], in_=ab[:, c0:c0 + nc_, :]), bsl, [b_cbuf])
                    act(cb3[:, :, 2:2 + n], pa3[:, :, f0:f0 + n], AF.Copy, [bpa], [b_cbuf])
                    sc.op("dve", lambda e, cb3=cb3, ab=ab, n=n, c0=c0, nc_=nc_: e.tensor_copy(out=ab[:, c0:c0 + nc_, :], in_=cb3[:, :, n:n + 2]), [b_cbuf], bsl)
                    tt(ty3, cb3[:, :, 2:2 + n], wv[:, :, 2:3].broadcast_to([128, nc_, n]), ALU.mult, [b_cbuf, b_aux], [b_scr[1]])
                    tt(tm3, cb3[:, :, 1:1 + n], wv[:, :, 1:2].broadcast_to([128, nc_, n]), ALU.mult, [b_cbuf, b_aux], [b_scr[2]])
                    tt(ty3, ty3, tm3, ALU.add, [b_scr[2]], [b_scr[1]])
                    tt(tm3, cb3[:, :, 0:n], wv[:, :, 0:1].broadcast_to([128, nc_, n]), ALU.mult, [b_cbuf, b_aux], [b_scr[2]])
                    tt(ty3, ty3, tm3, ALU.add, [b_scr[2]], [b_scr[1]])
                    act(ty3, ty3, AF.Silu, [b_scr[1]], [b_scr[1]])
                    tt(big[:, c0:c0 + nc_, f0:f0 + n], ty3, pu3[:, :, f0:f0 + n], ALU.mult, [b_scr[1], bpu], b_big[c0:c0 + nc_])
                    off += W_
                    o2 += nc_ * n
                continue
            for c4 in range(ncol // 128):
                c = blk * 4 + c4
                pa, bpa = fm_proj(wa, bwa, c4 * 128, T)
                pu, bpu = fm_proj(wu, bwu, c4 * 128, T)
                ffn_fin = []
                cb_, bcb_ = (cbuf, b_cbuf) if c % 2 == 0 else (scr[:, 0, :], b_scr[0])
                tyb, btyb = (scr[:, 1, :], b_scr[1]) if c % 2 == 0 else (scr[:, 2, :], b_scr[2])
                off = 0
                for cs in convsegs:
                    f0, n = cs["col0"], cs["n"]
                    hin, bhin = cs["ain"](c)
                    sc.op("dve", lambda e, off=off, hin=hin, cb_=cb_: e.tensor_copy(out=cb_[:, off:off + 2], in_=hin), [bhin], [bcb_])
                    act(cb_[:, off + 2:off + 2 + n], pa[:, f0:f0 + n], AF.Copy, [bpa], [bcb_])
                    hout, bhout = cs["aout"](c)
                    if cs.get("hv_cols"):
                        ts(cb_[:, off + 2:off + 2 + cs["hv_cols"]], cb_[:, off + 2:off + 2 + cs["hv_cols"]], A(A_HV), None, ALU.mult, None,
                           [b_aux], [bcb_])
                    sc.op("dve", lambda e, off=off, n=n, hout=hout, cb_=cb_: e.tensor_copy(out=hout, in_=cb_[:, off + n:off + n + 2]), [bcb_], [bhout])
                    ty = tyb[:, off:off + n]
                    ts(ty, cb_[:, off + 2:off + 2 + n], A(A_WFC + c * 3 + 2), None, ALU.mult, None, [bcb_, b_aux], [btyb])
                    stt(ty, cb_[:, off + 1:off + 1 + n], A(A_WFC + c * 3 + 1), ty, ALU.mult, ALU.add, [bcb_], [btyb])
                    stt(ty, cb_[:, off:off + n], A(A_WFC + c * 3 + 0), ty, ALU.mult, ALU.add, [bcb_], [btyb])
                    act(ty, ty, AF.Silu, [btyb], [btyb])
                    ffn_fin.append((lambda c=c, f0=f0, n=n, ty=ty, pu=pu, btyb=btyb, bpu=bpu:
                                    tt(big[:, c, f0:f0 + n], ty, pu[:, f0:f0 + n], ALU.mult, [btyb, bpu], [b_big[c]])))
                    off += n + 2
                for fn_ in ffn_prev:
                    fn_()
                ffn_prev[:] = ffn_fin
                ffn_fin = []
        for fn_ in ffn_prev:
            fn_()
        if next_front is not None:
            next_front("A")
        for nbk in range(2):
            pds = [PS() for _ in range(nseg)]
            for kb in range(3):
                kc = 8 if kb < 2 else 6
                wt, bw = wload([(w_ffn_down[kb * 1024:kb * 1024 + kc * 128, nbk * 512:(nbk + 1) * 512], kc, 512)])
                for si, s in enumerate(segs):
                    rows, col0 = s["rows"], s["col0"]
                    for k in range(kc):
                        kk = kb * 8 + k
                        mm(pds[si][0][:rows, :], big[:, kk, col0:col0 + rows], wt[:, k, :], kk == 0, kk == NFF - 1, [bw, b_big[kk]], pds[si][1])
            for si, s in enumerate(segs):
                rows = s["rows"]
                tt(xt[:rows, si, nbk * 512:(nbk + 1) * 512], xt[:rows, si, nbk * 512:(nbk + 1) * 512], pds[si][0][:rows, :], ALU.add,
                   [pds[si][1]], [b_xt[si]])

        if next_front is not None:
            next_front("B")
        ssq = small[:, 0:4]
        rr = small[:, 4:8]
        sc.op("dve", lambda e: e.memset(ssq, 0.0), [], [b_small[0]])
        for si, s in enumerate(segs):
            rows = s["rows"]
            act(xb[:rows, si, :], xt[:rows, si, :], AF.Square, [b_xt[si]], [b_xbm[2 * si], b_xbm[2 * si + 1], b_small[0]],
                accum_out=small[:rows, si:si + 1])
        ts(rr[:, :nseg], ssq[:, :nseg], 1.0 / D, EPS, ALU.mult, ALU.add, [b_small[0]], [b_small[1]])
        act(rr[:, :nseg], rr[:, :nseg], AF.Sqrt, [b_small[1]], [b_small[1]])
        sc.op("dve", lambda e: e.reciprocal(out=rr[:, :nseg], in_=rr[:, :nseg]), [b_small[1]], [b_small[1]])
        for si, s in enumerate(segs):
            rows = s["rows"]
            if s["ydst"] is None:
                continue
            yt = yts[si % 2]
            byt = b_yt[si % 2]
            act(yt[:rows, :], xt[:rows, si, :], AF.Copy, [b_xt[si], b_small[1]], [byt], scale=rr[:rows, si:si + 1])
            tt(yt[:rows, :], yt[:rows, :], gfin[:rows, :], ALU.mult, [b_gfin], [byt])
            p0 = s.get("p0", 0)
            st["out_toks"].append(sc.dma(st["q"], s["ydst"], yt[p0:rows, :], reads=[byt]))

    sc.op("dve", lambda e: e.tensor_copy(out=shalo_c[:, :, :].rearrange("p c r -> p (c r)"), in_=aux[:, A_SCONV:A_SCONV + 16]), [b_aux], b_shalo_c)
    sc.op("dve", lambda e: e.tensor_copy(out=shalo_a[:, :, :].rearrange("p c r -> p (c r)"), in_=aux[:, A_SFFN:A_SFFN + 44]), [b_aux], b_shalo_a)
    main_segs = []
    for it in range(4):
        r0 = PRE + it * 512
        segl = []
        for q in range(4):
            y0 = it * 512 + q * 128 - 4
            d = dict(rows=128, col0=q * 128, kind=0, xsrc=xp[r0 + q * 128:r0 + (q + 1) * 128, :], rope=r0, mem="p", state=mainS)
            if y0 < 0:
                d.update(ydst=y_d[0:124, :], p0=4)
            else:
                d.update(ydst=y_d[y0:y0 + 128, :])
            segl.append(d)
        main_segs.append(segl)

    def mk_conv(first):
        d = dict(col0=0, n=512,
                 cin=(lambda c: (zero2, b_small[2])) if first else (lambda c: (chalo[:, c, :], b_chalo[c])),
                 cout=lambda c: (chalo[:, c, :], b_chalo[c]),
                 ain=(lambda c: (zero2, b_small[2])) if first else (lambda c: (ahalo[:, c, :], b_ahalo[c])),
                 aout=lambda c: (ahalo[:, c, :], b_ahalo[c]))
        if first:
            d["hv_cols"] = 4
        return [d]

    if STAGE > 5:
        for it in range(4):
            nf = (lambda ph, it=it: tile_front(main_segs[it + 1], ph)) if it + 1 < 4 else None
            run_tile(main_segs[it], mk_conv(it == 0), False, it > 0, nf, ("save0", "save1", "scratch", "scratch")[it])

    sc.dma(st["q"], Ss, st_ret.rearrange("h (j p) e -> p (h j) e", p=128), reads=[], writes=b_misc_all)
    for c in range(8):
        act(Ssb[:, c, :], Ss[:, c, :], AF.Copy, b_misc_all, [b_big[16 + c]])
    mini_segs = [
        dict(rows=32, col0=0, kind=1, xsrc=xs[:, :], rope=XROWS, mem="s", state=sampS, ydst=ys_d[:, :]),
        dict(rows=4, col0=32, kind=2, xsrc=xp[PRE + SEGT:PRE + SEGT + 4, :], rope=PRE + SEGT, mem="p", state=mainS, ydst=y_d[SEGT - 4:SEGT, :]),
    ]
    mini_conv = [
        dict(col0=0, n=32, cin=lambda c: (shalo_c[:, c, :], b_shalo_c[c]), cout=lambda c: (shalo_c[:, c, :], b_shalo_c[c]),
             ain=lambda c: (shalo_a[:, c, :], b_shalo_a[c]), aout=lambda c: (shalo_a[:, c, :], b_shalo_a[c]),
             abuf=(shalo_a, b_shalo_a), cbufh=(shalo_c, b_shalo_c)),
        dict(col0=32, n=4, cin=lambda c: (chalo[:, c, :], b_chalo[c]), cout=lambda c: (chalo[:, c, :], b_chalo[c]),
             ain=lambda c: (ahalo[:, c, :], b_ahalo[c]), aout=lambda c: (ahalo[:, c, :], b_ahalo[c]), abuf=(ahalo, b_ahalo), cbufh=(chalo, b_chalo)),
    ]
    run_tile(mini_segs, mini_conv, True, False, None, "scratch" if STAGE > 5 else "save")
    st["out_toks"].append(sc.dma(st["q"], s_sconv[:, :], shalo_c[:, :, :].rearrange("p c r -> p (c r)"), reads=b_shalo_c))
    st["out_toks"].append(sc.dma(st["q"], s_sffn[:, :], shalo_a[:, :, :].rearrange("p c r -> p (c r)"), reads=b_shalo_a))

    st["out_toks"].append(sc.dma(st["q"], o_sret[:, :], S[:, :, :].rearrange("p c e -> p (c e)"), reads=b_S))
    st["out_toks"].append(sc.dma(st["q"], o_sconv[:, :], chalo[:, :, :].rearrange("p c r -> p (c r)"), reads=b_chalo))
    st["out_toks"].append(sc.dma(st["q"], o_sffn[:, :], ahalo[:, :, :].rearrange("p c r -> p (c r)"), reads=b_ahalo))
    return finish()


def _tables(core):
    b, j = core // 4, core % 4
    t0 = j * SEGT
    half = 128
    inv = (10000.0 ** (-np.arange(half, dtype=np.float32) / half)).astype(np.float32)
    pos = np.concatenate([np.maximum(t0 - (PRE + 4) + np.arange(XROWS), 0), PAST + np.arange(32)]).astype(np.float32)
    ang = inv[:, None] * pos[None, :]
    rope = np.stack([np.cos(ang), np.sin(ang)]).astype(np.float32)
    aux = np.zeros((128, NAUX), np.float32)
    m = np.arange(128, dtype=np.float64)
    for h in range(NH):
        g = GAM[h]
        l = np.arange(128)
        mk0 = np.where(l[None, :] >= m[:, None], (g ** (-(m[:, None] + 1.0))) / 16.0, 0.0)
        mk1 = np.where((l[None, :] >= m[:, None]) & (m[:, None] >= 2), (g ** (-(m[:, None] - 1.0))) / 16.0, 0.0)
        aux[:, A_MASK + (0 * 4 + h) * 128:A_MASK + (0 * 4 + h) * 128 + 128] = mk0
        aux[:, A_MASK + (1 * 4 + h) * 128:A_MASK + (1 * 4 + h) * 128 + 128] = mk1
        aux[:, A_KDEC + 0 * 4 + h] = g ** (127.0 - m) / 16.0
        aux[:, A_KDEC + 1 * 4 + h] = np.where(m < 32, g ** (31.0 - np.minimum(m, 31)), 0.0) / 16.0
        aux[:, A_KDEC + 2 * 4 + h] = np.where(m < 4, g ** (3.0 - np.minimum(m, 3)), 0.0) / 16.0
        for sg in range(4):
            aux[:, A_KDT + sg * 4 + h] = g ** (511.0 - (sg * 128 + m)) / 16.0
            aux[:, A_KSC + sg * 4 + h] = g ** (-(128.0 * sg + m + 1.0)) / 16.0
            aux[:, A_ROWT + sg * 4 + h] = g ** (128.0 * sg + m + 1.0)
        aux[:, A_ROWD + 0 * 4 + h] = g ** (m + 1.0)
        aux[:, A_ROWD + 1 * 4 + h] = g ** (m - 1.0)
        for slot in range(6):
            bb, d = slot // 3, slot % 3 + 1
            aux[:, A_COEF + slot * 4 + h] = (g ** (float(SEGT) * (d - 1 - j))) if (bb == b and d > j) else 0.0
    for slot in range(6):
        bb, d = slot // 3, slot % 3 + 1
        aux[:, A_SEL + slot] = 1.0 if (bb == b and d == j) else 0.0
    aux[:, A_HV] = 1.0 if j >= 1 else 0.0
    aux[:, A_EPS] = EPS
    return rope, aux


def _fm(v, n):
    return np.ascontiguousarray(np.asarray(v, np.float32).reshape(n, 128).T)


def make_in_maps(x_prompt, x_sample, mem_prompt, state_ret, state_conv, state_ffn_conv, cache_mem_k, cache_mem_v,
           g_mix, w_in, g_ret_gn, w_conv, g_mem, w_mem_kv, w_br_ret, w_br_conv, w_br_mem, w_out, g_ffn,
           w_ffn_in, w_ffn_conv, w_ffn_down, g_final):
    f = lambda a: np.ascontiguousarray(np.asarray(a, dtype=np.float32))
    x_prompt, x_sample, mem_prompt = f(x_prompt), f(x_sample), f(mem_prompt)
    shared = dict(w_in=f(w_in[0]), w_mem_kv=f(w_mem_kv[0]), w_br_ret=f(w_br_ret[0]), w_br_conv=f(w_br_conv[0]),
                  w_br_mem=f(w_br_mem[0]), w_out=f(w_out[0]), w_ffn_in=f(w_ffn_in[0]), w_ffn_down=f(w_ffn_down[0]),
                  gfin=np.ascontiguousarray(np.broadcast_to(f(g_final)[None, :], (128, D))))
    in_maps = []
    for c in range(8):
        b, j = c // 4, c % 4
        t0 = j * SEGT
        xp = np.zeros((XROWS, D), np.float32)
        lo = max(t0 - (PRE + 4), 0)
        xp[lo - (t0 - (PRE + 4)):] = x_prompt[b, lo:t0 + SEGT]
        rope, aux = _tables(c)
        aux[:, A_GMIX:A_GMIX + 8] = _fm(g_mix[0], 8)
        aux[:, A_GFFN:A_GFFN + 8] = _fm(g_ffn[0], 8)
        aux[:, A_GMEM:A_GMEM + 8] = _fm(g_mem[0], 8)
        aux[:, A_GGN:A_GGN + 16] = _fm(g_ret_gn[0], 16)
        aux[:, A_WCONV:A_WCONV + 24] = np.asarray(w_conv[0], np.float32).reshape(3, 8, 128).transpose(2, 1, 0).reshape(128, 24)
        aux[:, A_WFC:A_WFC + 66] = np.asarray(w_ffn_conv[0], np.float32).reshape(3, NFF, 128).transpose(2, 1, 0).reshape(128, 66)
        aux[:, A_SCONV:A_SCONV + 16] = np.asarray(state_conv[0, c], np.float32).reshape(2, 8, 128).transpose(2, 1, 0).reshape(128, 16)
        aux[:, A_SFFN:A_SFFN + 44] = np.asarray(state_ffn_conv[0, c], np.float32).reshape(2, NFF, 128).transpose(2, 1, 0).reshape(128, 44)
        m = dict(shared)
        m.update(xp=xp, xs=f(x_sample[c]), memp=f(mem_prompt[b]), st_ret=f(state_ret[0, c]),
                 cmk=f(cache_mem_k[0, c]).reshape(256, D), cmv=f(cache_mem_v[0, c]).reshape(256, D),
                 rope=rope, aux=aux)
        in_maps.append(m)
    return in_maps


def kernel(**inputs):
    in_maps = make_in_maps(**inputs)
    nc = build_nc()
    res = run_bass_kernel_spmd(nc, in_maps, core_ids=list(range(8)))
    R = res.results

    def unstate(a):
        return np.asarray(a, np.float32).reshape(128, 4, 2, 512).transpose(1, 2, 0, 3).reshape(4, 256, 512)

    def unfm(a, n):
        return np.asarray(a, np.float32).reshape(128, n, 2).transpose(2, 1, 0).reshape(2, n * 128)

    y_prompt = np.stack([np.concatenate([R[b * 4 + j]["y"] for j in range(4)], 0) for b in range(2)]).astype(np.float32)
    y_sample = np.stack([R[c]["ys"] for c in range(8)]).astype(np.float32)
    nsr_p = np.stack([unstate(R[b * 4 + 3]["o_sret"]) for b in range(2)])[None]
    nsc_p = np.stack([unfm(R[b * 4 + 3]["o_sconv"], 8) for b in range(2)])[None]
    nsf_p = np.stack([unfm(R[b * 4 + 3]["o_sffn"], NFF) for b in range(2)])[None]
    nmk_p = np.stack([np.asarray(R[b * 4]["o_mk"], np.float32).reshape(256, 4, 256) for b in range(2)])[None]
    nmv_p = np.stack([np.asarray(R[b * 4]["o_mv"], np.float32).reshape(256, 4, 256) for b in range(2)])[None]
    nsr_s = np.stack([unstate(R[c]["s_sret"]) for c in range(8)])[None]
    nsc_s = np.stack([unfm(R[c]["s_sconv"], 8) for c in range(8)])[None]
    nsf_s = np.stack([unfm(R[c]["s_sffn"], NFF) for c in range(8)])[None]
    return (y_prompt, y_sample, nsr_p, nsc_p, nsf_p, nmk_p, nmv_p, nsr_s, nsc_s, nsf_s)
```

```python
import numpy as np
import concourse.bass as bass
import concourse.mybir as mybir
from concourse.bass_utils import run_bass_kernel_spmd

F32 = mybir.dt.float32
BF = mybir.dt.bfloat16
AF = mybir.ActivationFunctionType
ALU = mybir.AluOpType
AX = mybir.AxisListType

D = 1024
SEQ = 8192
SEGT = 2048
NH = 4
DFF = 2816
NFF = 22
EPS = 1e-6
PAST = 1024
IN_COLS = 13312
C_Q, C_K, C_V, C_GR, C_CB, C_CC, C_CX, C_MQ, C_G = 0, 1024, 2048, 4096, 6144, 7168, 8192, 9216, 10240
GAM = [1.0 - 2.0 ** (-5.0 - h) for h in range(NH)]
PRE = 6144
XROWS = PRE + 4 + SEGT
NROPE = XROWS + 32

A_MASK = 0
A_KDEC = A_MASK + 2 * 4 * 128
A_ROWD = A_KDEC + 12
A_COEF = A_ROWD + 8
A_SEL = A_COEF + 24
A_HV = A_SEL + 6
A_GMIX = A_HV + 1
A_GFFN = A_GMIX + 8
A_GMEM = A_GFFN + 8
A_GGN = A_GMEM + 8
A_WCONV = A_GGN + 16
A_WFC = A_WCONV + 24
A_SCONV = A_WFC + 66
A_SFFN = A_SCONV + 16
A_EPS = A_SFFN + 44
A_KDT = A_EPS + 1
A_KSC = A_KDT + 16
A_ROWT = A_KSC + 16
NAUX = A_ROWT + 16


class Tok:
    __slots__ = ("sem", "val")

    def __init__(self, sem, val):
        self.sem = sem
        self.val = val


class Buf:
    __slots__ = ("w", "r")

    def __init__(self):
        self.w = None
        self.r = []


class Sched:
    def __init__(self, nc, sems, dma_sems):
        self.nc = nc
        self.sem = sems
        self.cnt = {k: 0 for k in sems}
        self.prog = {k: [] for k in sems}
        self.seen = {k: {} for k in sems}
        self.dma_sems = dma_sems
        self.dma_i = {k: 0 for k in dma_sems}
        self.dma_val = {}
        self.dma_last = {}

    def _waits(self, eng, toks):
        out = []
        seen = self.seen[eng]
        for t in toks:
            if t is None:
                continue
            if eng == "pe" and t.sem is self.sem["pe"]:
                continue
            k = id(t.sem)
            if seen.get(k, 0) < t.val:
                seen[k] = t.val
                out.append((t.sem, t.val))
        return out

    def _deps(self, reads, writes):
        toks = []
        for b in reads:
            toks.append(b.w)
        for b in writes:
            toks.append(b.w)
            toks.extend(b.r)
        return toks

    def op(self, eng, fn, reads=(), writes=()):
        waits = self._waits(eng, self._deps(reads, writes))
        self.cnt[eng] += 1
        tok = Tok(self.sem[eng], self.cnt[eng])
        self.prog[eng].append((waits, fn, self.sem[eng], 1))
        for b in reads:
            b.r.append(tok)
        for b in writes:
            b.w = tok
            b.r = []
        return tok

    def dma(self, q, out, in_, reads=(), writes=(), **kw):
        ring = self.dma_sems[q]
        s = ring[self.dma_i[q] % len(ring)]
        self.dma_i[q] += 1
        toks = self._deps(reads, writes)
        toks.append(self.dma_last.get(id(s)))
        waits = self._waits(q, toks)
        v = self.dma_val.get(id(s), 0) + 16
        self.dma_val[id(s)] = v
        tok = Tok(s, v)
        self.dma_last[id(s)] = tok
        self.prog[q].append((waits, lambda e: e.dma_start(out=out, in_=in_, **kw), s, 16))
        for b in reads:
            b.r.append(tok)
        for b in writes:
            b.w = tok
            b.r = []
        return tok

    def wait_all(self, eng, toks):
        waits = self._waits(eng, toks)
        if waits:
            self.prog[eng].append((waits, None, None, 0))

    def replay(self, eng, e):
        for waits, fn, sem, inc in self.prog[eng]:
            for s, v in waits:
                e.wait_ge(s, v)
            if fn is not None:
                ins = fn(e)
                ins.then_inc(sem, inc)


STAGE = 99


def build_nc():
    nc = bass.Bass("TRN2", target_bir_lowering=False)

    def din(name, shape, dt=F32):
        return nc.dram_tensor(name, list(shape), dt, kind="ExternalInput").ap()

    def dout(name, shape, dt=F32):
        return nc.dram_tensor(name, list(shape), dt, kind="ExternalOutput").ap()

    xp = din("xp", [XROWS, D])
    xs = din("xs", [32, D])
    memp = din("memp", [256, D])
    st_ret = din("st_ret", [NH, 256, 512])
    cmk = din("cmk", [256, D])
    cmv = din("cmv", [256, D])
    w_in = din("w_in", [D, IN_COLS])
    w_mem_kv = din("w_mem_kv", [D, 2048])
    w_br_ret = din("w_br_ret", [2048, D])
    w_br_conv = din("w_br_conv", [D, D])
    w_br_mem = din("w_br_mem", [D, D])
    w_out = din("w_out", [D, D])
    w_ffn_in = din("w_ffn_in", [D, 2 * DFF])
    w_ffn_down = din("w_ffn_down", [DFF, D])
    rope_d = din("rope", [2, 128, NROPE])
    aux_d = din("aux", [128, NAUX])
    gfin_d = din("gfin", [128, D])

    y_d = dout("y", [SEGT, D])
    ys_d = dout("ys", [32, D])
    o_sret = dout("o_sret", [128, 8 * 512])
    o_sconv = dout("o_sconv", [128, 16])
    o_sffn = dout("o_sffn", [128, 44])
    o_mk = dout("o_mk", [256, D])
    o_mv = dout("o_mv", [256, D])
    s_sret = dout("s_sret", [128, 8 * 512])
    s_sconv = dout("s_sconv", [128, 16])
    s_sffn = dout("s_sffn", [128, 44])


    NBLK = 64
    wscr = nc.dram_tensor("wscr", [NBLK, 128, 4096], BF)

    def sb(name, shape, dt):
        return nc.alloc_sbuf_tensor("sb_" + name, shape, dt)
    ident = sb("ident", [128, 128], BF)
    identf = sb("identf", [128, 128], F32)
    aux = sb("aux", [128, NAUX], F32)
    gfin = sb("gfin_sb", [128, D], F32)
    rope = sb("rope_sb", [128, 2, 512], F32)
    S = sb("S", [128, 8, 512], F32)
    Sb = sb("Sb", [128, 8, 512], BF)
    mkT_p = sb("mkT_p", [128, 8, 256], BF)
    mv_p = sb("mv_p", [128, 2, 1024], BF)
    mkT_s = sb("mkT_s", [128, 8, 256], BF)
    mv_s = sb("mv_s", [128, 2, 1024], BF)
    xt = sb("xt", [128, 4, 1024], F32)
    xbm = sb("xbm", [128, 4096], BF)
    hT = sb("hT", [128, 8, 512], BF)
    qf = sb("qf", [128, 2, 512], BF)
    kf = sb("kf", [128, 2, 512], BF)
    kd = sb("kd", [128, 4, 256], BF)
    vt = sb("vt", [128, 4, 512], BF)
    gs = sb("gs", [128, 4, 512], BF)
    o_sb = sb("o_sb", [128, 4, 512], F32)
    o_tm = sb("o_tm", [128, 4, 512], BF)
    att = sb("att", [128, 1280], BF)
    scr = sb("scr", [128, 3, 516], F32)
    big = sb("big", [128, 24, 512], BF)
    o_memT = sb("o_memT", [128, 8, 512], BF)
    misc = sb("misc", [128, 4096], F32)
    om = sb("om", [128, 4, 256], BF)
    pT4 = sb("pT4", [128, 8, 128], BF)
    cbuf = sb("cbuf", [128, 520], F32)
    small = sb("small", [128, 64], F32)
    wring = [sb(f"w{i}", [128, 8, 512], BF) for i in range(4)]
    psum = [nc.alloc_psum_tensor(f"ps{i}", [128, 512], F32) for i in range(8)]

    xb = xbm[:, :].rearrange("p (s f) -> p s f", s=4)
    merged = xbm[:, :].rearrange("p (c t) -> p c t", c=8)
    macc = misc[:, 0:2048].rearrange("p (c t) -> p c t", c=4)
    yts = [misc[:, 2048:3072], misc[:, 3072:4096]]
    Ss = misc[:, :].rearrange("p (c e) -> p c e", c=8)
    Ssb = big[:, 16:24, :]

    sem_names = ["pe", "act", "dve", "pool", "sp"]
    sems = {k: nc.alloc_semaphore(f"sem_{k}") for k in sem_names}
    dma_sems = {"sp": [nc.alloc_semaphore(f"dsp{i}") for i in range(24)],
                "pool": [nc.alloc_semaphore(f"dpl{i}") for i in range(16)],
                "act": [nc.alloc_semaphore(f"dac{i}") for i in range(24)]}
    cc_sem = nc.alloc_semaphore("cc_sem")
    sc = Sched(nc, sems, dma_sems)

    B = {}

    def nb(name, n=None):
        B[name] = Buf() if n is None else [Buf() for _ in range(n)]
        return B[name]

    b_ident = nb("ident"); b_aux = nb("aux"); b_gfin = nb("gfin"); b_rope = nb("rope")
    b_S = nb("S", 8); b_Sb = nb("Sb", 8)
    b_mkT_p = nb("mkT_p"); b_mv_p = nb("mv_p"); b_mkT_s = nb("mkT_s"); b_mv_s = nb("mv_s")
    b_xt = nb("xt", 4); b_xbm = nb("xbm", 8); b_hT = nb("hT", 8)
    b_qf = nb("qf"); b_kf = nb("kf"); b_kd = nb("kd", 4); b_vt = nb("vt", 4); b_gs = nb("gs", 4)
    b_osb = nb("osb", 4); b_otm = nb("otm", 4); b_att = nb("att"); b_scr = nb("scr", 3)
    b_big = nb("big", 24); b_omT = nb("omT", 8); b_macc = nb("macc"); b_yt = nb("yt", 2)
    b_om = nb("om", 4); b_pT = nb("pT"); b_cbuf = nb("cbuf"); b_small = nb("small", 8)
    b_w = nb("w", 4); b_ps = nb("ps", 8)
    b_ccsrc = nb("ccsrc"); b_ccdst = nb("ccdst")
    b_misc_all = [b_macc, b_yt[0], b_yt[1]]

    st = {"ps": 0, "w": 0, "out_toks": [], "q": "sp", "wmode": "cast", "wn": 0}

    def PS():
        i = st["ps"] % 8
        st["ps"] += 1
        return psum[i], b_ps[i]

    def A(col, n=1):
        return aux[:, col:col + n]

    b_wscr = [Buf() for _ in range(NBLK)]

    def wload(pieces):
        i = st["w"] % 4
        st["w"] += 1
        wt = wring[i]
        mode = st["wmode"]
        kc = pieces[0][1]
        ctot = sum(p[2] for p in pieces)
        n = st["wn"]
        st["wn"] += 1
        if mode == "save0":
            mode = "save" if n % 2 == 0 else "cast"
        elif mode == "save1":
            mode = "scratch" if n % 2 == 0 else "save"
        if mode == "scratch":
            sc.dma("pool", wt[:, 0:kc, 0:ctot], wscr[n, :, 0:kc * ctot].rearrange("p (k c) -> p k c", k=kc),
                   reads=[b_wscr[n]], writes=[b_w[i]])
            return wt, b_w[i]
        c0 = 0
        for (src, kc_, ncols) in pieces:
            sc.dma("pool", wt[:, 0:kc, c0:c0 + ncols], src.rearrange("(k p) n -> p k n", p=128),
                   reads=[], writes=[b_w[i]])
            c0 += ncols
        if mode == "save":
            sc.dma("sp", wscr[n, :, 0:kc * ctot].rearrange("p (k c) -> p k c", k=kc), wt[:, 0:kc, 0:ctot],
                   reads=[b_w[i]], writes=[b_wscr[n]])
        return wt, b_w[i]

    def w_in_blk(col0, ncols=512):
        return wload([(w_in[:, col0:col0 + ncols], 8, ncols)])

    def act(out, in_, func, reads, writes, **kw):
        sc.op("act", lambda e: e.activation(out=out, in_=in_, func=func, **kw), reads, writes)

    def tt(out, in0, in1, op, reads, writes):
        sc.op("dve", lambda e: e.tensor_tensor(out=out, in0=in0, in1=in1, op=op), reads, writes)

    def ts(out, in0, s1, s2, op0, op1, reads, writes):
        if op1 is None:
            sc.op("dve", lambda e: e.tensor_scalar(out=out, in0=in0, scalar1=s1, scalar2=None, op0=op0), reads, writes)
        else:
            sc.op("dve", lambda e: e.tensor_scalar(out=out, in0=in0, scalar1=s1, scalar2=s2, op0=op0, op1=op1), reads, writes)

    def stt(out, in0, scalar, in1, op0, op1, reads, writes):
        sc.op("dve", lambda e: e.scalar_tensor_tensor(out=out, in0=in0, scalar=scalar, in1=in1, op0=op0, op1=op1), reads, writes)

    def mm(ps_ap, lhsT, rhs, start, stop, reads, bps):
        sc.op("pe", lambda e: e.matmul(ps_ap, lhsT=lhsT, rhs=rhs, start=start, stop=stop), reads, [bps])

    def tr(out_ap, in_ap, rows, reads, bps):
        sc.op("pe", lambda e: e.transpose(out=out_ap, in_=in_ap, identity=ident[:rows, :rows]), list(reads) + [b_ident], [bps])

    dbg_d = dout("dbg", [128, 4096]) if STAGE < 99 else None

    def dump(ap, bufs):
        st["out_toks"].append(sc.dma(st["q"], dbg_d[:, 0:ap.shape[-1]] if len(ap.shape) == 2 else dbg_d[:, :], ap, reads=bufs))

    def finish():
        sc.wait_all(st["q"], st["out_toks"])
        with nc.Block() as block:
            @block.sync
            def _(e):
                sc.replay("sp", e)

            @block.gpsimd
            def _(e):
                sc.replay("pool", e)

            @block.scalar
            def _(e):
                sc.replay("act", e)

            @block.vector
            def _(e):
                sc.replay("dve", e)

            @block.tensor
            def _(e):
                sc.replay("pe", e)
        return nc

    sc.dma(st["q"], aux[:, :], aux_d[:, :], writes=[b_aux])
    sc.dma(st["q"], gfin[:, :], gfin_d[:, :], writes=[b_gfin])
    sc.op("pool", lambda e: e.memset(identf[:, :], 0.0), [], [b_ident])
    sc.op("pool", lambda e: e.affine_select(out=identf[:, :], in_=identf[:, :], pattern=[[-1, 128]],
                                            compare_op=ALU.not_equal, fill=1.0, base=0, channel_multiplier=1), [], [b_ident])
    sc.op("dve", lambda e: e.tensor_copy(out=ident[:, :], in_=identf[:, :]), [], [b_ident])
    sc.op("dve", lambda e: e.memset(small[:, :], 0.0), [], b_small)

    def norm_to_fm(segs, src_fn, b_src_fn0, g_col, phase="all"):
        def b_src_fn(si):
            r = b_src_fn0(si)
            return r if isinstance(r, list) else [r]
        n = len(segs)
        ssq = small[:, 0:4]
        rr = small[:, 4:8]
        if phase in ("all", "A"):
            sc.op("dve", lambda e: e.memset(ssq, 0.0), [], [b_small[0]])
            for si, (rows, col0) in enumerate(segs):
                junk = xb[:rows, si, :]
                act(junk, src_fn(si), AF.Square, b_src_fn(si), [b_xbm[2 * si], b_xbm[2 * si + 1], b_small[0]],
                    accum_out=small[:rows, si:si + 1])
            ts(rr[:, :n], ssq[:, :n], 1.0 / D, EPS, ALU.mult, ALU.add, [b_small[0]], [b_small[1]])
            act(rr[:, :n], rr[:, :n], AF.Sqrt, [b_small[1]], [b_small[1]])
            sc.op("dve", lambda e: e.reciprocal(out=rr[:, :n], in_=rr[:, :n]), [b_small[1]], [b_small[1]])
            for si, (rows, col0) in enumerate(segs):
                act(xb[:rows, si, :], src_fn(si), AF.Copy, b_src_fn(si) + [b_small[1]], [b_xbm[2 * si], b_xbm[2 * si + 1]],
                    scale=rr[:rows, si:si + 1])
        if phase in ("all", "B"):
            for si, (rows, col0) in enumerate(segs):
                pt, bp = PS()
                ptb = pt[:, :].bitcast(BF)
                for k in range(8):
                    tr(ptb[:, k * 128:k * 128 + rows], xb[:rows, si, k * 128:(k + 1) * 128], rows,
                       [b_xbm[2 * si], b_xbm[2 * si + 1]], bp)
                tt(hT[:, :, col0:col0 + rows], ptb.rearrange("p (k t) -> p k t", k=8)[:, :, :rows],
                   A(g_col, 8).unsqueeze(2).broadcast_to([128, 8, rows]), ALU.mult, [bp, b_aux], b_hT)

    def mem_finish(mkT, b_mkT, mv, b_mv):
        for i in range(2):
            act(xb[:, i, :], xt[:, i, :], AF.Copy, [b_xt[i]], [b_xbm[2 * i], b_xbm[2 * i + 1]])
            sc.op("dve", lambda e, i=i: e.tensor_copy(out=mv[:, i, :], in_=xt[:, 2 + i, :]), [b_xt[2 + i]], [b_mv])
        for i in range(2):
            pt, bp = PS()
            ptb = pt[:, :].bitcast(BF)
            for c in range(8):
                tr(ptb[:, c * 128:(c + 1) * 128], xb[:, i, c * 128:(c + 1) * 128], 128, [b_xbm[2 * i], b_xbm[2 * i + 1]], bp)
            act(mkT[:, :, i * 128:(i + 1) * 128], ptb.rearrange("p (c t) -> p c t", c=8), AF.Copy, [bp], [b_mkT])

    def load_rope(c0, T, parts):
        for (dc, scol, n) in parts:
            sc.dma(st["q"], rope[:, :, dc:dc + n], rope_d[:, :, scol:scol + n].rearrange("t p n -> p t n"), writes=[b_rope])

    def rotary(ps1, b1, ps2, b2, dst, b_dst, T):
        cosT = rope[:, 0, :T]
        sinT = rope[:, 1, :T]
        t1 = scr[:, 0, :T]
        t2 = scr[:, 1, :T]
        tt(t1, ps1[:, :T], cosT, ALU.mult, [b1, b_rope], [b_scr[0]])
        tt(t2, ps2[:, :T], sinT, ALU.mult, [b2, b_rope], [b_scr[1]])
        tt(dst[:, 0, :T], t1, t2, ALU.subtract, [b_scr[0], b_scr[1]], [b_dst])
        tt(t1, ps1[:, :T], sinT, ALU.mult, [b1, b_rope], [b_scr[0]])
        tt(t2, ps2[:, :T], cosT, ALU.mult, [b2, b_rope], [b_scr[1]])
        tt(dst[:, 1, :T], t1, t2, ALU.add, [b_scr[0], b_scr[1]], [b_dst])

    def fm_proj(wt, bw, wc0, T):
        pt, bp = PS()
        for k in range(8):
            mm(pt[:, :T], wt[:, k, wc0:wc0 + 128], hT[:, k, :T], k == 0, k == 7, [bw, b_hT[k]], bp)
        return pt, bp

    def tm_proj(wt, bw, rows, col0, ncols=512):
        pt, bp = PS()
        for k in range(8):
            mm(pt[:rows, :ncols], hT[:, k, col0:col0 + rows], wt[:, k, :ncols], k == 0, k == 7, [bw, b_hT[k]], bp)
        return pt, bp

    def head_kv(h, segs, T, wqk, bwqk, kc0=256):
        p1, bp1 = fm_proj(wqk, bwqk, kc0, T)
        p2, bp2 = fm_proj(wqk, bwqk, kc0 + 128, T)
        rotary(p1, bp1, p2, bp2, kf, b_kf, T)
        wv, bwv = w_in_blk(C_V + h * 512)
        for si, (rows, col0, kind, _s) in enumerate(segs):
            pt, bp = PS()
            ptb = pt[:, :].bitcast(BF)
            for j in range(2):
                tr(ptb[:rows, j * 128:(j + 1) * 128], kf[:, j, col0:col0 + rows], 128, [b_kf], bp)
            ts(kd[:rows, si, :], ptb[:rows, 0:256], A(A_KDEC + kind * 4 + h)[:rows, :], None, ALU.mult, None,
               [bp, b_aux], [b_kd[si]])
            pv, bpv = tm_proj(wv, bwv, rows, col0)
            act(vt[:rows, si, :], pv[:rows, :], AF.Copy, [bpv], [b_vt[si]])

    SDEC = [[GAM[h] ** 128 for h in range(NH)], [GAM[h] ** 32 for h in range(NH)], [GAM[h] ** 4 for h in range(NH)]]

    def state_update(h, si, rows, kind, Sf, bSf, Sbf, bSbf, with_bf=True):
        for j in range(2):
            c = h * 2 + j
            pt, bp = PS()
            mm(pt[:, :], kd[:rows, si, j * 128:(j + 1) * 128], vt[:rows, si, :], True, True, [b_kd[si], b_vt[si]], bp)
            stt(Sf[:, c, :], Sf[:, c, :], float(SDEC[kind][h]), pt[:, :], ALU.mult, ALU.add, [bp], bSf(c))
            if with_bf:
                act(Sbf[:, c, :], Sf[:, c, :], AF.Copy, bSf(c), bSbf(c))

    mainS = (S, lambda c: [b_S[c]], Sb, lambda c: [b_Sb[c]])
    sampS = (Ss, lambda c: b_misc_all, Ssb, lambda c: [b_big[16 + c]])

    if STAGE <= 0:
        dump(identf[:, :], [b_ident])
        return finish()
    sc.op("dve", lambda e: e.memset(S[:, :, :].rearrange("p c e -> p (c e)"), 0.0), [], b_S)
    NT1 = PRE // 512
    p1sets = [
        dict(xt=xt, bxt=[b_xt[0], b_xt[1], b_xt[2], b_xt[3]], xb=xb, bxb=[[b_xbm[2 * q], b_xbm[2 * q + 1]] for q in range(4)],
             hT=hT, bhT=b_hT, rope=rope, brope=[b_rope]),
        dict(xt=misc[:, :].rearrange("p (s f) -> p s f", s=4), bxt=[b_macc, b_macc, b_yt[0], b_yt[1]],
             xb=o_memT[:, :, :].rearrange("p (s a) t -> p s (a t)", s=4), bxb=[[b_omT[2 * q], b_omT[2 * q + 1]] for q in range(4)],
             hT=big[:, 0:8, :], bhT=b_big[0:8], rope=o_sb[:, 0:2, :], brope=[b_osb[0], b_osb[1]]),
    ]
    hsets = [dict(kf=kf, bkf=[b_kf], kd=kd, bkd=b_kd, vt=vt, bvt=b_vt),
             dict(kf=qf, bkf=[b_qf], kd=o_tm[:, :, 0:256], bkd=b_otm, vt=gs, bvt=b_gs)]

    def p1_heads(it):
        dist = (NT1 - 1 - it) * 512
        return [h for h in range(NH) if GAM[h] ** dist >= 1e-12]

    def p1_front(it, bs):
        r0 = it * 512
        for q in range(4):
            sc.dma(st["q"], bs["xt"][:, q, :], xp[r0 + q * 128:r0 + (q + 1) * 128, :], writes=[bs["bxt"][q]])
        sc.dma(st["q"], bs["rope"][:, :, :], rope_d[:, :, r0:r0 + 512].rearrange("t p n -> p t n"), writes=bs["brope"])
        ssq = small[:, 0:4]
        rr = small[:, 4:8]
        sc.op("dve", lambda e: e.memset(ssq, 0.0), [], [b_small[0]])
        for q in range(4):
            act(bs["xb"][:, q, :], bs["xt"][:, q, :], AF.Square, [bs["bxt"][q]], bs["bxb"][q] + [b_small[0]], accum_out=small[:, q:q + 1])
        ts(rr, ssq, 1.0 / D, EPS, ALU.mult, ALU.add, [b_small[0]], [b_small[1]])
        act(rr, rr, AF.Sqrt, [b_small[1]], [b_small[1]])
        sc.op("dve", lambda e: e.reciprocal(out=rr, in_=rr), [b_small[1]], [b_small[1]])
        for q in range(4):
            act(bs["xb"][:, q, :], bs["xt"][:, q, :], AF.Copy, [bs["bxt"][q], b_small[1]], bs["bxb"][q], scale=rr[:, q:q + 1])
            pt, bp = PS()
            ptb = pt[:, :].bitcast(BF)
            for k in range(8):
                tr(ptb[:, k * 128:(k + 1) * 128], bs["xb"][:, q, k * 128:(k + 1) * 128], 128, bs["bxb"][q], bp)
            tt(bs["hT"][:, :, q * 128:(q + 1) * 128], ptb.rearrange("p (k t) -> p k t", k=8),
               A(A_GMIX, 8).unsqueeze(2).broadcast_to([128, 8, 128]), ALU.mult, [bp, b_aux], bs["bhT"])

    def p1_A(bs, h, hs):
        hT_, bhT_ = bs["hT"], bs["bhT"]
        cosT, sinT = bs["rope"][:, 0, :], bs["rope"][:, 1, :]
        wk, bwk = wload([(w_in[:, C_K + h * 256:C_K + (h + 1) * 256], 8, 256)])
        wv, bwv = w_in_blk(C_V + h * 512)
        pk = []
        for j in range(2):
            pt, bp = PS()
            for k in range(8):
                mm(pt[:, :], wk[:, k, j * 128:(j + 1) * 128], hT_[:, k, :], k == 0, k == 7, [bwk, bhT_[k]], bp)
            pk.append((pt, bp))
        for q in range(4):
            pv, bpv = PS()
            for k in range(8):
                mm(pv[:, :], hT_[:, k, q * 128:(q + 1) * 128], wv[:, k, :], k == 0, k == 7, [bwv, bhT_[k]], bpv)
            act(hs["vt"][:, q, :], pv[:, :], AF.Copy, [bpv], [hs["bvt"][q]])
        (p1_, b1_), (p2_, b2_) = pk
        t1 = scr[:, 0, :512]
        t2 = scr[:, 1, :512]
        kfd = hs["kf"]
        tt(t1, p1_[:, :], cosT, ALU.mult, [b1_] + bs["brope"], [b_scr[0]])
        tt(t2, p2_[:, :], sinT, ALU.mult, [b2_] + bs["brope"], [b_scr[1]])
        tt(kfd[:, 0, :], t1, t2, ALU.subtract, [b_scr[0], b_scr[1]], hs["bkf"])
        tt(t1, p1_[:, :], sinT, ALU.mult, [b1_] + bs["brope"], [b_scr[0]])
        tt(t2, p2_[:, :], cosT, ALU.mult, [b2_] + bs["brope"], [b_scr[1]])
        tt(kfd[:, 1, :], t1, t2, ALU.add, [b_scr[0], b_scr[1]], hs["bkf"])

    def p1_B(h, hs):
        kfd = hs["kf"]
        for q in range(4):
            pt, bp = PS()
            ptb = pt[:, :].bitcast(BF)
            for j in range(2):
                tr(ptb[:, j * 128:(j + 1) * 128], kfd[:, j, q * 128:(q + 1) * 128], 128, hs["bkf"], bp)
            ts(hs["kd"][:, q, :], ptb[:, 0:256], A(A_KDT + q * 4 + h), None, ALU.mult, None, [bp, b_aux], [hs["bkd"][q]])
        for j in range(2):
            c = h * 2 + j
            pt, bp = PS()
            for q in range(4):
                mm(pt[:, :], hs["kd"][:, q, j * 128:(j + 1) * 128], hs["vt"][:, q, :], q == 0, q == 3, [hs["bkd"][q], hs["bvt"][q]], bp)
            stt(S[:, c, :], S[:, c, :], float(GAM[h] ** 512), pt[:, :], ALU.mult, ALU.add, [bp], [b_S[c]])

    p1_front(0, p1sets[0])
    flat = [(it, h) for it in range(NT1) for h in p1_heads(it)]
    prevB = None
    seen_tiles = set()
    for n, (it, h) in enumerate(flat):
        if it not in seen_tiles:
            seen_tiles.add(it)
            if it + 1 < NT1:
                p1_front(it + 1, p1sets[(it + 1) % 2])
        hs = hsets[n % 2]
        p1_A(p1sets[it % 2], h, hs)
        if prevB is not None:
            prevB()
        prevB = (lambda h=h, hs=hs: p1_B(h, hs))
    prevB()
    for c in range(8):
        act(Sb[:, c, :], S[:, c, :], AF.Copy, [b_S[c]], [b_Sb[c]])
    if STAGE <= 1:
        dump(S[:, :, :].rearrange("p c e -> p (c e)"), b_S)
        return finish()
    for i in range(2):
        sc.dma(st["q"], xt[:, i, :], memp[i * 128:(i + 1) * 128, :], writes=[b_xt[i]])
    norm_to_fm([(128, 0), (128, 128)], lambda si: xt[:, si, :], lambda si: b_xt[si], A_GMEM)
    for nbk in range(4):
        wt, bw = wload([(w_mem_kv[:, nbk * 512:(nbk + 1) * 512], 8, 512)])
        for i in range(2):
            pt, bp = tm_proj(wt, bw, 128, i * 128)
            kvi = (nbk // 2) * 2 + i
            act(xt[:, kvi, (nbk % 2) * 512:(nbk % 2 + 1) * 512], pt[:, :], AF.Copy, [bp], [b_xt[kvi]])
    for i in range(2):
        st["out_toks"].append(sc.dma(st["q"], o_mk[i * 128:(i + 1) * 128, :], xt[:, i, :], reads=[b_xt[i]]))
        st["out_toks"].append(sc.dma(st["q"], o_mv[i * 128:(i + 1) * 128, :], xt[:, 2 + i, :], reads=[b_xt[2 + i]]))
    mem_finish(mkT_p, b_mkT_p, mv_p, b_mv_p)
    for i in range(2):
        sc.dma(st["q"], xt[:, i, :], cmk[i * 128:(i + 1) * 128, :], writes=[b_xt[i]])
        sc.dma(st["q"], xt[:, 2 + i, :], cmv[i * 128:(i + 1) * 128, :], writes=[b_xt[2 + i]])
    mem_finish(mkT_s, b_mkT_s, mv_s, b_mv_s)

    if STAGE <= 3:
        return finish()
    if STAGE <= 4:
        dump(S[:, :, :].rearrange("p c e -> p (c e)"), b_S)
        return finish()
    zero2 = small[:, 8:10]
    sc.op("dve", lambda e: e.memset(small[:, 8:16], 0.0), [], [b_small[2]])
    chalo = sb("chalo", [128, 8, 2], F32); b_chalo = nb("chalo", 8)
    ahalo = sb("ahalo", [128, NFF, 2], F32); b_ahalo = nb("ahalo", NFF)
    shalo_c = sb("shalo_c", [128, 8, 2], F32); b_shalo_c = nb("shalo_c", 8)
    shalo_a = sb("shalo_a", [128, NFF, 2], F32); b_shalo_a = nb("shalo_a", NFF)

    def tile_front(segs, phase):
        tmp = [(o_sb[:, 0:2, :].rearrange("p a b -> p (a b)"), [b_osb[0], b_osb[1]]),
               (o_sb[:, 2:4, :].rearrange("p a b -> p (a b)"), [b_osb[2], b_osb[3]]),
               (gs[:, :, :].rearrange("p a b -> p (a b)").bitcast(F32), list(b_gs)),
               (vt[:, :, :].rearrange("p a b -> p (a b)").bitcast(F32), list(b_vt))]
        if phase == "A":
            for si, s_ in enumerate(segs):
                sc.dma(st["q"], tmp[si][0], s_["xsrc"], writes=tmp[si][1])
            load_rope(0, 512, [(0, segs[0]["rope"], 512)])
        norm_to_fm([(128, q * 128) for q in range(4)], lambda si: tmp[si][0], lambda si: tmp[si][1], A_GMIX, phase)

    def run_tile(segs, convsegs, is_mini, prefetched=False, next_front=None, wmode="scratch"):
        T = sum(s["rows"] for s in segs)
        nseg = len(segs)
        st["wmode"] = wmode
        st["wn"] = 0
        for si, s in enumerate(segs):
            sc.dma(st["q"], xt[:s["rows"], si, :], s["xsrc"], writes=[b_xt[si]])
        if not prefetched:
            if is_mini:
                load_rope(0, T, [(s["col0"], s["rope"], s["rows"]) for s in segs])
            else:
                load_rope(0, T, [(0, segs[0]["rope"], T)])
            norm_to_fm([(s["rows"], s["col0"]) for s in segs], lambda si: xt[:segs[si]["rows"], si, :], lambda si: b_xt[si], A_GMIX)

        pending = [None]
        for h in range(NH):
            wqk, bwqk = wload([(w_in[:, C_Q + h * 256:C_Q + (h + 1) * 256], 8, 256),
                               (w_in[:, C_K + h * 256:C_K + (h + 1) * 256], 8, 256)])
            wv, bwv = w_in_blk(C_V + h * 512)
            wg, bwg = w_in_blk(C_GR + h * 512)
            q1, bq1 = fm_proj(wqk, bwqk, 0, T)
            q2, bq2 = fm_proj(wqk, bwqk, 128, T)
            k1, bk1 = fm_proj(wqk, bwqk, 256, T)
            k2, bk2 = fm_proj(wqk, bwqk, 384, T)
            rotary(q1, bq1, q2, bq2, qf, b_qf, T)
            rotary(k1, bk1, k2, bk2, kf, b_kf, T)
            for si, s in enumerate(segs):
                rows, col0 = s["rows"], s["col0"]
                pv, bpv = tm_proj(wv, bwv, rows, col0)
                act(vt[:rows, si, :], pv[:rows, :], AF.Copy, [bpv], [b_vt[si]])
            if is_mini and pending[0] is not None:
                pending[0]()
                pending[0] = None

            def gr_proj():
                for si, s in enumerate(segs):
                    rows, col0 = s["rows"], s["col0"]
                    pg, bpg = tm_proj(wg, bwg, rows, col0)
                    act(gs[:rows, si, :], pg[:rows, :], AF.Silu, [bpg], [b_gs[si]])
            gr_early = True
            if gr_early:
                gr_proj()
            for si, s in enumerate(segs):
                rows, col0, kind = s["rows"], s["col0"], s["kind"]
                pt, bp = PS()
                ptb = pt[:, :].bitcast(BF)
                for j in range(2):
                    tr(ptb[:rows, j * 128:(j + 1) * 128], kf[:, j, col0:col0 + rows], 128, [b_kf], bp)
                kcol = (A_KDEC + kind * 4 + h) if is_mini else (A_KDT + si * 4 + h)
                ts(kd[:rows, si, :], ptb[:rows, 0:256], A(kcol)[:rows, :], None, ALU.mult, None,
                   [bp, b_aux], [b_kd[si]])
            stats = small[:, 16:40].rearrange("p (s k) -> p s k", s=4)
            mvs = small[:, 40:48].rearrange("p (s k) -> p s k", s=4)
            if is_mini:
                for si, s in enumerate(segs):
                    rows, col0, kind = s["rows"], s["col0"], s["kind"]
                    Sf, bSf, Sbf, bSbf = s["state"]
                    mk = 0
                    sT = att[:, 0:128]
                    pss, bpss = PS()
                    for j in range(2):
                        mm(pss[:rows, :rows], kf[:, j, col0:col0 + rows], qf[:, j, col0:col0 + rows], j == 0, j == 1, [b_kf, b_qf], bpss)
                    tt(sT[:rows, :rows], pss[:rows, :rows], aux[:rows, A_MASK + (mk * 4 + h) * 128:A_MASK + (mk * 4 + h) * 128 + rows],
                       ALU.mult, [bpss, b_aux], [b_att])
                    po, bpo = PS()
                    mm(po[:rows, :], sT[:rows, :rows], vt[:rows, si, :], True, False, [b_att, b_vt[si]], bpo)
                    for j in range(2):
                        mm(po[:rows, :], qf[:, j, col0:col0 + rows], Sbf[:, h * 2 + j, :], False, j == 1, [b_qf] + bSbf(h * 2 + j), bpo)
                    act(o_sb[:rows, si, :], po[:rows, :], AF.Copy, [bpo, b_aux], [b_osb[si]], scale=A(A_ROWD + mk * 4 + h)[:rows, :])
                    state_update(h, si, rows, kind, Sf, bSf, Sbf, bSbf)
                    sc.op("dve", lambda e, rows=rows, si=si: e.bn_stats(out=stats[:rows, si, :], in_=o_sb[:rows, si, :]), [b_osb[si]], [b_small[3]])
                    sc.op("dve", lambda e, rows=rows, si=si: e.bn_aggr(out=mvs[:rows, si, :], in_=stats[:rows, si, :]), [b_small[3]], [b_small[4]])
            else:
                offs = [0, 512, 896, 1152]
                for kp in range(4):
                    N = (4 - kp) * 128
                    pss, bpss = PS()
                    for j in range(2):
                        mm(pss[:, :N], kf[:, j, kp * 128:(kp + 1) * 128], qf[:, j, kp * 128:512], j == 0, j == 1, [b_kf, b_qf], bpss)
                    stt(att[:, offs[kp]:offs[kp] + 128], pss[:, 0:128], float(GAM[h] ** (-128.0 * kp)),
                        aux[:, A_MASK + h * 128:A_MASK + (h + 1) * 128], ALU.mult, ALU.mult, [bpss, b_aux], [b_att])
                    if N > 128:
                        ts(att[:, offs[kp] + 128:offs[kp] + N], pss[:, 128:N], A(A_KSC + kp * 4 + h), None, ALU.mult, None,
                           [bpss, b_aux], [b_att])
                if pending[0] is not None:
                    pending[0]()
                    pending[0] = None
                if not gr_early:
                    gr_proj()
                spend = []
                for j in range(2):
                    pt, bp = PS()
                    for si in range(4):
                        mm(pt[:, :], kd[:, si, j * 128:(j + 1) * 128], vt[:, si, :], si == 0, si == 3, [b_kd[si], b_vt[si]], bp)
                    spend.append((pt, bp))
                for si in range(4):
                    po, bpo = PS()
                    for kp in range(si + 1):
                        o0 = offs[kp] + (si - kp) * 128
                        mm(po[:, :], att[:, o0:o0 + 128], vt[:, kp, :], kp == 0, False, [b_att, b_vt[kp]], bpo)
                    for j in range(2):
                        mm(po[:, :], qf[:, j, si * 128:(si + 1) * 128], Sb[:, h * 2 + j, :], False, j == 1, [b_qf, b_Sb[h * 2 + j]], bpo)
                    act(o_sb[:, si, :], po[:, :], AF.Copy, [bpo, b_aux], [b_osb[si]], scale=A(A_ROWT + si * 4 + h))
                    sc.op("dve", lambda e, si=si: e.bn_stats(out=stats[:, si, :], in_=o_sb[:, si, :]), [b_osb[si]], [b_small[3]])
                    sc.op("dve", lambda e, si=si: e.bn_aggr(out=mvs[:, si, :], in_=stats[:, si, :]), [b_small[3]], [b_small[4]])
                for j in range(2):
                    c = h * 2 + j
                    pt, bp = spend[j]
                    stt(S[:, c, :], S[:, c, :], float(GAM[h] ** 512), pt[:, :], ALU.mult, ALU.add, [bp], [b_S[c]])
                    act(Sb[:, c, :], S[:, c, :], AF.Copy, [b_S[c]], [b_Sb[c]])
            rstd = small[:, 48:52]
            ts(rstd[:, :nseg], mvs[:, :nseg, 1], EPS, None, ALU.add, None, [b_small[4]], [b_small[5]])
            act(rstd[:, :nseg], rstd[:, :nseg], AF.Sqrt, [b_small[5]], [b_small[5]])
            sc.op("dve", lambda e: e.reciprocal(out=rstd[:, :nseg], in_=rstd[:, :nseg]), [b_small[5]], [b_small[5]])
            for si, s in enumerate(segs):
                rows, col0 = s["rows"], s["col0"]
                ts(o_sb[:rows, si, :], o_sb[:rows, si, :], mvs[:rows, si, 0:1], rstd[:rows, si:si + 1], ALU.subtract, ALU.mult,
                   [b_small[4], b_small[5]], [b_osb[si]])
                tt(o_tm[:rows, si, :], o_sb[:rows, si, :], gs[:rows, si, :], ALU.mult, [b_osb[si], b_gs[si]], [b_otm[si]])

            def fin(h=h):
                for si, s in enumerate(segs):
                    rows, col0 = s["rows"], s["col0"]
                    pt, bp = PS()
                    ptb = pt[:, :].bitcast(BF)
                    for c in range(4):
                        tr(ptb[:, c * 128:c * 128 + rows], o_tm[:rows, si, c * 128:(c + 1) * 128], rows, [b_otm[si]], bp)
                    for c in range(4):
                        act(big[:, h * 4 + c, col0:col0 + rows], ptb[:, c * 128:c * 128 + rows], AF.Copy, [bp, b_aux], [b_big[h * 4 + c]],
                            scale=A(A_GGN + h * 4 + c))
            pending[0] = fin
        for s in segs:
            if s["kind"] == 1:
                st["out_toks"].append(sc.dma(st["q"], s_sret[:, :], misc[:, :], reads=b_misc_all))

        def conv_gen():
            for cg in range(2):
                wcc, bwcc = w_in_blk(C_CC + cg * 512)
                wcx, bwcx = w_in_blk(C_CX + cg * 512)
                wcb, bwcb = w_in_blk(C_CB + cg * 512)
                if is_mini:
                    c0 = cg * 4
                    banks = []
                    for (ww, bww) in ((wcc, bwcc), (wcx, bwcx), (wcb, bwcb)):
                        pp, bpp = PS()
                        for c4 in range(4):
                            for k in range(8):
                                mm(pp[:, c4 * T:(c4 + 1) * T], ww[:, k, c4 * 128:(c4 + 1) * 128], hT[:, k, :T], k == 0, k == 7, [bww, b_hT[k]], bpp)
                        banks.append((pp[:, 0:4 * T].rearrange("p (c t) -> p c t", c=4), bpp))
                    (pcc3, bpcc), (pcx3, bpcx), (pcb3, bpcb) = banks
                    if pending[0] is not None:
                        pending[0]()
                        pending[0] = None
                    wv = aux[:, A_WCONV + c0 * 3:A_WCONV + (c0 + 4) * 3].rearrange("p (c j) -> p c j", j=3)
                    off = 0
                    o2 = 0
                    for cs in convsegs:
                        f0, n = cs["col0"], cs["n"]
                        hb, bhb = cs["cbufh"]
                        bsl = bhb[c0:c0 + 4]
                        W_ = 4 * (n + 2)
                        cb3 = cbuf[:, off:off + W_].rearrange("p (c t) -> p c t", c=4)
                        ty3 = scr[:, 1, o2:o2 + 4 * n].rearrange("p (c t) -> p c t", c=4)
                        tm3 = scr[:, 2, o2:o2 + 4 * n].rearrange("p (c t) -> p c t", c=4)
                        sc.op("dve", lambda e, cb3=cb3, hb=hb, c0=c0: e.tensor_copy(out=cb3[:, :, 0:2], in_=hb[:, c0:c0 + 4, :]), bsl, [b_cbuf])
                        act(cb3[:, :, 2:2 + n], pcc3[:, :, f0:f0 + n], AF.Copy, [bpcc], [b_cbuf])
                        tt(cb3[:, :, 2:2 + n], cb3[:, :, 2:2 + n], pcx3[:, :, f0:f0 + n], ALU.mult, [bpcx], [b_cbuf])
                        sc.op("dve", lambda e, cb3=cb3, hb=hb, n=n, c0=c0: e.tensor_copy(out=hb[:, c0:c0 + 4, :], in_=cb3[:, :, n:n + 2]), [b_cbuf], bsl)
                        tt(ty3, cb3[:, :, 2:2 + n], wv[:, :, 2:3].broadcast_to([128, 4, n]), ALU.mult, [b_cbuf, b_aux], [b_scr[1]])
                        tt(tm3, cb3[:, :, 1:1 + n], wv[:, :, 1:2].broadcast_to([128, 4, n]), ALU.mult, [b_cbuf, b_aux], [b_scr[2]])
                        tt(ty3, ty3, tm3, ALU.add, [b_scr[2]], [b_scr[1]])
                        tt(tm3, cb3[:, :, 0:n], wv[:, :, 0:1].broadcast_to([128, 4, n]), ALU.mult, [b_cbuf, b_aux], [b_scr[2]])
                        tt(ty3, ty3, tm3, ALU.add, [b_scr[2]], [b_scr[1]])
                        tt(big[:, 16 + c0:16 + c0 + 4, f0:f0 + n], ty3, pcb3[:, :, f0:f0 + n], ALU.mult, [b_scr[1], bpcb], b_big[16 + c0:16 + c0 + 4])
                        off += W_
                        o2 += 4 * n
                    yield
                    continue
                for c4 in range(4):
                    c = cg * 4 + c4
                    pcc, bpcc = fm_proj(wcc, bwcc, c4 * 128, T)
                    pcx, bpcx = fm_proj(wcx, bwcx, c4 * 128, T)
                    pcb, bpcb = fm_proj(wcb, bwcb, c4 * 128, T)
                    if pending[0] is not None:
                        pending[0]()
                        pending[0] = None
                    cb_, bcb_ = (cbuf, b_cbuf) if c % 2 == 0 else (scr[:, 0, :], b_scr[0])
                    tyb, btyb = (scr[:, 1, :], b_scr[1]) if c % 2 == 0 else (scr[:, 2, :], b_scr[2])
                    off = 0
                    for cs in convsegs:
                        f0, n = cs["col0"], cs["n"]
                        hin, bhin = cs["cin"](c)
                        sc.op("dve", lambda e, off=off, hin=hin, cb_=cb_: e.tensor_copy(out=cb_[:, off:off + 2], in_=hin), [bhin], [bcb_])
                        act(cb_[:, off + 2:off + 2 + n], pcc[:, f0:f0 + n], AF.Copy, [bpcc], [bcb_])
                        tt(cb_[:, off + 2:off + 2 + n], cb_[:, off + 2:off + 2 + n], pcx[:, f0:f0 + n], ALU.mult, [bpcx], [bcb_])
                        hout, bhout = cs["cout"](c)
                        sc.op("dve", lambda e, off=off, n=n, hout=hout, cb_=cb_: e.tensor_copy(out=hout, in_=cb_[:, off + n:off + n + 2]), [bcb_], [bhout])
                        ty = tyb[:, :n]
                        ts(ty, cb_[:, off + 2:off + 2 + n], A(A_WCONV + c * 3 + 2), None, ALU.mult, None, [bcb_, b_aux], [btyb])
                        stt(ty, cb_[:, off + 1:off + 1 + n], A(A_WCONV + c * 3 + 1), ty, ALU.mult, ALU.add, [bcb_], [btyb])
                        stt(ty, cb_[:, off:off + n], A(A_WCONV + c * 3 + 0), ty, ALU.mult, ALU.add, [bcb_], [btyb])
                        tt(big[:, 16 + c, f0:f0 + n], ty, pcb[:, f0:f0 + n], ALU.mult, [btyb, bpcb], [b_big[16 + c]])
                        off += n + 2
                    yield
        def mem_gen():
            pend4 = [None]
            for h in range(NH):
                if h % 2 == 0:
                    wmq, bwmq = w_in_blk(C_MQ + (h // 2) * 512)
                mqf = qf
                for j in range(2):
                    pq, bpq = fm_proj(wmq, bwmq, (h % 2) * 256 + j * 128, T)
                    act(mqf[:, j, :T], pq[:, :T], AF.Copy, [bpq], [b_qf], scale=1.0 / 16.0)
                memsel = [((mkT_s, b_mkT_s, mv_s, b_mv_s) if s_["mem"] == "s" else (mkT_p, b_mkT_p, mv_p, b_mv_p)) for s_ in segs]
                pexp4 = att[:, 0:1024].rearrange("p (s m) -> p s m", s=4)
                nmx = small[:, 52:56]
                ssum = small[:, 56:60]
                sc.op("dve", lambda e: e.memset(ssum, 0.0), [], [b_small[7]])
                psl = []
                for si, s in enumerate(segs):
                    rows, col0 = s["rows"], s["col0"]
                    mkT, bmkT, mv, bmv = memsel[si]
                    pss, bpss = PS()
                    for j in range(2):
                        mm(pss[:rows, :256], mqf[:, j, col0:col0 + rows], mkT[:, h * 2 + j, :], j == 0, j == 1, [b_qf, bmkT], bpss)
                    psl.append((pss, bpss))
                for si, s in enumerate(segs):
                    rows = s["rows"]
                    pss, bpss = psl[si]
                    sc.op("dve", lambda e, rows=rows, pss=pss, si=si: e.tensor_reduce(out=nmx[:rows, si:si + 1], in_=pss[:rows, :256], axis=AX.X, op=ALU.max, negate=True),
                          [bpss], [b_small[6]])
                    act(pexp4[:rows, si, :], pss[:rows, :256], AF.Exp, [bpss, b_small[6]], [b_att, b_small[7]], bias=nmx[:rows, si:si + 1],
                        accum_out=ssum[:rows, si:si + 1])
                if pend4[0] is not None:
                    pend4[0]()
                    pend4[0] = None
                yield
                pt, bp = PS()
                ptb = pt[:, :].bitcast(BF)
                for si, s in enumerate(segs):
                    rows = s["rows"]
                    for i in range(2):
                        tr(ptb[:, (si * 2 + i) * 128:(si * 2 + i) * 128 + rows], pexp4[:rows, si, i * 128:(i + 1) * 128], rows, [b_att], bp)
                for si, s in enumerate(segs):
                    rows = s["rows"]
                    act(pT4[:, si * 2:si * 2 + 2, :rows], ptb[:, si * 256:(si + 1) * 256].rearrange("p (i t) -> p i t", i=2)[:, :, :rows], AF.Copy, [bp], [b_pT])
                yield
                sc.op("dve", lambda e: e.reciprocal(out=ssum, in_=ssum), [b_small[7]], [b_small[7]])
                for si, s in enumerate(segs):
                    rows = s["rows"]
                    mkT, bmkT, mv, bmv = memsel[si]
                    pom, bpom = PS()
                    for i in range(2):
                        mm(pom[:rows, :256], pT4[:, si * 2 + i, :rows], mv[:, i, h * 256:(h + 1) * 256], i == 0, i == 1, [b_pT, bmv], bpom)
                    ts(om[:rows, si, :], pom[:rows, :256], ssum[:rows, si:si + 1], None, ALU.mult, None, [bpom, b_small[7]], [b_om[si]])

                def s4(h=h):
                    pt, bp = PS()
                    ptb = pt[:, :].bitcast(BF)
                    for si, s in enumerate(segs):
                        rows = s["rows"]
                        for i in range(2):
                            tr(ptb[:, (si * 2 + i) * 128:(si * 2 + i) * 128 + rows], om[:rows, si, i * 128:(i + 1) * 128], rows, [b_om[si]], bp)
                    for si, s in enumerate(segs):
                        rows, col0 = s["rows"], s["col0"]
                        for i in range(2):
                            act(o_memT[:, h * 2 + i, col0:col0 + rows], ptb[:, (si * 2 + i) * 128:(si * 2 + i) * 128 + rows], AF.Copy, [bp], [b_omT[h * 2 + i]])
                pend4[0] = s4
                yield
            pend4[0]()
            pend4[0] = None
            yield
        gm, gc = mem_gen(), conv_gen()
        alive_m, alive_c = True, True
        it_ = 0
        while alive_m or alive_c:
            if alive_m:
                try:
                    next(gm)
                except StopIteration:
                    alive_m = False
            conv_now = (it_ == 0 or it_ == 8) if is_mini else (it_ % 3 in (0, 1))
            if alive_c and (conv_now or not alive_m):
                try:
                    next(gc)
                except StopIteration:
                    alive_c = False
            it_ += 1

        for cg in range(2):
            for br in range(3):
                if br == 0:
                    wsrc, nk, inbuf, binb = w_br_ret, 16, (lambda k: big[:, k, :T]), (lambda k: b_big[k])
                elif br == 1:
                    wsrc, nk, inbuf, binb = w_br_conv, 8, (lambda k: big[:, 16 + k, :T]), (lambda k: b_big[16 + k])
                else:
                    wsrc, nk, inbuf, binb = w_br_mem, 8, (lambda k: o_memT[:, k, :T]), (lambda k: b_omT[k])
                pbs = [PS() for _ in range(4)]
                for kb in range(nk // 8):
                    wb, bwb = wload([(wsrc[kb * 1024:(kb + 1) * 1024, cg * 512:(cg + 1) * 512], 8, 512)])
                    for c4 in range(4):
                        for k in range(8):
                            kk = kb * 8 + k
                            mm(pbs[c4][0][:, :T], wb[:, k, c4 * 128:(c4 + 1) * 128], inbuf(kk), kk == 0, kk == nk - 1, [bwb, binb(kk)], pbs[c4][1])
                wgt, bwgt = w_in_blk(C_G + br * 1024 + cg * 512)
                for c4 in range(4):
                    pg, bpg = fm_proj(wgt, bwgt, c4 * 128, T)
                    sg = scr[:, 2, :T]
                    act(sg, pg[:, :T], AF.Sigmoid, [bpg], [b_scr[2]])
                    if br == 0:
                        tt(macc[:, c4, :T], sg, pbs[c4][0][:, :T], ALU.mult, [b_scr[2], pbs[c4][1]], [b_macc])
                    else:
                        tt(sg, sg, pbs[c4][0][:, :T], ALU.mult, [b_scr[2], pbs[c4][1]], [b_scr[2]])
                        if br == 1:
                            tt(macc[:, c4, :T], macc[:, c4, :T], sg, ALU.add, [b_scr[2]], [b_macc])
                        else:
                            tt(merged[:, cg * 4 + c4, :T], macc[:, c4, :T], sg, ALU.add, [b_scr[2], b_macc], [b_xbm[cg * 4 + c4]])

        for nbk in range(2):
            wt, bw = wload([(w_out[:, nbk * 512:(nbk + 1) * 512], 8, 512)])
            for si, s in enumerate(segs):
                rows, col0 = s["rows"], s["col0"]
                pt, bp = PS()
                for k in range(8):
                    mm(pt[:rows, :], merged[:, k, col0:col0 + rows], wt[:, k, :], k == 0, k == 7, [bw, b_xbm[k]], bp)
                tt(xt[:rows, si, nbk * 512:(nbk + 1) * 512], xt[:rows, si, nbk * 512:(nbk + 1) * 512], pt[:rows, :], ALU.add, [bp], [b_xt[si]])

        norm_to_fm([(s["rows"], s["col0"]) for s in segs], lambda si: xt[:segs[si]["rows"], si, :], lambda si: b_xt[si], A_GFFN)
        ffn_prev = []
        for blk in range(6):
            ncol = 512 if blk < 5 else 256
            wa, bwa = wload([(w_ffn_in[:, blk * 512:blk * 512 + ncol], 8, ncol)])
            wu, bwu = wload([(w_ffn_in[:, DFF + blk * 512:DFF + blk * 512 + ncol], 8, ncol)])
            if is_mini:
                nc_ = ncol // 128
                c0 = blk * 4
                pa, bpa = PS()
                pu, bpu = PS()
                for (pp, bpp, ww, bww) in ((pa, bpa, wa, bwa), (pu, bpu, wu, bwu)):
                    for c4 in range(nc_):
                        for k in range(8):
                            mm(pp[:, c4 * T:(c4 + 1) * T], ww[:, k, c4 * 128:(c4 + 1) * 128], hT[:, k, :T], k == 0, k == 7, [bww, b_hT[k]], bpp)
                pa3 = pa[:, 0:nc_ * T].rearrange("p (c t) -> p c t", c=nc_)
                pu3 = pu[:, 0:nc_ * T].rearrange("p (c t) -> p c t", c=nc_)
                wv = aux[:, A_WFC + c0 * 3:A_WFC + (c0 + nc_) * 3].rearrange("p (c j) -> p c j", j=3)
                off = 0
                o2 = 0
                for cs in convsegs:
                    f0, n = cs["col0"], cs["n"]
                    ab, bab = cs["abuf"]
                    bsl = bab[c0:c0 + nc_]
                    W_ = nc_ * (n + 2)
                    cb3 = cbuf[:, off:off + W_].rearrange("p (c t) -> p c t", c=nc_)
                    ty3 = scr[:, 1, o2:o2 + nc_ * n].rearrange("p (c t) -> p c t", c=nc_)
                    tm3 = scr[:, 2, o2:o2 + nc_ * n].rearrange("p (c t) -> p c t", c=nc_)
                    sc.op("dve", lambda e, cb3=cb3, ab=ab, c0=c0, nc_=nc_: e.tensor_copy(out=cb3[:, :, 0:2], in_=ab[:, c0:c0 + nc_, :]), bsl, [b_cbuf])
                    act(cb3[:, :, 2:2 + n], pa3[:, :, f0:f0 + n], AF.Copy, [bpa], [b_cbuf])
                    sc.op("dve", lambda e, cb3=cb3, ab=ab, n=n, c0=c0, nc_=nc_: e.tensor_copy(out=ab[:, c0:c0 + nc_, :], in_=cb3[:, :, n:n + 2]), [b_cbuf], bsl)
                    tt(ty3, cb3[:, :, 2:2 + n], wv[:, :, 2:3].broadcast_to([128, nc_, n]), ALU.mult, [b_cbuf, b_aux], [b_scr[1]])
                    tt(tm3, cb3[:, :, 1:1 + n], wv[:, :, 1:2].broadcast_to([128, nc_, n]), ALU.mult, [b_cbuf, b_aux], [b_scr[2]])
                    tt(ty3, ty3, tm3, ALU.add, [b_scr[2]], [b_scr[1]])
                    tt(tm3, cb3[:, :, 0:n], wv[:, :, 0:1].broadcast_to([128, nc_, n]), ALU.mult, [b_cbuf, b_aux], [b_scr[2]])
                    tt(ty3, ty3, tm3, ALU.add, [b_scr[2]], [b_scr[1]])
                    act(ty3, ty3, AF.Silu, [b_scr[1]], [b_scr[1]])
                    tt(big[:, c0:c0 + nc_, f0:f0 + n], ty3, pu3[:, :, f0:f0 + n], ALU.mult, [b_scr[1], bpu], b_big[c0:c0 + nc_])
                    off += W_
                    o2 += nc_ * n
                continue
            for c4 in range(ncol // 128):
                c = blk * 4 + c4
                pa, bpa = fm_proj(wa, bwa, c4 * 128, T)
                pu, bpu = fm_proj(wu, bwu, c4 * 128, T)
                ffn_fin = []
                cb_, bcb_ = (cbuf, b_cbuf) if c % 2 == 0 else (scr[:, 0, :], b_scr[0])
                tyb, btyb = (scr[:, 1, :], b_scr[1]) if c % 2 == 0 else (scr[:, 2, :], b_scr[2])
                off = 0
                for cs in convsegs:
                    f0, n = cs["col0"], cs["n"]
                    hin, bhin = cs["ain"](c)
                    sc.op("dve", lambda e, off=off, hin=hin, cb_=cb_: e.tensor_copy(out=cb_[:, off:off + 2], in_=hin), [bhin], [bcb_])
                    act(cb_[:, off + 2:off + 2 + n], pa[:, f0:f0 + n], AF.Copy, [bpa], [bcb_])
                    hout, bhout = cs["aout"](c)
                    if cs.get("hv_cols"):
                        ts(cb_[:, off + 2:off + 2 + cs["hv_cols"]], cb_[:, off + 2:off + 2 + cs["hv_cols"]], A(A_HV), None, ALU.mult, None,
                           [b_aux], [bcb_])
                    sc.op("dve", lambda e, off=off, n=n, hout=hout, cb_=cb_: e.tensor_copy(out=hout, in_=cb_[:, off + n:off + n + 2]), [bcb_], [bhout])
                    ty = tyb[:, off:off + n]
                    ts(ty, cb_[:, off + 2:off + 2 + n], A(A_WFC + c * 3 + 2), None, ALU.mult, None, [bcb_, b_aux], [btyb])
                    stt(ty, cb_[:, off + 1:off + 1 + n], A(A_WFC + c * 3 + 1), ty, ALU.mult, ALU.add, [bcb_], [btyb])
                    stt(ty, cb_[:, off:off + n], A(A_WFC + c * 3 + 0), ty, ALU.mult, ALU.add, [bcb_], [btyb])
                    act(ty, ty, AF.Silu, [btyb], [btyb])
                    ffn_fin.append((lambda c=c, f0=f0, n=n, ty=ty, pu=pu, btyb=btyb, bpu=bpu:
                                    tt(big[:, c, f0:f0 + n], ty, pu[:, f0:f0 + n], ALU.mult, [btyb, bpu], [b_big[c]])))
                    off += n + 2
                for fn_ in ffn_prev:
                    fn_()
                ffn_prev[:] = ffn_fin
                ffn_fin = []
        for fn_ in ffn_prev:
            fn_()
        if next_front is not None:
            next_front("A")
        for nbk in range(2):
            pds = [PS() for _ in range(nseg)]
            for kb in range(3):
                kc = 8 if kb < 2 else 6
                wt, bw = wload([(w_ffn_down[kb * 1024:kb * 1024 + kc * 128, nbk * 512:(nbk + 1) * 512], kc, 512)])
                for si, s in enumerate(segs):
                    rows, col0 = s["rows"], s["col0"]
                    for k in range(kc):
                        kk = kb * 8 + k
                        mm(pds[si][0][:rows, :], big[:, kk, col0:col0 + rows], wt[:, k, :], kk == 0, kk == NFF - 1, [bw, b_big[kk]], pds[si][1])
            for si, s in enumerate(segs):
                rows = s["rows"]
                tt(xt[:rows, si, nbk * 512:(nbk + 1) * 512], xt[:rows, si, nbk * 512:(nbk + 1) * 512], pds[si][0][:rows, :], ALU.add,
                   [pds[si][1]], [b_xt[si]])

        if next_front is not None:
            next_front("B")
        ssq = small[:, 0:4]
        rr = small[:, 4:8]
        sc.op("dve", lambda e: e.memset(ssq, 0.0), [], [b_small[0]])
        for si, s in enumerate(segs):
            rows = s["rows"]
            act(xb[:rows, si, :], xt[:rows, si, :], AF.Square, [b_xt[si]], [b_xbm[2 * si], b_xbm[2 * si + 1], b_small[0]],
                accum_out=small[:rows, si:si + 1])
        ts(rr[:, :nseg], ssq[:, :nseg], 1.0 / D, EPS, ALU.mult, ALU.add, [b_small[0]], [b_small[1]])
        act(rr[:, :nseg], rr[:, :nseg], AF.Sqrt, [b_small[1]], [b_small[1]])
        sc.op("dve", lambda e: e.reciprocal(out=rr[:, :nseg], in_=rr[:, :nseg]), [b_small[1]], [b_small[1]])
        for si, s in enumerate(segs):
            rows = s["rows"]
            if s["ydst"] is None:
                continue
            yt = yts[si % 2]
            byt = b_yt[si % 2]
            act(yt[:rows, :], xt[:rows, si, :], AF.Copy, [b_xt[si], b_small[1]], [byt], scale=rr[:rows, si:si + 1])
            tt(yt[:rows, :], yt[:rows, :], gfin[:rows, :], ALU.mult, [b_gfin], [byt])
            p0 = s.get("p0", 0)
            st["out_toks"].append(sc.dma(st["q"], s["ydst"], yt[p0:rows, :], reads=[byt]))

    sc.op("dve", lambda e: e.tensor_copy(out=shalo_c[:, :, :].rearrange("p c r -> p (c r)"), in_=aux[:, A_SCONV:A_SCONV + 16]), [b_aux], b_shalo_c)
    sc.op("dve", lambda e: e.tensor_copy(out=shalo_a[:, :, :].rearrange("p c r -> p (c r)"), in_=aux[:, A_SFFN:A_SFFN + 44]), [b_aux], b_shalo_a)
    main_segs = []
    for it in range(4):
        r0 = PRE + it * 512
        segl = []
        for q in range(4):
            y0 = it * 512 + q * 128 - 4
            d = dict(rows=128, col0=q * 128, kind=0, xsrc=xp[r0 + q * 128:r0 + (q + 1) * 128, :], rope=r0, mem="p", state=mainS)
            if y0 < 0:
                d.update(ydst=y_d[0:124, :], p0=4)
            else:
                d.update(ydst=y_d[y0:y0 + 128, :])
            segl.append(d)
        main_segs.append(segl)

    def mk_conv(first):
        d = dict(col0=0, n=512,
                 cin=(lambda c: (zero2, b_small[2])) if first else (lambda c: (chalo[:, c, :], b_chalo[c])),
                 cout=lambda c: (chalo[:, c, :], b_chalo[c]),
                 ain=(lambda c: (zero2, b_small[2])) if first else (lambda c: (ahalo[:, c, :], b_ahalo[c])),
                 aout=lambda c: (ahalo[:, c, :], b_ahalo[c]))
        if first:
            d["hv_cols"] = 4
        return [d]

    if STAGE > 5:
        for it in range(4):
            nf = (lambda ph, it=it: tile_front(main_segs[it + 1], ph)) if it + 1 < 4 else None
            run_tile(main_segs[it], mk_conv(it == 0), False, it > 0, nf, ("save0", "save1", "scratch", "scratch")[it])

    sc.dma(st["q"], Ss, st_ret.rearrange("h (j p) e -> p (h j) e", p=128), reads=[], writes=b_misc_all)
    for c in range(8):
        act(Ssb[:, c, :], Ss[:, c, :], AF.Copy, b_misc_all, [b_big[16 + c]])
    mini_segs = [
        dict(rows=32, col0=0, kind=1, xsrc=xs[:, :], rope=XROWS, mem="s", state=sampS, ydst=ys_d[:, :]),
        dict(rows=4, col0=32, kind=2, xsrc=xp[PRE + SEGT:PRE + SEGT + 4, :], rope=PRE + SEGT, mem="p", state=mainS, ydst=y_d[SEGT - 4:SEGT, :]),
    ]
    mini_conv = [
        dict(col0=0, n=32, cin=lambda c: (shalo_c[:, c, :], b_shalo_c[c]), cout=lambda c: (shalo_c[:, c, :], b_shalo_c[c]),
             ain=lambda c: (shalo_a[:, c, :], b_shalo_a[c]), aout=lambda c: (shalo_a[:, c, :], b_shalo_a[c]),
             abuf=(shalo_a, b_shalo_a), cbufh=(shalo_c, b_shalo_c)),
        dict(col0=32, n=4, cin=lambda c: (chalo[:, c, :], b_chalo[c]), cout=lambda c: (chalo[:, c, :], b_chalo[c]),
             ain=lambda c: (ahalo[:, c, :], b_ahalo[c]), aout=lambda c: (ahalo[:, c, :], b_ahalo[c]), abuf=(ahalo, b_ahalo), cbufh=(chalo, b_chalo)),
    ]
    run_tile(mini_segs, mini_conv, True, False, None, "scratch" if STAGE > 5 else "save")
    st["out_toks"].append(sc.dma(st["q"], s_sconv[:, :], shalo_c[:, :, :].rearrange("p c r -> p (c r)"), reads=b_shalo_c))
    st["out_toks"].append(sc.dma(st["q"], s_sffn[:, :], shalo_a[:, :, :].rearrange("p c r -> p (c r)"), reads=b_shalo_a))

    st["out_toks"].append(sc.dma(st["q"], o_sret[:, :], S[:, :, :].rearrange("p c e -> p (c e)"), reads=b_S))
    st["out_toks"].append(sc.dma(st["q"], o_sconv[:, :], chalo[:, :, :].rearrange("p c r -> p (c r)"), reads=b_chalo))
    st["out_toks"].append(sc.dma(st["q"], o_sffn[:, :], ahalo[:, :, :].rearrange("p c r -> p (c r)"), reads=b_ahalo))
    return finish()


def _tables(core):
    b, j = core // 4, core % 4
    t0 = j * SEGT
    half = 128
    inv = (10000.0 ** (-np.arange(half, dtype=np.float32) / half)).astype(np.float32)
    pos = np.concatenate([np.maximum(t0 - (PRE + 4) + np.arange(XROWS), 0), PAST + np.arange(32)]).astype(np.float32)
    ang = inv[:, None] * pos[None, :]
    rope = np.stack([np.cos(ang), np.sin(ang)]).astype(np.float32)
    aux = np.zeros((128, NAUX), np.float32)
    m = np.arange(128, dtype=np.float64)
    for h in range(NH):
        g = GAM[h]
        l = np.arange(128)
        mk0 = np.where(l[None, :] >= m[:, None], (g ** (-(m[:, None] + 1.0))) / 16.0, 0.0)
        mk1 = np.where((l[None, :] >= m[:, None]) & (m[:, None] >= 2), (g ** (-(m[:, None] - 1.0))) / 16.0, 0.0)
        aux[:, A_MASK + (0 * 4 + h) * 128:A_MASK + (0 * 4 + h) * 128 + 128] = mk0
        aux[:, A_MASK + (1 * 4 + h) * 128:A_MASK + (1 * 4 + h) * 128 + 128] = mk1
        aux[:, A_KDEC + 0 * 4 + h] = g ** (127.0 - m) / 16.0
        aux[:, A_KDEC + 1 * 4 + h] = np.where(m < 32, g ** (31.0 - np.minimum(m, 31)), 0.0) / 16.0
        aux[:, A_KDEC + 2 * 4 + h] = np.where(m < 4, g ** (3.0 - np.minimum(m, 3)), 0.0) / 16.0
        for sg in range(4):
            aux[:, A_KDT + sg * 4 + h] = g ** (511.0 - (sg * 128 + m)) / 16.0
            aux[:, A_KSC + sg * 4 + h] = g ** (-(128.0 * sg + m + 1.0)) / 16.0
            aux[:, A_ROWT + sg * 4 + h] = g ** (128.0 * sg + m + 1.0)
        aux[:, A_ROWD + 0 * 4 + h] = g ** (m + 1.0)
        aux[:, A_ROWD + 1 * 4 + h] = g ** (m - 1.0)
        for slot in range(6):
            bb, d = slot // 3, slot % 3 + 1
            aux[:, A_COEF + slot * 4 + h] = (g ** (float(SEGT) * (d - 1 - j))) if (bb == b and d > j) else 0.0
    for slot in range(6):
        bb, d = slot // 3, slot % 3 + 1
        aux[:, A_SEL + slot] = 1.0 if (bb == b and d == j) else 0.0
    aux[:, A_HV] = 1.0 if j >= 1 else 0.0
    aux[:, A_EPS] = EPS
    return rope, aux


def _fm(v, n):
    return np.ascontiguousarray(np.asarray(v, np.float32).reshape(n, 128).T)


def make_in_maps(x_prompt, x_sample, mem_prompt, state_ret, state_conv, state_ffn_conv, cache_mem_k, cache_mem_v,
           g_mix, w_in, g_ret_gn, w_conv, g_mem, w_mem_kv, w_br_ret, w_br_conv, w_br_mem, w_out, g_ffn,
           w_ffn_in, w_ffn_conv, w_ffn_down, g_final):
    f = lambda a: np.ascontiguousarray(np.asarray(a, dtype=np.float32))
    x_prompt, x_sample, mem_prompt = f(x_prompt), f(x_sample), f(mem_prompt)
    shared = dict(w_in=f(w_in[0]), w_mem_kv=f(w_mem_kv[0]), w_br_ret=f(w_br_ret[0]), w_br_conv=f(w_br_conv[0]),
                  w_br_mem=f(w_br_mem[0]), w_out=f(w_out[0]), w_ffn_in=f(w_ffn_in[0]), w_ffn_down=f(w_ffn_down[0]),
                  gfin=np.ascontiguousarray(np.broadcast_to(f(g_final)[None, :], (128, D))))
    in_maps = []
    for c in range(8):
        b, j = c // 4, c % 4
        t0 = j * SEGT
        xp = np.zeros((XROWS, D), np.float32)
        lo = max(t0 - (PRE + 4), 0)
        xp[lo - (t0 - (PRE + 4)):] = x_prompt[b, lo:t0 + SEGT]
        rope, aux = _tables(c)
        aux[:, A_GMIX:A_GMIX + 8] = _fm(g_mix[0], 8)
        aux[:, A_GFFN:A_GFFN + 8] = _fm(g_ffn[0], 8)
        aux[:, A_GMEM:A_GMEM + 8] = _fm(g_mem[0], 8)
        aux[:, A_GGN:A_GGN + 16] = _fm(g_ret_gn[0], 16)
        aux[:, A_WCONV:A_WCONV + 24] = np.asarray(w_conv[0], np.float32).reshape(3, 8, 128).transpose(2, 1, 0).reshape(128, 24)
        aux[:, A_WFC:A_WFC + 66] = np.asarray(w_ffn_conv[0], np.float32).reshape(3, NFF, 128).transpose(2, 1, 0).reshape(128, 66)
        aux[:, A_SCONV:A_SCONV + 16] = np.asarray(state_conv[0, c], np.float32).reshape(2, 8, 128).transpose(2, 1, 0).reshape(128, 16)
        aux[:, A_SFFN:A_SFFN + 44] = np.asarray(state_ffn_conv[0, c], np.float32).reshape(2, NFF, 128).transpose(2, 1, 0).reshape(128, 44)
        m = dict(shared)
        m.update(xp=xp, xs=f(x_sample[c]), memp=f(mem_prompt[b]), st_ret=f(state_ret[0, c]),
                 cmk=f(cache_mem_k[0, c]).reshape(256, D), cmv=f(cache_mem_v[0, c]).reshape(256, D),
                 rope=rope, aux=aux)
        in_maps.append(m)
    return in_maps


def kernel(**inputs):
    in_maps = make_in_maps(**inputs)
    nc = build_nc()
    res = run_bass_kernel_spmd(nc, in_maps, core_ids=list(range(8)))
    R = res.results

    def unstate(a):
        return np.asarray(a, np.float32).reshape(128, 4, 2, 512).transpose(1, 2, 0, 3).reshape(4, 256, 512)

    def unfm(a, n):
        return np.asarray(a, np.float32).reshape(128, n, 2).transpose(2, 1, 0).reshape(2, n * 128)

    y_prompt = np.stack([np.concatenate([R[b * 4 + j]["y"] for j in range(4)], 0) for b in range(2)]).astype(np.float32)
    y_sample = np.stack([R[c]["ys"] for c in range(8)]).astype(np.float32)
    nsr_p = np.stack([unstate(R[b * 4 + 3]["o_sret"]) for b in range(2)])[None]
    nsc_p = np.stack([unfm(R[b * 4 + 3]["o_sconv"], 8) for b in range(2)])[None]
    nsf_p = np.stack([unfm(R[b * 4 + 3]["o_sffn"], NFF) for b in range(2)])[None]
    nmk_p = np.stack([np.asarray(R[b * 4]["o_mk"], np.float32).reshape(256, 4, 256) for b in range(2)])[None]
    nmv_p = np.stack([np.asarray(R[b * 4]["o_mv"], np.float32).reshape(256, 4, 256) for b in range(2)])[None]
    nsr_s = np.stack([unstate(R[c]["s_sret"]) for c in range(8)])[None]
    nsc_s = np.stack([unfm(R[c]["s_sconv"], 8) for c in range(8)])[None]
    nsf_s = np.stack([unfm(R[c]["s_sffn"], NFF) for c in range(8)])[None]
    return (y_prompt, y_sample, nsr_p, nsc_p, nsf_p, nmk_p, nmv_p, nsr_s, nsc_s, nsf_s)
```
